# Optimizing a Trainium2 kernel written in Bass

```python
import jax, jax.numpy as jnp
from jax import lax
import numpy as np

D_MODEL = 1024
BATCH = 8
SEQ = 2048
DEPTH = 1

ATTN_HEADS = 8
QK_NOPE_DIM = 64
QK_ROPE_DIM = 32
QK_HEAD_DIM = QK_NOPE_DIM + QK_ROPE_DIM
V_HEAD_DIM = 64
Q_LORA_RANK = D_MODEL // 4
KV_LORA_RANK = D_MODEL // 8
ROPE_THETA = 10000.0
Q_BLOCK = 128
ATTN_WIDTH = ATTN_HEADS * V_HEAD_DIM

SSM_HEADS = 8
SSM_HEAD_DIM = 64
SSM_INNER = SSM_HEADS * SSM_HEAD_DIM
SSM_GROUPS = 2
SSM_STATE = 128
SSM_CONV = 5
SSM_CHUNK = 128
SSM_CONV_CH = SSM_INNER + 2 * SSM_GROUPS * SSM_STATE

D_MIX = ATTN_WIDTH + SSM_INNER
D_FF = 4 * D_MODEL
EPS = 1e-6

IN_SPLITS = (Q_LORA_RANK, KV_LORA_RANK, QK_ROPE_DIM, SSM_INNER, SSM_CONV_CH, 2 * SSM_HEADS)
IN_WIDTH = Q_LORA_RANK + KV_LORA_RANK + QK_ROPE_DIM + SSM_INNER + SSM_CONV_CH + 2 * SSM_HEADS

kernel_name = 'hybrid_mla_mamba2_sqrelu_encoder'


def rms_norm(x, g):
    xf = x.astype(jnp.float32)
    y = xf * lax.rsqrt(jnp.mean(xf * xf, axis=-1, keepdims=True) + EPS)
    return (y * g.astype(jnp.float32)).astype(x.dtype)


def rope_cos_sin(positions, dtype):
    inv_freq = 1.0 / (ROPE_THETA ** (jnp.arange(0, QK_ROPE_DIM, 2, dtype=jnp.float32) / QK_ROPE_DIM))
    ang = positions.astype(jnp.float32)[..., None] * inv_freq
    ang = jnp.concatenate([ang, ang], axis=-1)[:, :, None, :]
    return jnp.cos(ang).astype(dtype), jnp.sin(ang).astype(dtype)


def apply_rope(x, cos, sin):
    x1, x2 = jnp.split(x, 2, axis=-1)
    return x * cos + jnp.concatenate([-x2, x1], axis=-1) * sin


def mla_attention(c_q, c_kv, k_pe, cos, sin, q_a_norm_g, w_uq, kv_a_norm_g, w_ukv, q_norm_g, k_norm_g):
    b, s, _ = c_q.shape
    q = (rms_norm(c_q, q_a_norm_g) @ w_uq).reshape(b, s, ATTN_HEADS, QK_HEAD_DIM)
    kv = (rms_norm(c_kv, kv_a_norm_g) @ w_ukv).reshape(b, s, ATTN_HEADS, QK_NOPE_DIM + V_HEAD_DIM)
    k_nope, v = jnp.split(kv, [QK_NOPE_DIM], axis=-1)
    k_pe_h = jnp.broadcast_to(k_pe[:, :, None, :], (b, s, ATTN_HEADS, QK_ROPE_DIM))
    k = jnp.concatenate([k_nope, k_pe_h], axis=-1)
    q = rms_norm(q, q_norm_g)
    k = rms_norm(k, k_norm_g)
    q = jnp.concatenate([q[..., :QK_NOPE_DIM], apply_rope(q[..., QK_NOPE_DIM:], cos, sin)], axis=-1)
    k = jnp.concatenate([k[..., :QK_NOPE_DIM], apply_rope(k[..., QK_NOPE_DIM:], cos, sin)], axis=-1)
    scale = QK_HEAD_DIM ** -0.5
    n_blk = s // Q_BLOCK
    q_blocks = jnp.moveaxis(q.reshape(b, n_blk, Q_BLOCK, ATTN_HEADS, QK_HEAD_DIM), 1, 0)

    def attend(qb):
        logits = jnp.einsum('bqhd,bkhd->bhqk', qb, k).astype(jnp.float32) * scale
        p = jax.nn.softmax(logits, axis=-1).astype(v.dtype)
        return jnp.einsum('bhqk,bkhd->bqhd', p, v)

    o = lax.map(attend, q_blocks)
    return jnp.moveaxis(o, 0, 1).reshape(b, s, ATTN_WIDTH)


def ssd_scan(x, dt, a, bmat, cmat):
    b, s, h, p = x.shape
    g, n = bmat.shape[-2], bmat.shape[-1]
    r = h // g
    c = s // SSM_CHUNK
    L = SSM_CHUNK
    xdt = (x * dt[..., None]).reshape(b, c, L, g, r, p)
    da = jnp.moveaxis((dt * a).reshape(b, c, L, g, r), 2, -1)
    a_cs = jnp.cumsum(da, axis=-1)
    bm = bmat.reshape(b, c, L, g, n)
    cm = cmat.reshape(b, c, L, g, n)
    seg = a_cs[..., :, None] - a_cs[..., None, :]
    lower = jnp.tril(jnp.ones((L, L), dtype=bool))
    decay_in = jnp.exp(jnp.where(lower, seg, -jnp.inf))
    cb = jnp.einsum('bclgn,bcsgn->bcgls', cm, bm)
    y_diag = jnp.einsum('bcgls,bcgrls,bcsgrp->bclgrp', cb, decay_in, xdt)
    decay_to_end = jnp.exp(a_cs[..., -1:] - a_cs)
    chunk_states = jnp.einsum('bclgn,bcgrl,bclgrp->bcgrpn', bm, decay_to_end, xdt)
    chunk_decay = jnp.exp(a_cs[..., -1])

    def step(state, inp):
        dec, new = inp
        return state * dec[..., None, None] + new, state

    init = jnp.zeros((b, g, r, p, n), dtype=chunk_states.dtype)
    _, prev = lax.scan(step, init, (jnp.moveaxis(chunk_decay, 1, 0), jnp.moveaxis(chunk_states, 1, 0)))
    prev = jnp.moveaxis(prev, 0, 1)
    y_off = jnp.einsum('bclgn,bcgrpn,bcgrl->bclgrp', cm, prev, jnp.exp(a_cs))
    return (y_diag + y_off).reshape(b, s, h, p)


def mamba2_mixer(z, xbc, dt_raw, conv_w, conv_b, a_log_fwd, a_log_bwd, dt_bias_fwd, dt_bias_bwd, d_skip, ssm_norm_g):
    b, s, _ = z.shape
    xbc = lax.conv_general_dilated(
        xbc, conv_w, window_strides=(1,), padding=[(SSM_CONV // 2, SSM_CONV // 2)],
        dimension_numbers=('NWC', 'WIO', 'NWC'), feature_group_count=SSM_CONV_CH) + conv_b
    xbc = jax.nn.silu(xbc)
    xs, bm, cm = jnp.split(xbc, [SSM_INNER, SSM_INNER + SSM_GROUPS * SSM_STATE], axis=-1)
    xs = xs.reshape(b, s, SSM_HEADS, SSM_HEAD_DIM)
    bm = bm.reshape(b, s, SSM_GROUPS, SSM_STATE)
    cm = cm.reshape(b, s, SSM_GROUPS, SSM_STATE)
    dt_f, dt_b = jnp.split(dt_raw, 2, axis=-1)
    dt_f = jax.nn.softplus(dt_f + dt_bias_fwd)
    dt_b = jax.nn.softplus(dt_b + dt_bias_bwd)
    y_f = ssd_scan(xs, dt_f, -jnp.exp(a_log_fwd), bm, cm)
    flip = lambda t: jnp.flip(t, axis=1)
    y_b = flip(ssd_scan(flip(xs), flip(dt_b), -jnp.exp(a_log_bwd), flip(bm), flip(cm)))
    y = y_f + y_b + d_skip[:, None] * xs
    y = y.reshape(b, s, SSM_INNER) * jax.nn.silu(z)
    y = rms_norm(y.reshape(b, s, SSM_GROUPS, SSM_INNER // SSM_GROUPS),
                 ssm_norm_g.reshape(SSM_GROUPS, SSM_INNER // SSM_GROUPS))
    return y.reshape(b, s, SSM_INNER)


def hybrid_layer(x, cos, sin, ln_mix_g, w_in, q_a_norm_g, w_uq, kv_a_norm_g, w_ukv, q_norm_g, k_norm_g,
                 attn_out_norm_g, conv_w, conv_b, a_log_fwd, a_log_bwd, dt_bias_fwd, dt_bias_bwd, d_skip,
                 ssm_norm_g, w_out, ln_mlp_g, w_mlp_up, w_mlp_down):
    h = rms_norm(x, ln_mix_g)
    proj = h @ w_in
    split_idx = np.cumsum(IN_SPLITS)[:-1].tolist()
    c_q, c_kv, k_pe, z, xbc, dt_raw = jnp.split(proj, split_idx, axis=-1)
    attn = mla_attention(c_q, c_kv, k_pe, cos, sin, q_a_norm_g, w_uq, kv_a_norm_g, w_ukv, q_norm_g, k_norm_g)
    attn = rms_norm(attn, attn_out_norm_g)
    ssm = mamba2_mixer(z, xbc, dt_raw, conv_w, conv_b, a_log_fwd, a_log_bwd, dt_bias_fwd, dt_bias_bwd,
                       d_skip, ssm_norm_g)
    x = x + jnp.concatenate([attn, ssm], axis=-1) @ w_out
    hm = rms_norm(x, ln_mlp_g)
    x = x + jnp.square(jax.nn.relu(hm @ w_mlp_up)) @ w_mlp_down
    return x


def setup_inputs(seed: int = 0) -> dict:
    key = jax.random.key(seed)
    ks = jax.random.split(key, 24)
    f32 = jnp.float32

    def nrm(k, shape, fan_in):
        return jax.random.normal(k, shape, f32) * (fan_in ** -0.5)

    def gain(k, shape):
        return 1.0 + 0.02 * jax.random.normal(k, shape, f32)

    x = jax.random.normal(ks[0], (BATCH, SEQ, D_MODEL), f32)
    positions = (jnp.arange(SEQ, dtype=jnp.int32)[None, :]
                 + jax.random.randint(ks[1], (BATCH, 1), 0, 4096, dtype=jnp.int32))
    a_log_fwd = jnp.log(jax.random.uniform(ks[13], (DEPTH, SSM_HEADS), f32, 1.0, 16.0))
    a_log_bwd = jnp.log(jax.random.uniform(ks[14], (DEPTH, SSM_HEADS), f32, 1.0, 16.0))

    def dt_bias(k):
        dt = jnp.exp(jax.random.uniform(k, (DEPTH, SSM_HEADS), f32, np.log(1e-3), np.log(1e-1)))
        return dt + jnp.log(-jnp.expm1(-dt))

    return {
        'x': x,
        'positions': positions,
        'ln_mix_g': gain(ks[2], (DEPTH, D_MODEL)),
        'w_in': nrm(ks[3], (DEPTH, D_MODEL, IN_WIDTH), D_MODEL),
        'q_a_norm_g': gain(ks[4], (DEPTH, Q_LORA_RANK)),
        'w_uq': nrm(ks[5], (DEPTH, Q_LORA_RANK, ATTN_HEADS * QK_HEAD_DIM), Q_LORA_RANK),
        'kv_a_norm_g': gain(ks[6], (DEPTH, KV_LORA_RANK)),
        'w_ukv': nrm(ks[7], (DEPTH, KV_LORA_RANK, ATTN_HEADS * (QK_NOPE_DIM + V_HEAD_DIM)), KV_LORA_RANK),
        'q_norm_g': gain(ks[8], (DEPTH, QK_HEAD_DIM)),
        'k_norm_g': gain(ks[9], (DEPTH, QK_HEAD_DIM)),
        'attn_out_norm_g': gain(ks[10], (DEPTH, ATTN_WIDTH)),
        'conv_w': nrm(ks[11], (DEPTH, SSM_CONV, 1, SSM_CONV_CH), SSM_CONV),
        'conv_b': 0.02 * jax.random.normal(ks[12], (DEPTH, SSM_CONV_CH), f32),
        'a_log_fwd': a_log_fwd,
        'a_log_bwd': a_log_bwd,
        'dt_bias_fwd': dt_bias(ks[15]),
        'dt_bias_bwd': dt_bias(ks[16]),
        'd_skip': gain(ks[17], (DEPTH, SSM_HEADS)),
        'ssm_norm_g': gain(ks[18], (DEPTH, SSM_INNER)),
        'w_out': nrm(ks[19], (DEPTH, D_MIX, D_MODEL), D_MIX),
        'ln_mlp_g': gain(ks[20], (DEPTH, D_MODEL)),
        'w_mlp_up': nrm(ks[21], (DEPTH, D_MODEL, D_FF), D_MODEL),
        'w_mlp_down': nrm(ks[22], (DEPTH, D_FF, D_MODEL), D_FF),
    }


def reference(x, positions, ln_mix_g, w_in, q_a_norm_g, w_uq, kv_a_norm_g, w_ukv, q_norm_g, k_norm_g,
              attn_out_norm_g, conv_w, conv_b, a_log_fwd, a_log_bwd, dt_bias_fwd, dt_bias_bwd, d_skip,
              ssm_norm_g, w_out, ln_mlp_g, w_mlp_up, w_mlp_down):
    cos, sin = rope_cos_sin(positions, x.dtype)
    for l in range(DEPTH):
        x = hybrid_layer(x, cos, sin, ln_mix_g[l], w_in[l], q_a_norm_g[l], w_uq[l], kv_a_norm_g[l], w_ukv[l],
                         q_norm_g[l], k_norm_g[l], attn_out_norm_g[l], conv_w[l], conv_b[l], a_log_fwd[l],
                         a_log_bwd[l], dt_bias_fwd[l], dt_bias_bwd[l], d_skip[l], ssm_norm_g[l], w_out[l],
                         ln_mlp_g[l], w_mlp_up[l], w_mlp_down[l])
    return x
```

```python
import math
from contextlib import ExitStack

import numpy as np
import concourse.bass as bass
import concourse.mybir as mybir
from concourse.bass_utils import run_bass_kernel_spmd

F32 = mybir.dt.float32
BF16 = mybir.dt.bfloat16
I32 = mybir.dt.int32
AF = mybir.ActivationFunctionType
ALU = mybir.AluOpType
AX = mybir.AxisListType

S = 2048
D = 1024
NT = 16
EPS = 1e-6
IN_W = 1968
C_LNMIX, C_LNMLP, C_QA, C_KVA, C_GK, C_GQ, C_GATT, C_GSSM, C_CONVB, C_CONVW, NCOL = 0, 8, 16, 18, 19, 20, 21, 29, 33, 41, 81
B_DTB, B_ALOG, B_DSKIP, B_GQR, B_GKR, B_INVF, NBC = 0, 16, 32, 40, 72, 104, 120
NEG = -30000.0
MAGIC = 12582912.0
TWO_PI = 2.0 * math.pi
C1 = 6.28125
C2 = TWO_PI - C1


class Buf:
    __slots__ = ("w", "r", "name")

    def __init__(self, name=""):
        self.w = None
        self.r = []
        self.name = name


def bufs(n, name=""):
    return [Buf(f"{name}{i}") for i in range(n)]


def is_psum_buf(b):
    n = b.name
    return n.startswith("p") and n != "par" and not n.startswith("prev")


class Sched:
    def __init__(self, nc):
        self.nc = nc
        self.ops = []
        self.barrier_deps = set()
        self.last = {}
        self.dma_count = {}
        self.psum_names = set()

    def op(self, eng, fn, rd=(), wr=(), dma_key=None):
        idx = len(self.ops)
        deps = set(self.barrier_deps)
        for b in rd:
            if b.w is not None:
                deps.add(b.w)
            if is_psum_buf(b):
                deps.update(r for r in b.r if self.ops[r]["eng"] != eng)
                self.psum_names.add(b.name)
        for b in wr:
            if b.w is not None:
                deps.add(b.w)
            deps.update(b.r)
        deps.discard(idx)
        semval = None
        if dma_key is not None:
            self.dma_count[dma_key] = self.dma_count.get(dma_key, 0) + 16
            semval = self.dma_count[dma_key]
        self.ops.append(dict(eng=eng, fn=fn, deps=deps, dma=dma_key, semval=semval, sig=False, sigidx=0))
        for b in rd:
            b.r.append(idx)
        for b in wr:
            b.w = idx
            b.r = []
        if dma_key is None:
            self.last[eng] = idx
        else:
            self.last[("dma", dma_key)] = idx
        return idx

    def barrier(self):
        self.barrier_deps = set(self.last.values())

    def emit(self, es):
        nc = self.nc
        engs = {"pe": nc.tensor, "act": nc.scalar, "dve": nc.vector, "pool": nc.gpsimd, "sp": nc.sync}
        ops = self.ops
        for o in ops:
            for d in o["deps"]:
                p = ops[d]
                if p["dma"] is None:
                    if p["eng"] == "pe" and o["eng"] == "pe" and o["dma"] is None:
                        continue
                    p["sig"] = True
        cnt = {}
        for o in ops:
            if o["dma"] is None and o["sig"]:
                cnt[o["eng"]] = cnt.get(o["eng"], 0) + 1
                o["sigidx"] = cnt[o["eng"]]
        self.sig_counts = dict(cnt)
        esem = {e: es.enter_context(nc.semaphore("sem_" + e)) for e in ("pe", "act", "dve", "pool")}
        dsem = {k: es.enter_context(nc.semaphore("dsem_" + str(k))) for k in self.dma_count}
        waited = {e: {} for e in engs}
        nwait = 0
        for o in ops:
            E = engs[o["eng"]]
            need = {}
            for d in o["deps"]:
                p = ops[d]
                if p["dma"] is not None:
                    key, sem, val = ("d", p["dma"]), dsem[p["dma"]], p["semval"]
                else:
                    if p["eng"] == "pe" and o["eng"] == "pe" and o["dma"] is None:
                        continue
                    key, sem, val = ("e", p["eng"]), esem[p["eng"]], p["sigidx"]
                if need.get(key, (None, 0))[1] < val:
                    need[key] = (sem, val)
            for key, (sem, val) in need.items():
                if waited[o["eng"]].get(key, 0) >= val:
                    continue
                E.wait_ge(sem, val)
                waited[o["eng"]][key] = val
                nwait += 1
            ins = o["fn"]()
            if o["dma"] is not None:
                ins.then_inc(dsem[o["dma"]], 16)
            elif o["sig"]:
                ins.then_inc(esem[o["eng"]], 1)
        for k, sem in dsem.items():
            if waited["sp"].get(("d", k), 0) < self.dma_count[k]:
                nc.sync.wait_ge(sem, self.dma_count[k])
        return nwait


def build_nc(debug=None):
    nc = bass.Bass("TRN2", target_bir_lowering=False)
    x_d = nc.dram_tensor("x", [S, D], F32, kind="ExternalInput").ap()
    pos_d = nc.dram_tensor("pos", [128, NT], I32, kind="ExternalInput").ap()
    pcol_d = nc.dram_tensor("pcol", [128, NCOL], F32, kind="ExternalInput").ap()
    pbc_d = nc.dram_tensor("pbc", [128, NBC], F32, kind="ExternalInput").ap()
    w_in_d = nc.dram_tensor("w_in", [D, IN_W], F32, kind="ExternalInput").ap()
    w_uq_d = nc.dram_tensor("w_uq", [256, 768], F32, kind="ExternalInput").ap()
    w_ukv_d = nc.dram_tensor("w_ukv", [128, 1024], F32, kind="ExternalInput").ap()
    w_out_d = nc.dram_tensor("w_out", [D, D], F32, kind="ExternalInput").ap()
    w_up_d = nc.dram_tensor("w_up", [D, 4096], F32, kind="ExternalInput").ap()
    w_dn_d = nc.dram_tensor("w_dn", [4096, D], F32, kind="ExternalInput").ap()
    out_d = nc.dram_tensor("out", [S, D], F32, kind="ExternalOutput").ap()
    dbg_d = {}
    if debug:
        for name, shape in debug.items():
            dbg_d[name] = nc.dram_tensor("dbg_" + name, list(shape), F32, kind="ExternalOutput").ap()

    K = Sched(nc)
    V, A, P, T, SP = "dve", "act", "pool", "pe", "sp"

    def mm(out, lhsT, rhs, start, stop, rd, wr):
        K.op(T, lambda: nc.tensor.matmul(out, lhsT=lhsT, rhs=rhs, start=start, stop=stop), rd, wr)

    def tr(out, in_, ident, rd, wr):
        K.op(T, lambda: nc.tensor.transpose(out, in_, ident), rd, wr)

    def act(out, in_, func, rd, wr, bias=None, scale=None, accum=None):
        kw = {}
        if bias is not None:
            kw["bias"] = bias
            if not isinstance(bias, float):
                rd = list(rd) + [b_const]
        if scale is not None:
            kw["scale"] = scale
        if accum is not None:
            kw["accum_out"] = accum
        K.op(A, lambda: nc.scalar.activation(out, in_, func, **kw), rd, wr)

    def ts(eng, out, in0, s1, s2, op0, op1, rd, wr):
        e = nc.vector if eng == V else nc.gpsimd
        if op1 is None:
            K.op(eng, lambda: e.tensor_scalar(out, in0, s1, None, op0), rd, wr)
        else:
            K.op(eng, lambda: e.tensor_scalar(out, in0, s1, s2, op0, op1), rd, wr)

    def tt(eng, out, in0, in1, op, rd, wr):
        e = nc.vector if eng == V else nc.gpsimd
        K.op(eng, lambda: e.tensor_tensor(out, in0, in1, op), rd, wr)

    def stt(out, in0, scalar, in1, op0, op1, rd, wr):
        K.op(V, lambda: nc.vector.scalar_tensor_tensor(out, in0, scalar, in1, op0, op1), rd, wr)

    def cp(eng, out, in_, rd, wr):
        if eng == A:
            K.op(A, lambda: nc.scalar.copy(out, in_), rd, wr)
        else:
            e = nc.vector if eng == V else nc.gpsimd
            K.op(eng, lambda: e.tensor_copy(out, in_), rd, wr)

    def recip(out, in_, rd, wr):
        K.op(V, lambda: nc.vector.reciprocal(out, in_), rd, wr)

    def red(out, in_, rd, wr):
        K.op(V, lambda: nc.vector.tensor_reduce(out, in_, AX.X, ALU.add), rd, wr)

    def memset(eng, ap, val, wr):
        e = nc.vector if eng == V else nc.gpsimd
        K.op(eng, lambda: e.memset(ap, val), (), wr)

    def dma(out, in_, rd, wr, key, q=SP):
        e = nc.sync if q == SP else nc.gpsimd
        K.op(q, lambda: e.dma_start(out=out, in_=in_), rd, wr, dma_key=key)

    def rstd_from_ss(ss, n, rstd, b_ss, b_rstd, tmp):
        act(tmp, ss, AF.Sqrt, [b_ss], [b_rstd], bias=epsb[:ss.shape[0], 0:1], scale=1.0 / n)
        recip(rstd, tmp, [b_rstd], [b_rstd])

    def dump(name, ap, rd):
        if name in dbg_d:
            dma(dbg_d[name], ap, rd, [], "dbg_" + name)

    def dbg_copy(name, ap, rd):
        if name in dbg_d:
            tns = dbgbuf[name]
            b_ = Buf("dbgc" + name)
            cp(V, tns[:], ap, rd, [b_])
            dma(dbg_d[name], tns[:], [b_], [], "dbg_" + name)

    with ExitStack() as es:
        sb = lambda name, shape, dt=F32: es.enter_context(nc.sbuf_tensor("s_" + name, shape, dt))
        pcol = sb("pcol", [128, NCOL])
        pbc = sb("pbc", [128, NBC])
        posi = sb("posi", [128, NT], I32)
        ident_f = sb("ident_f", [128, 128])
        ident_b = sb("ident_b", [128, 128], BF16)
        ones_b = sb("ones_b", [128, 128], BF16)
        ones_f = sb("ones_f", [128, 128])
        epsb = sb("epsb", [128, 1])
        fence_t = sb("fence_t", [128, 1])
        b_const = Buf("const")
        b_par = Buf("par")
        dma(pcol[:], pcol_d, [], [b_par], "par")
        dma(pbc[:], pbc_d, [], [b_par], "par")
        dma(posi[:], pos_d, [], [b_par], "par")
        memset(P, ident_f[:], 1.0, [b_const])
        K.op(P, lambda: nc.gpsimd.affine_select(out=ident_f[:], in_=ident_f[:], pattern=[[1, 128]], compare_op=ALU.is_equal,
                                                fill=0.0, base=0, channel_multiplier=-1), [b_const], [b_const])
        cp(P, ident_b[:], ident_f[:], [b_const], [b_const])
        memset(P, ones_b[:], 1.0, [b_const])
        memset(P, ones_f[:], 1.0, [b_const])
        memset(P, epsb[:], EPS, [b_const])

        dbgbuf = {n_: sb("dbgc_" + n_, list(ap_.shape)) for n_, ap_ in dbg_d.items() if n_ in ("ssx", "xn0", "hT2")}
        mixA = sb("mixA", [128, 4, S], BF16)
        b_mixA = bufs(4, "mixA")
        hT = sb("hT", [128, 8, S], BF16)
        b_hT = bufs(NT, "hT")

        def run_pipeline(stages, n, skew=1, order=None):
            ns = len(stages)
            order = order if order is not None else list(range(ns - 1, -1, -1))
            for step in range(n + (ns - 1) * skew):
                for si in order:
                    stg = stages[si]
                    t_ = step - si * skew
                    if 0 <= t_ < n:
                        stg(t_)

        with ExitStack() as ph:
            sbp = lambda name, shape, dt=F32: ph.enter_context(nc.sbuf_tensor("s_" + name, shape, dt))
            psp = lambda name, shape, dt=F32: ph.enter_context(nc.psum_tensor("p_" + name, shape, dt))
            QT = sbp("QT", [96, 8, S], BF16)
            KT = sbp("KT", [96, 8, S], BF16)
            Vt = sbp("Vt", [128, NT, 8, 65], BF16)
            b_QT = bufs(NT, "QT")
            b_KT = bufs(NT, "KT")
            b_V = bufs(NT, "V")
            sincos = sbp("sincos", [128, 2, NT, 16])
            qsc = sbp("qsc", [128, 2, NT, 2, 16])
            b_rope = Buf("rope")
            with ExitStack() as phr:
                sbr = lambda name, shape, dt=F32: phr.enter_context(nc.sbuf_tensor("s_" + name, shape, dt))
                posf = sbr("posf", [128, NT])
                ang = sbr("ang", [128, 2, NT, 16])
                kk = sbr("kk", [128, 2, NT, 16])
                r1 = sbr("r1", [128, 2, NT, 16])
                cp(V, posf[:], posi[:], [b_par], [b_rope])
                tt(V, ang[:, 0], pbc[:, B_INVF:B_INVF + 16].unsqueeze(1).to_broadcast([128, NT, 16]),
                   posf[:].unsqueeze(2).to_broadcast([128, NT, 16]), ALU.mult, [b_par, b_rope], [b_rope])
                ts(V, ang[:, 1], ang[:, 0], math.pi / 2, None, ALU.add, None, [b_rope], [b_rope])
                ts(V, kk[:], ang[:], 1.0 / TWO_PI, MAGIC, ALU.mult, ALU.add, [b_rope], [b_rope])
                ts(V, kk[:], kk[:], MAGIC, None, ALU.subtract, None, [b_rope], [b_rope])
                stt(r1[:], kk[:], -C1, ang[:], ALU.mult, ALU.add, [b_rope], [b_rope])
                stt(r1[:], kk[:], -C2, r1[:], ALU.mult, ALU.add, [b_rope], [b_rope])
                ts(V, r1[:], r1[:], math.pi, -math.pi, ALU.min, ALU.max, [b_rope], [b_rope])
                act(sincos[:], r1[:], AF.Sin, [b_rope], [b_rope])
                for a_ in range(2):
                    for j_ in range(2):
                        tt(V, qsc[:, a_, :, j_, :], sincos[:, a_, :, :],
                           pbc[:, B_GQR + 16 * j_:B_GQR + 16 * (j_ + 1)].unsqueeze(1).to_broadcast([128, NT, 16]), ALU.mult,
                           [b_rope, b_par], [b_rope])
            K.barrier()
            sin_t = lambda t: sincos[:, 0, t, :]
            cos_t = lambda t: sincos[:, 1, t, :]
            with ExitStack() as ph2:
                sb2 = lambda name, shape, dt=F32: ph2.enter_context(nc.sbuf_tensor("s_" + name, shape, dt))
                ps2 = lambda name, shape, dt=F32: ph2.enter_context(nc.psum_tensor("p_" + name, shape, dt))
                wA = sb2("wA", [128, 8, 416], BF16)
                b_wA = bufs(8, "wA")
                for dc in range(8):
                    dma(wA[:, dc, :], w_in_d[dc * 128:(dc + 1) * 128, 0:416], [], [b_wA[dc]], f"wA{dc}", q=P)
                wuq = sb2("wuq", [128, 2, 768], BF16)
                wukv = sb2("wukv", [128, 1024], BF16)
                b_wq = Buf("wq")
                dma(wuq[:], w_uq_d.rearrange("(c p) n -> p c n", p=128), [], [b_wq], "wq", q=P)
                dma(wukv[:], w_ukv_d, [], [b_wq], "wq", q=P)
                xt = [sb2(f"xt{i}", [128, D]) for i in range(3)]
                b_xt = bufs(3, "xt")
                xn = [sb2(f"xn{i}", [128, D], BF16) for i in range(2)]
                b_xn = bufs(2, "xn")
                sq_scr = sb2("sq_scr", [128, D], BF16)
                b_sqs = Buf("sqscr")
                ssx = sb2("ssx", [128, NT, 2])
                b_ss = bufs(NT, "ss")
                gmix = sb2("gmix", [128, 8, 128])
                b_g = Buf("gmix")
                ps_trh = ps2("ps_trh", [128, D], BF16)
                b_ptrh = Buf("ptrh")
                cp(V, gmix[:], pcol[:, C_LNMIX:C_LNMIX + 8].unsqueeze(2).to_broadcast([128, 8, 128]), [b_par], [b_g])
                cqT = sb2("cqT", [128, 2, 512], BF16)
                ckvT = sb2("ckvT", [128, 512], BF16)
                b_cqT = bufs(4, "cqT")
                b_ckvT = bufs(4, "ckvT")
                psA = ps2("psA", [128, 512])
                b_psA = Buf("psA")
                ps_t = ps2("ps_t", [128, 1024], BF16)
                b_pst = Buf("pst")
                psQ = ps2("psQ", [128, 1024])
                b_psQ = Buf("psQ")
                psKV = ps2("psKV", [128, 1024])
                b_psKV = Buf("psKV")
                ps_t2 = ps2("ps_t2", [128, 1024], BF16)
                b_pst2 = Buf("pst2")
                cn = [sb2(f"cn{i}", [128, 384], BF16) for i in range(2)]
                b_cn = bufs(2, "cn")
                st = sb2("st", [128, NT, 8])
                b_st = bufs(NT, "st")
                scr = sb2("scr2", [128, 512], BF16)
                b_scr = Buf("scr2")
                scrb = sb2("scrb", [128, 416], BF16)
                b_scrb = Buf("scrb")
                kpe = sb2("kpe", [128, NT, 32])
                b_kpe = bufs(NT, "kpe")
                kpr = sb2("kpr", [128, NT, 32])
                scrq = sb2("scrq", [128, 768], BF16)
                b_scrq = Buf("scrq")
                ktmp = sb2("ktmp", [128, 4, 16])
                b_kr = Buf("kr")
                ssh = sb2("ssh", [128, 4, 8])
                b_sshk = Buf("sshk")
                b_sshq = Buf("sshq")
                qn = sb2("qn", [128, 8, 96])
                qr = sb2("qr", [128, 8, 32])
                qtmp = sb2("qtmp", [128, 4, 8, 16])
                b_qn = Buf("qn")
                b_qr = Buf("qr")
                Qtm = [sb2(f"Qtm{i}", [128, 8, 96], BF16) for i in range(2)]
                Ktm = [sb2(f"Ktm{i}", [128, 8, 96], BF16) for i in range(2)]
                b_Qtm = bufs(2, "Qtm")
                b_Ktm = bufs(2, "Ktm")
                for t in range(NT):
                    memset(P, Vt[:, t, :, 64:65], 1.0, [b_V[t]])

                def tk(t):
                    return slice(t * 128, (t + 1) * 128)

                def s0(t):
                    i3 = t % 3
                    dma(xt[i3][:], x_d[tk(t), :], [], [b_xt[i3]], f"x{i3}")
                    act(sq_scr[:], xt[i3][:], AF.Square, [b_xt[i3]], [b_sqs, b_ss[t]], accum=ssx[:, t, 0:1])
                    act(ssx[:, t, 1:2], ssx[:, t, 0:1], AF.Sqrt, [b_ss[t]], [b_ss[t]], bias=epsb[:, 0:1], scale=1.0 / D)
                    recip(ssx[:, t, 1:2], ssx[:, t, 1:2], [b_ss[t]], [b_ss[t]])

                def s1(t):
                    i3, i2 = t % 3, t % 2
                    K.op(A, lambda: nc.scalar.mul(xn[i2][:], xt[i3][:], ssx[:, t, 1:2]), [b_xt[i3], b_ss[t]], [b_xn[i2]])

                def s2(t):
                    i2 = t % 2
                    for dc in range(8):
                        tr(ps_trh[:, dc * 128:(dc + 1) * 128], xn[i2][:, dc * 128:(dc + 1) * 128], ident_b[:], [b_xn[i2], b_const], [b_ptrh])

                def s3(t):
                    tt(V, hT[:, :, tk(t)], ps_trh[:].rearrange("p (c k) -> p c k", k=128), gmix[:], ALU.mult, [b_ptrh, b_g], [b_hT[t]])

                def s4(t):
                    for dc in range(8):
                        mm(psA[:, 0:416], hT[:, dc, tk(t)], wA[:, dc, :], dc == 0, dc == 7, [b_hT[t], b_wA[dc]], [b_psA])

                def s5(t):
                    i2 = t % 2
                    S_ = st[:, t, :]
                    act(scrb[:, 0:256], psA[:, 0:256], AF.Square, [b_psA], [b_scrb, b_st[t]], accum=S_[:, 0:1])
                    act(scrb[:, 256:384], psA[:, 256:384], AF.Square, [b_psA], [b_scrb, b_st[t]], accum=S_[:, 1:2])
                    act(scrb[:, 384:416], psA[:, 384:416], AF.Square, [b_psA], [b_scrb, b_st[t]], accum=S_[:, 2:3])
                    act(S_[:, 3:4], S_[:, 0:1], AF.Sqrt, [b_st[t]], [b_st[t]], bias=epsb[:, 0:1], scale=1.0 / 256)
                    act(S_[:, 4:5], S_[:, 1:2], AF.Sqrt, [b_st[t]], [b_st[t]], bias=epsb[:, 0:1], scale=1.0 / 128)
                    recip(S_[:, 5:7], S_[:, 3:5], [b_st[t]], [b_st[t]])
                    ts(V, cn[i2][:, 0:256], psA[:, 0:256], S_[:, 5:6], None, ALU.mult, None, [b_psA, b_st[t]], [b_cn[i2]])
                    ts(V, cn[i2][:, 256:384], psA[:, 256:384], S_[:, 6:7], None, ALU.mult, None, [b_psA, b_st[t]], [b_cn[i2]])
                    tt(V, kpe[:, t, :], psA[:, 384:416], pbc[:, B_GKR:B_GKR + 32], ALU.mult, [b_psA, b_par], [b_kpe[t]])

                def s5k(t):
                    kp = kpe[:, t, :]
                    cs2 = sincos[:, :, t, :]
                    tt(P, ktmp[:, 0:2, :], kp[:, 0:16].unsqueeze(1).to_broadcast([128, 2, 16]), cs2, ALU.mult, [b_kpe[t], b_rope], [b_kr])
                    tt(P, ktmp[:, 2:4, :], kp[:, 16:32].unsqueeze(1).to_broadcast([128, 2, 16]), cs2, ALU.mult, [b_kpe[t], b_rope], [b_kr])
                    tt(P, kpr[:, t, 0:16], ktmp[:, 1, :], ktmp[:, 2, :], ALU.subtract, [b_kr], [b_kpe[t]])
                    tt(P, kpr[:, t, 16:32], ktmp[:, 3, :], ktmp[:, 0, :], ALU.add, [b_kr], [b_kpe[t]])

                def s6(t):
                    i2 = t % 2
                    for c in range(3):
                        tr(ps_t[:, c * 128:(c + 1) * 128], cn[i2][:, c * 128:(c + 1) * 128], ident_b[:], [b_cn[i2], b_const], [b_pst])

                def s7(t):
                    r4 = t % 4
                    for c in range(2):
                        K.op(A, lambda c=c: nc.scalar.mul(cqT[:, c, tk(r4)], ps_t[:, c * 128:(c + 1) * 128], pcol[:, C_QA + c:C_QA + c + 1]),
                             [b_pst, b_par], [b_cqT[r4]])
                    K.op(A, lambda: nc.scalar.mul(ckvT[:, tk(r4)], ps_t[:, 256:384], pcol[:, C_KVA:C_KVA + 1]), [b_pst, b_par], [b_ckvT[r4]])

                def s8(t):
                    r4 = t % 4
                    for (lo, hi) in ((0, 512), (512, 768)):
                        for c in range(2):
                            mm(psQ[:, lo:hi], cqT[:, c, tk(r4)], wuq[:, c, lo:hi], c == 0, c == 1, [b_cqT[r4], b_wq], [b_psQ])
                    for (lo, hi) in ((0, 512), (512, 1024)):
                        mm(psKV[:, lo:hi], ckvT[:, tk(r4)], wukv[:, lo:hi], True, True, [b_ckvT[r4], b_wq], [b_psKV])

                def s9(t):
                    i2 = t % 2
                    S_ = st[:, t, :]
                    q3 = psQ[:, 0:768].rearrange("p (h d) -> p h d", d=96)
                    kv3 = psKV[:].rearrange("p (h d) -> p h d", d=128)
                    sk = scr[:, 0:512].rearrange("p (h d) -> p h d", d=64)
                    sq_ = scrq[:, 0:768].rearrange("p (h d) -> p h d", d=96)
                    act(sk, kv3[:, :, 0:64], AF.Square, [b_psKV], [b_scr])
                    act(sq_, q3, AF.Square, [b_psQ], [b_scrq])
                    cp(A, Vt[:, t, :, 0:64], kv3[:, :, 64:128], [b_psKV], [b_V[t]])
                    red(ssh[:, 0, :], sk, [b_scr], [b_sshk])
                    red(ssh[:, 2, :], sq_, [b_scrq], [b_sshq])
                    ts(V, ssh[:, 0, :], ssh[:, 0, :], S_[:, 2:3], None, ALU.add, None, [b_sshk, b_st[t]], [b_sshk])

                def s9b(t):
                    i2 = t % 2
                    q3 = psQ[:, 0:768].rearrange("p (h d) -> p h d", d=96)
                    kv3 = psKV[:].rearrange("p (h d) -> p h d", d=128)
                    act(ssh[:, 1, :], ssh[:, 0, :], AF.Sqrt, [b_sshk], [b_sshk], bias=epsb[:, 0:1], scale=1.0 / 96)
                    act(ssh[:, 3, :], ssh[:, 2, :], AF.Sqrt, [b_sshq], [b_sshq], bias=epsb[:, 0:1], scale=1.0 / 96)
                    recip(ssh[:, 1, :], ssh[:, 1, :], [b_sshk], [b_sshk])
                    recip(ssh[:, 3, :], ssh[:, 3, :], [b_sshq], [b_sshq])
                    rk = ssh[:, 1, :]
                    rq = ssh[:, 3, :]
                    tt(V, qn[:], q3, rq.unsqueeze(2).to_broadcast([128, 8, 96]), ALU.mult, [b_psQ, b_sshq], [b_qn])
                    tt(V, Ktm[i2][:, :, 0:64], kv3[:, :, 0:64], rk.unsqueeze(2).to_broadcast([128, 8, 64]), ALU.mult,
                       [b_psKV, b_sshk], [b_Ktm[i2]])
                    cp(V, Qtm[i2][:, :, 0:64], qn[:, :, 0:64], [b_qn], [b_Qtm[i2]])
                    tt(P, Ktm[i2][:, :, 64:96], kpr[:, t, :].unsqueeze(1).to_broadcast([128, 8, 32]),
                       rk.unsqueeze(2).to_broadcast([128, 8, 32]), ALU.mult, [b_kpe[t], b_sshk], [b_Ktm[i2]])
                    t1q = qsc[:, :, t, 0, :].unsqueeze(2).to_broadcast([128, 2, 8, 16])
                    t2q = qsc[:, :, t, 1, :].unsqueeze(2).to_broadcast([128, 2, 8, 16])
                    tt(P, qtmp[:, 0:2], qn[:, :, 64:80].unsqueeze(1).to_broadcast([128, 2, 8, 16]), t1q, ALU.mult, [b_qn, b_rope], [b_qr])
                    tt(P, qtmp[:, 2:4], qn[:, :, 80:96].unsqueeze(1).to_broadcast([128, 2, 8, 16]), t2q, ALU.mult, [b_qn, b_rope], [b_qr])
                    tt(P, Qtm[i2][:, :, 64:80], qtmp[:, 1], qtmp[:, 2], ALU.subtract, [b_qr], [b_Qtm[i2]])
                    tt(P, Qtm[i2][:, :, 80:96], qtmp[:, 3], qtmp[:, 0], ALU.add, [b_qr], [b_Qtm[i2]])

                def s10k(t):
                    i2 = t % 2
                    for h in range(8):
                        tr(ps_t2[0:96, h * 128:(h + 1) * 128], Ktm[i2][:, h, :], ident_b[:], [b_Ktm[i2], b_const], [b_pst2])

                def s10q(t):
                    i2 = t % 2
                    for h in range(8):
                        tr(ps_t[0:96, h * 128:(h + 1) * 128], Qtm[i2][:, h, :], ident_b[:], [b_Qtm[i2], b_const], [b_pst])

                def s11k(t):
                    K.op(A, lambda: nc.scalar.mul(KT[:, :, tk(t)], ps_t2[0:96, :].rearrange("p (h k) -> p h k", k=128), pcol[0:96, C_GK:C_GK + 1]),
                         [b_pst2, b_par], [b_KT[t]])

                def s11q(t):
                    ts(V, QT[:, :, tk(t)], ps_t[0:96, :].rearrange("p (h k) -> p h k", k=128), pcol[0:96, C_GQ:C_GQ + 1], None,
                       ALU.mult, None, [b_pst, b_par], [b_QT[t]])

                slots = [(5, s5), (9, lambda t: (s10k(t), s11k(t), s10q(t), s11q(t))), (8, s9), (1, s1), (3, s3), (8, s9b), (5, s5k),
                         (6, lambda t: (s6(t), s7(t))), (4, s4), (2, s2), (0, s0), (7, s8)]
                for step in range(NT + 9):
                    for si, fn_ in slots:
                        t_ = step - si
                        if 0 <= t_ < NT:
                            fn_(t_)
                dbg_copy("ssx", ssx[:], b_ss)
                dbg_copy("xn0", xn[0][:, 0:256], b_xn)
                dbg_copy("hT2", hT[:, 0, 0:256], b_hT)
            K.barrier()
            if "QT" in dbg_d:
                if "hT" in dbg_d:
                    with ExitStack() as phd:
                        htf = phd.enter_context(nc.sbuf_tensor("s_htf", [128, 8, 256], F32))
                        b_dh = Buf("dbgh")
                        cp(V, htf[:], hT[:, :, 0:256], b_hT, [b_dh])
                        dump("hT", htf[:], [b_dh])
                    K.barrier()
                qtf = sbp("qtf", [96, 8, 256])
                ktf = sbp("ktf", [96, 8, 256])
                vtf = sbp("vtf", [128, 2, 8, 65])
                b_d = Buf("dbg")
                cp(V, qtf[:], QT[:, :, 0:256], b_QT, [b_d])
                cp(V, ktf[:], KT[:, :, 0:256], b_KT, [b_d])
                cp(V, vtf[:], Vt[:, 0:2], b_V, [b_d])
                dump("QT", qtf[:], [b_d])
                dump("KT", ktf[:], [b_d])
                dump("Vt", vtf[:], [b_d])

            with ExitStack() as ph3:
                sb3 = lambda name, shape, dt=F32: ph3.enter_context(nc.sbuf_tensor("s_" + name, shape, dt))
                ps3 = lambda name, shape, dt=F32: ph3.enter_context(nc.psum_tensor("p_" + name, shape, dt))
                NPL = 2
                ps_l = [ps3(f"ps_l{i}", [128, 1024]) for i in range(NPL)]
                b_psl = bufs(NPL, "psl")
                ps_o = [ps3(f"ps_o{i}", [65, 512]) for i in range(2)]
                b_pso = bufs(2, "pso")
                ps_d = ps3("ps_d", [64, 512])
                b_psd = Buf("psd")
                ps_s = ps3("ps_s", [64, 512])
                b_pss = Buf("pss")
                NPT = 3
                PT = [sb3(f"PT{i}", [128, 1024], BF16) for i in range(NPT)]
                b_PT = bufs(NPT, "PT")
                osb = [sb3(f"osb{i}", [65, 512]) for i in range(2)]
                b_osb = bufs(2, "osb")
                rden = sb3("rden", [64, 512])
                b_rden = Buf("rden")
                att2 = [sb3(f"att{i}", [64, 8, 512]) for i in range(2)]
                b_att2 = [bufs(8, f"att{i}_") for i in range(2)]
                sqa2 = [sb3(f"sqa{i}", [64, 8, 512], BF16) for i in range(2)]
                b_sqa2 = [bufs(8, f"sqa{i}_") for i in range(2)]
                rstd_a = sb3("rstd_a", [64, 512])
                b_rsa = Buf("rsa")
                modd = sb3("modd", [64, 4, 512], BF16)
                b_modd = Buf("modd")
                dh = sb3("dh", [128, 2, 512], BF16)
                sel_b = sb3("sel_b", [128, 64], BF16)
                dtmp = sb3("dtmp", [65, 512])
                b_dh = Buf("dh")
                sel65 = sb3("sel65", [65, 64])
                b_sel = Buf("sel")
                memset(P, sel65[:], 0.0, [b_sel])
                memset(P, sel65[64:65, :], 1.0, [b_sel])
                memset(P, sel_b[:], 0.0, [b_sel])
                memset(P, sel_b[64:65, :], 1.0, [b_sel])
                memset(P, dh[:], 0.0, [b_dh])
                scale = 96 ** -0.5
                iters = [(qc, h, kp) for qc in range(4) for h in range(8) for kp in range(NT // 2)]
                N_IT = len(iters)
                deferred = {}

                def defer(step, fn):
                    deferred.setdefault(step, []).append(fn)

                def qsl(qc):
                    return slice(qc * 512, (qc + 1) * 512)

                def st_L(i):
                    qc, h, kp = iters[i]
                    il = i % NPL
                    for j in range(2):
                        kt = 2 * kp + j
                        mm(ps_l[il][:, j * 512:(j + 1) * 512], KT[:, h, kt * 128:(kt + 1) * 128], QT[:, h, qsl(qc)], True, True,
                           [b_KT[kt]] + b_QT[qc * 4:(qc + 1) * 4], [b_psl[il]])

                def st_E(i):
                    il, ip = i % NPL, i % NPT
                    act(PT[ip][:], ps_l[il][:], AF.Exp, [b_psl[il]], [b_PT[ip]], scale=scale)

                def st_PV(i):
                    qc, h, kp = iters[i]
                    hh = i // (NT // 2)
                    ip = i % NPT
                    for j in range(2):
                        kt = 2 * kp + j
                        mm(ps_o[hh % 2][:], Vt[:, kt, h, :], PT[ip][:, j * 512:(j + 1) * 512], kt == 0, kt == NT - 1,
                           [b_V[kt], b_PT[ip]], [b_pso[hh % 2]])

                def tailA(hh):
                    cp(V, osb[hh % 2][:], ps_o[hh % 2][:], [b_pso[hh % 2]], [b_osb[hh % 2]])

                def tailA2(hh):
                    o64 = osb[hh % 2][64:65, :]
                    cp(V, dh[64:65, 0, :], o64, [b_osb[hh % 2]], [b_dh])
                    cp(V, dtmp[64:65, :], dh[64:65, 0, :], [b_dh], [b_dh])
                    tt(V, dh[64:65, 1, :], o64, dtmp[64:65, :], ALU.subtract, [b_osb[hh % 2], b_dh], [b_dh])

                def tailB(hh):
                    mm(ps_d[:], sel_b[:], dh[:, 0, :], True, False, [b_sel, b_dh], [b_psd])
                    mm(ps_d[:], sel_b[:], dh[:, 1, :], False, True, [b_sel, b_dh], [b_psd])

                def tailC(hh):
                    h = hh % 8
                    att, b_att, sqa, b_sqa = att2[(hh // 8) % 2], b_att2[(hh // 8) % 2], sqa2[(hh // 8) % 2], b_sqa2[(hh // 8) % 2]
                    recip(rden[:], ps_d[:], [b_psd], [b_rden])
                    tt(V, att[:, h, :], osb[hh % 2][0:64, :], rden[:], ALU.mult, [b_osb[hh % 2], b_rden], [b_att[h]])
                    tt(P, sqa[:, h, :], att[:, h, :], att[:, h, :], ALU.mult, [b_att[h]], [b_sqa[h]])

                def normA(qc):
                    sqa, b_sqa = sqa2[qc % 2], b_sqa2[qc % 2]
                    for h in range(8):
                        mm(ps_s[:], ones_b[0:64, 0:64], sqa[:, h, :], h == 0, h == 7, [b_const, b_sqa[h]], [b_pss])

                def normB(qc):
                    act(rstd_a[:], ps_s[:], AF.Ln, [b_pss], [b_rsa], bias=epsb[0:64, 0:1], scale=1.0 / 512)
                    act(rstd_a[:], rstd_a[:], AF.Exp, [b_rsa], [b_rsa], scale=-0.5)

                def normC(qc, hp):
                    att, b_att = att2[qc % 2], b_att2[qc % 2]
                    h = 2 * hp
                    stt(mixA[0:64, hp, qsl(qc)], att[:, h, :], pcol[0:64, C_GATT + h:C_GATT + h + 1], rstd_a[:], ALU.mult, ALU.mult,
                        [b_att[h], b_par, b_rsa], [b_mixA[qc]])
                    h = 2 * hp + 1
                    stt(modd[:, hp, :], att[:, h, :], pcol[0:64, C_GATT + h:C_GATT + h + 1], rstd_a[:], ALU.mult, ALU.mult,
                        [b_att[h], b_par, b_rsa], [b_modd])
                    if hp == 3:
                        dma(mixA[64:128, :, qsl(qc)], modd[:], [b_modd], [b_mixA[qc]], "modd")

                for step in range(N_IT + 36):
                    if step < N_IT:
                        st_L(step)
                    if 0 <= step - 1 < N_IT:
                        st_E(step - 1)
                    i = step - 2
                    if 0 <= i < N_IT:
                        st_PV(i)
                        qc, h, kp = iters[i]
                        if kp == NT // 2 - 1:
                            hh = i // (NT // 2)
                            defer(step + 1, lambda hh=hh: tailA(hh))
                            defer(step + 1, lambda hh=hh: tailA2(hh))
                            defer(step + 7, lambda hh=hh: tailB(hh))
                            defer(step + 8, lambda hh=hh: tailC(hh))
                            if h == 7:
                                defer(step + 18, lambda qc=qc: normA(qc))
                                defer(step + 22, lambda qc=qc: normB(qc))
                                for hp in range(4):
                                    defer(step + 23 + 2 * hp, lambda qc=qc, hp=hp: normC(qc, hp))
                    for fn in deferred.pop(step, []):
                        fn()
                assert not deferred
        if "mixA" in dbg_d:
            K.barrier()
            with ExitStack() as phd:
                mf = phd.enter_context(nc.sbuf_tensor("mixAf", [128, 4, 512], F32))
                b_d = Buf("dbg2")
                cp(V, mf[:], mixA[:, :, 0:512], b_mixA, [b_d])
                dump("mixA", mf[:], [b_d])
            K.barrier()

        mixS = sb("mixS", [128, 4, S], BF16)
        b_mixS = bufs(NT, "mixS")
        wo = sb("wo", [128, 8, D], BF16)
        b_wo = bufs(8, "wo")

        with ExitStack() as phS:
            sbS = lambda name, shape, dt=F32: phS.enter_context(nc.sbuf_tensor("s_" + name, shape, dt))
            zact = sbS("zact", [128, NT, 512], BF16)
            xbcT = sbS("xbcT", [128, 8, S], BF16)
            xs_tm = sbS("xs_tm", [128, NT, 512], BF16)
            B_tm = sbS("B_tm", [128, NT, 256], BF16)
            dtsb = sbS("dtsb", [128, 2, NT, 8])
            b_zact = bufs(NT, "zact")
            b_xbcT = [bufs(4, f"xbcT{cc}_") for cc in range(8)]
            b_xs = [bufs(2, f"xs{cc}_") for cc in range(4)]
            b_Btm = [bufs(2, f"Btm{g}_") for g in range(2)]
            b_dt = bufs(NT, "dt")
            Tm = sbS("Tm", [128, 2, 128])
            nmask = sbS("nmask", [128, 2, 128], BF16)
            dskipI = sbS("dskipI", [128, 8, 128], BF16)
            gssm = sbS("gssm", [128, 4, 128])
            a_bc = sbS("a_bc", [128, 16])
            da = sbS("da", [128, 2, NT, 8])
            acs = sbS("acs", [128, 2, NT, 8])
            negu = sbS("negu", [128, 2, NT, 8])
            eacs = sbS("eacs", [128, 2, NT, 8])
            wsb = sbS("wsb", [128, 2, NT, 8])
            cdec = sbS("cdec", [128, 2, NT, 8])
            tmp5 = sbS("tmp5", [128, 256])
            fl = lambda a: a[:].rearrange("p d c h -> p (d c h)")
            wBz_v = xbcT[:, 0:3, :].rearrange("p a b -> p (a b)")[:, 0:8 * 528].rearrange("p (c n) -> p c n", n=528)
            b_fence = Buf("fence")
            b_wBz = bufs(8, "wBz")
            b_wBd = bufs(8, "wBd")
            memset(P, fence_t[:], 0.0, b_KT + b_QT + b_V + [b_fence])
            for dc in range(8):
                dma(wBz_v[:, dc, 0:512], w_in_d[dc * 128:(dc + 1) * 128, 416:928], [b_fence], [b_wBz[dc]], f"wBz{dc}", q=P)
            dma(wBz_v[:, :, 512:528], w_in_d[:, 1952:1968].rearrange("(c p) n -> p c n", p=128), [b_fence], b_wBd, "wBd", q=P)
            K.barrier()
            b_set = Buf("ssdset")
            b_gssm = Buf("gssm")
            b_abc = Buf("abc")

            def ssd_consts():
                memset(P, Tm[:], 1.0, [b_set])
                K.op(P, lambda: nc.gpsimd.affine_select(out=Tm[:, 0, :], in_=Tm[:, 0, :], pattern=[[1, 128]], compare_op=ALU.is_ge,
                                                        fill=0.0, base=0, channel_multiplier=-1), [b_set], [b_set])
                K.op(P, lambda: nc.gpsimd.affine_select(out=Tm[:, 1, :], in_=Tm[:, 1, :], pattern=[[-1, 128]], compare_op=ALU.is_ge,
                                                        fill=0.0, base=0, channel_multiplier=1), [b_set], [b_set])
                memset(P, nmask[:], 0.0, [b_set])
                K.op(P, lambda: nc.gpsimd.affine_select(out=nmask[:, 0, :], in_=nmask[:, 0, :], pattern=[[1, 128]], compare_op=ALU.is_ge,
                                                        fill=NEG, base=0, channel_multiplier=-1), [b_set], [b_set])
                K.op(P, lambda: nc.gpsimd.affine_select(out=nmask[:, 1, :], in_=nmask[:, 1, :], pattern=[[-1, 128]], compare_op=ALU.is_ge,
                                                        fill=NEG, base=0, channel_multiplier=1), [b_set], [b_set])
                for h in range(8):
                    ts(P, dskipI[:, h, :], ident_b[:], pbc[:, B_DSKIP + h:B_DSKIP + h + 1], None, ALU.mult, None, [b_const, b_par], [b_set])
                cp(V, gssm[:], pcol[:, C_GSSM:C_GSSM + 4].unsqueeze(2).to_broadcast([128, 4, 128]), [b_par], [b_gssm])
                act(a_bc[:], pbc[:, B_ALOG:B_ALOG + 16], AF.Exp, [b_par], [b_abc])
                ts(V, a_bc[:], a_bc[:], -1.0, None, ALU.mult, None, [b_abc], [b_abc])


            with ExitStack() as ph4:
                sb4 = lambda name, shape, dt=F32: ph4.enter_context(nc.sbuf_tensor("s_" + name, shape, dt))
                ps4 = lambda name, shape, dt=F32: ph4.enter_context(nc.psum_tensor("p_" + name, shape, dt))
                wB = sb4("wB", [128, 8, 1024], BF16)
                b_wB = bufs(8, "wB")
                for dc in range(8):
                    dma(wB[:, dc, :], w_in_d[dc * 128:(dc + 1) * 128, 928:1952], b_wBz, [b_wB[dc]], f"wB{dc}", q=P)
                ssd_consts()
                diag = sb4("diag", [128, 5, 8, 128], BF16)
                b_diag = Buf("diag")
                tt(V, diag[:].rearrange("p k c n -> p (k c) n"), ident_b[:].unsqueeze(1).to_broadcast([128, 40, 128]),
                   pcol[:, C_CONVW:C_CONVW + 40].unsqueeze(2).to_broadcast([128, 40, 128]), ALU.mult, [b_const, b_par], [b_diag])
                raw = [sb4(f"raw{i}", [128, S + 4], BF16) for i in range(2)]
                b_raw = bufs(2, "raw")
                for i in range(2):
                    memset(P, raw[i][:, 0:2], 0.0, [b_raw[i]])
                    memset(P, raw[i][:, S + 2:S + 4], 0.0, [b_raw[i]])
                psZ = [ps4(f"psZ{i}", [128, 512]) for i in range(2)]
                b_psZ = bufs(2, "psZ")
                psDT = ps4("psDT", [128, 512])
                b_psDT = Buf("psDT")
                psX = [ps4(f"psX{i}", [128, 512]) for i in range(2)]
                b_psX = bufs(2, "psX")
                psC = [ps4(f"psC{i}", [128, 512]) for i in range(2)]
                b_psC = bufs(2, "psC")
                ps_tr4 = ps4("ps_tr4", [128, 1024], BF16)
                b_ptr4 = Buf("ptr4")
                dtx = sb4("dtx", [128, NT, 16])
                b_dtx = bufs(NT, "dtx")
                def p4a(t):
                    tok = slice(t * 128, (t + 1) * 128)
                    i2 = t % 2
                    for dc in range(8):
                        mm(psZ[i2][:], hT[:, dc, tok], wBz_v[:, dc, 0:512], dc == 0, dc == 7, [b_hT[t], b_wBz[dc]], [b_psZ[i2]])

                def p4b(t):
                    i2 = t % 2
                    act(zact[:, t, :], psZ[i2][:], AF.Silu, [b_psZ[i2]], [b_zact[t]])

                run_pipeline([p4a, p4b], NT)

                def p4d(t):
                    tok = slice(t * 128, (t + 1) * 128)
                    for dc in range(8):
                        mm(psDT[:, 0:16], hT[:, dc, tok], wBz_v[:, dc, 512:528], dc == 0, dc == 7, [b_hT[t], b_wBd[dc]], [b_psDT])

                def p4e(t):
                    tt(V, dtx[:, t, :], psDT[:, 0:16], pbc[:, B_DTB:B_DTB + 16], ALU.add, [b_psDT, b_par], [b_dtx[t]])

                run_pipeline([p4d, p4e], NT)
                act(dtx[:], dtx[:], AF.Exp, b_dtx, b_dtx)
                act(dtsb[:].rearrange("p d c h -> p c d h"), dtx[:].rearrange("p c (d h) -> p c d h", h=8), AF.Ln, b_dtx, b_dt, bias=1.0)

                ps_acs = psDT[:, 0:256]
                ps_tot = psDT[:, 256:512]
                tt(V, da[:], dtsb[:], a_bc[:].rearrange("p (d h) -> p d h", h=8).unsqueeze(2).to_broadcast([128, 2, NT, 8]), ALU.mult,
                   [b_set, b_abc] + b_dt, [b_set])
                for d_ in range(2):
                    mm(ps_acs[:, d_ * 128:(d_ + 1) * 128], Tm[:, d_, :], da[:, d_].rearrange("p c h -> p (c h)"), True, True,
                       [b_set], [b_psDT])
                mm(ps_tot, ones_f[:], fl(da), True, True, [b_set, b_const], [b_psDT])
                cp(V, fl(acs), ps_acs, [b_psDT], [b_set])
                act(fl(negu), fl(dtsb), AF.Ln, b_dt, [b_set])
                tt(V, fl(negu), fl(negu), fl(acs), ALU.subtract, [b_set], [b_set])
                act(fl(eacs), fl(acs), AF.Exp, [b_set], [b_set])
                tt(V, tmp5[:], ps_tot, fl(acs), ALU.subtract, [b_psDT, b_set], [b_set])
                act(tmp5[:], tmp5[:], AF.Exp, [b_set], [b_set])
                tt(V, fl(wsb), tmp5[:], fl(dtsb), ALU.mult, [b_set] + b_dt, [b_set])
                act(fl(cdec), ps_tot, AF.Exp, [b_psDT], [b_set])

                cnt4 = [0, 0]

                def p4x(cc):
                    rw, brw = raw[cc % 2], b_raw[cc % 2]
                    for j in range(4):
                        i2 = cnt4[0] % 2
                        cnt4[0] += 1
                        for dc in range(8):
                            mm(psX[i2][:], wB[:, dc, cc * 128:(cc + 1) * 128], hT[:, dc, j * 512:(j + 1) * 512],
                               dc == 0, dc == 7, b_hT[j * 4:(j + 1) * 4] + [b_wB[dc]], [b_psX[i2]])
                        cp(V, rw[:, 2 + j * 512:2 + (j + 1) * 512], psX[i2][:], [b_psX[i2]], [brw])

                def p4c(cc):
                    rw, brw = raw[cc % 2], b_raw[cc % 2]
                    for j in range(4):
                        i2 = cnt4[1] % 2
                        cnt4[1] += 1
                        for k in range(5):
                            mm(psC[i2][:], diag[:, k, cc, :], rw[:, j * 512 + k:j * 512 + k + 512], k == 0, k == 4,
                               [b_diag, brw], [b_psC[i2]])
                        act(xbcT[:, cc, j * 512:(j + 1) * 512], psC[i2][:], AF.Silu, [b_psC[i2], b_par],
                            [b_xbcT[cc][j]] + ((b_wBz + b_wBd) if cc < 3 else []), bias=pcol[:, C_CONVB + cc:C_CONVB + cc + 1])

                def p4t(cc):
                    if cc >= 6:
                        return
                    for half in range(2):
                        for tl in range(8):
                            t = half * 8 + tl
                            tr(ps_tr4[:, tl * 128:(tl + 1) * 128], xbcT[:, cc, t * 128:(t + 1) * 128], ident_b[:],
                               [b_xbcT[cc][t // 4], b_const], [b_ptr4])
                        src = ps_tr4[:].rearrange("p (t k) -> p t k", k=128)
                        if cc < 4:
                            cp(V, xs_tm[:, half * 8:(half + 1) * 8, cc * 128:(cc + 1) * 128], src, [b_ptr4], [b_xs[cc][half]])
                        else:
                            cp(V, B_tm[:, half * 8:(half + 1) * 8, (cc - 4) * 128:(cc - 3) * 128], src, [b_ptr4], [b_Btm[cc - 4][half]])

                run_pipeline([p4x, p4c, p4t], 8)
            K.barrier()
            if "xbcT" in dbg_d:
                with ExitStack() as phd:
                    xf = phd.enter_context(nc.sbuf_tensor("s_xbcTf", [128, 8, 512], F32))
                    xsf = phd.enter_context(nc.sbuf_tensor("s_xsf", [128, 2, 512], F32))
                    b_d = Buf("dbg3")
                    cp(V, xf[:], xbcT[:, :, 0:512], [], [b_d])
                    cp(V, xsf[:], xs_tm[:, 0:2, :], [], [b_d])
                    dump("xbcT", xf[:], [b_d])
                    dump("xs_tm", xsf[:], [b_d])
                    dump("dtsb", dtsb[:], [b_d])
                K.barrier()

            for c_ in range(8):
                dma(wo[:, c_, :], w_out_d[c_ * 128:(c_ + 1) * 128, :], [], [b_wo[c_]], f"wo{c_}", q=P)
            with ExitStack() as ph5:
                sb5 = lambda name, shape, dt=F32: ph5.enter_context(nc.sbuf_tensor("s_" + name, shape, dt))
                ps5 = lambda name, shape, dt=F32: ph5.enter_context(nc.psum_tensor("p_" + name, shape, dt))
                ypart = hT[:].bitcast(F32).rearrange("p a (b c) -> p (a b) c", c=512)
                b_yp = bufs(NT, "ypart")
                state = [sb5(f"state{d_}", [128, 512]) for d_ in range(2)]
                prevbf = [sb5(f"prevbf{d_}", [128, 512], BF16) for d_ in range(2)]
                b_state = bufs(2, "state")
                b_prev = bufs(2, "prev")
                Et = [sb5(f"Et{i}", [128, 8, 128], BF16) for i in range(2)]
                Mt = [sb5(f"Mt{i}", [128, 8, 128], BF16) for i in range(2)]
                b_Et = bufs(2, "Et")
                b_Mt = bufs(2, "Mt")
                xsw = [sb5(f"xsw{i}", [128, 512], BF16) for i in range(2)]
                b_xsw = bufs(2, "xsw")
                t1s = [sb5(f"t1_{i}", [128, 512]) for i in range(2)]
                b_t1s = bufs(2, "t1")
                t1, b_t1 = t1s[0], b_t1s[0]
                print("[kernel] ph5 sbuf remaining", nc.sbuf_bytes_remaining)
                yg = sb5("yg", [128, 512])
                b_yg = Buf("yg")
                ynb = sb5("ynb", [128, 512], BF16)
                b_ynb = Buf("ynb")
                scr5 = sb5("scr5", [128, 512], BF16)
                ss5 = sb5("ss5", [128, NT, 4])
                b_ss5 = bufs(NT, "ss5")
                ps_zz = [ps5(f"ps_zz{i}", [128, 512]) for i in range(4)]
                b_pzz = bufs(4, "pzz")
                ps_cbtr = ps5("ps_cbtr", [128, 512])
                ps_cb = ps_cbtr
                ps_tr5 = ps_cbtr[:].bitcast(BF16)
                b_pcb = Buf("pcb")
                b_ptr5 = b_pcb
                ps_y = ps5("ps_y", [128, 512])
                b_py = Buf("py")
                ps_st = ps5("ps_st", [128, 512])
                b_pst5 = Buf("pst5")
                ps_yo = ps5("ps_yo", [128, 512])
                b_pyo = Buf("pyo")
                h8 = lambda ap: ap.rearrange("p (h d) -> p h d", d=64)
                da_hi = sb5("da_hi", [128, 2, NT, 8], BF16)
                da_lo = sb5("da_lo", [128, 2, NT, 8], BF16)
                Tb16 = sb5("Tb16", [128, 2, 128], BF16)
                cp(V, da_hi[:], da[:], [b_set], [b_set])
                cp(V, t1[:, 0:256], fl(da_hi), [b_set], [b_t1])
                tt(V, fl(da_lo), fl(da), t1[:, 0:256], ALU.subtract, [b_set, b_t1], [b_set])
                cp(V, Tb16[:], Tm[:], [b_set], [b_set])

                def make_xsw(c, d_):
                    i2 = c % 2
                    tt(P, h8(xsw[i2][:]), h8(xs_tm[:, c, :]), wsb[:, d_, c, :].unsqueeze(2).to_broadcast([128, 8, 64]), ALU.mult,
                       [b_xs[cc_][c // 8] for cc_ in range(4)] + [b_set], [b_xsw[i2]])

                def states_and_yoff(c, d_, first, ti=0, cast_eng=P):
                    tokc = slice(c * 128, (c + 1) * 128)
                    i2 = c % 2
                    for g in range(2):
                        mm(ps_st[:, g * 256:(g + 1) * 256], B_tm[:, c, g * 128:(g + 1) * 128], xsw[i2][:, g * 256:(g + 1) * 256], True, True,
                           [b_Btm[g][c // 8], b_xsw[i2]], [b_pst5])
                    if not first:
                        for g in range(2):
                            mm(ps_yo[:, g * 256:(g + 1) * 256], xbcT[:, 6 + g, tokc], prevbf[d_][:, g * 256:(g + 1) * 256], True, True,
                               [b_xbcT[6 + g][c // 4], b_prev[d_]], [b_pyo])
                        tt(V, h8(state[d_][:]), h8(state[d_][:]), cdec[:, d_, c, :].unsqueeze(2).to_broadcast([128, 8, 64]), ALU.mult,
                           [b_state[d_], b_set], [b_state[d_]])
                        tt(V, state[d_][:], state[d_][:], ps_st[:], ALU.add, [b_state[d_], b_pst5], [b_state[d_]])
                        tt(V, h8(t1s[ti][:]), h8(ps_yo[:]), eacs[:, d_, c, :].unsqueeze(2).to_broadcast([128, 8, 64]), ALU.mult,
                           [b_pyo, b_set], [b_t1s[ti]])
                    else:
                        cp(V, state[d_][:], ps_st[:], [b_pst5], [b_state[d_]])
                    cp(cast_eng, prevbf[d_][:], state[d_][:], [b_state[d_]], [b_prev[d_]])

                units = [(c, g) for c in range(NT) for g in range(2)]

                def w1A(u):
                    c, g = units[u]
                    for d_ in range(2):
                        zb = (u % 2) * 2 + d_
                        for hl in range(4):
                            h = g * 4 + hl
                            o_ = ps_zz[zb][:, hl * 128:(hl + 1) * 128]
                            mm(o_, da_hi[:, d_, c, h:h + 1].to_broadcast([128, 128]), Tb16[:, d_, :], True, False, [b_set], [b_pzz[zb]])
                            mm(o_, da_lo[:, d_, c, h:h + 1].to_broadcast([128, 128]), Tb16[:, d_, :], False, False, [b_set], [b_pzz[zb]])
                            mm(o_, ident_b[:], nmask[:, d_, :], False, True, [b_set, b_const], [b_pzz[zb]])

                def w1B(u):
                    c, g = units[u]
                    i2 = u % 2
                    for d_ in range(2):
                        zb = (u % 2) * 2 + d_
                        for hl in range(4):
                            h = g * 4 + hl
                            act(Et[i2][:, d_ * 4 + hl, :], ps_zz[zb][:, hl * 128:(hl + 1) * 128], AF.Exp, [b_pzz[zb], b_set], [b_Et[i2]],
                                bias=negu[:, d_, c, h:h + 1])

                def w1C(u):
                    c, g = units[u]
                    i2 = u % 2
                    tokc = slice(c * 128, (c + 1) * 128)
                    mm(ps_cb[:, 0:128], xbcT[:, 4 + g, tokc], xbcT[:, 6 + g, tokc], True, True,
                       [b_xbcT[4 + g][c // 4], b_xbcT[6 + g][c // 4]], [b_pcb])
                    tt(V, Mt[i2][:], Et[i2][:], ps_cb[:, 0:128].unsqueeze(1).to_broadcast([128, 8, 128]), ALU.mult,
                       [b_Et[i2], b_pcb], [b_Mt[i2]])

                def w1D(u):
                    c, g = units[u]
                    i2 = u % 2
                    for hl in range(4):
                        h = g * 4 + hl
                        o_ = ps_y[:, h * 64:(h + 1) * 64]
                        r_ = xs_tm[:, c, h * 64:(h + 1) * 64]
                        brd = [b_xs[h // 2][c // 8]]
                        mm(o_, Mt[i2][:, hl, :], r_, True, False, [b_Mt[i2]] + brd, [b_py])
                        mm(o_, Mt[i2][:, 4 + hl, :], r_, False, False, [b_Mt[i2]] + brd, [b_py])
                        mm(o_, dskipI[:, h, :], r_, False, True, [b_set] + brd, [b_py])
                    if g == 1:
                        make_xsw(c, 0)

                def w1E(u):
                    c, g = units[u]
                    if g == 1:
                        cp(V, ypart[:, c, :], ps_y[:], [b_py], [b_yp[c]])
                        states_and_yoff(c, 0, c == 0)
                        if c > 0:
                            tt(P, ypart[:, c, :], ypart[:, c, :], t1[:], ALU.add, [b_yp[c], b_t1], [b_yp[c]])

                run_pipeline([w1A, w1B, w1C, w1D, w1E], len(units), order=[2, 4, 3, 1, 0])

                def w2x(i):
                    make_xsw(NT - 1 - i, 1)

                def w2a(i):
                    c = NT - 1 - i
                    states_and_yoff(c, 1, i == 0, ti=i % 2, cast_eng=A)

                def w2b(i):
                    c = NT - 1 - i
                    if i == 0:
                        tt(P, yg[:], ypart[:, c, :], zact[:, c, :], ALU.mult, [b_yp[c], b_zact[c]], [b_yg])
                    else:
                        tt(V, yg[:], ypart[:, c, :], t1s[i % 2][:], ALU.add, [b_yp[c], b_t1s[i % 2]], [b_yg])
                        tt(V, yg[:], yg[:], zact[:, c, :], ALU.mult, [b_yg, b_zact[c]], [b_yg])
                    S5 = ss5[:, c, :]
                    for g in range(2):
                        act(scr5[:, g * 256:(g + 1) * 256], yg[:, g * 256:(g + 1) * 256], AF.Square, [b_yg], [b_ss5[c]], accum=S5[:, g:g + 1])
                    act(S5[:, 2:4], S5[:, 0:2], AF.Sqrt, [b_ss5[c]], [b_ss5[c]], bias=epsb[:, 0:1], scale=1.0 / 256)

                def w2b2(i):
                    c = NT - 1 - i
                    S5 = ss5[:, c, :]
                    recip(S5[:, 2:4], S5[:, 2:4], [b_ss5[c]], [b_ss5[c]])
                    for g in range(2):
                        K.op(A, lambda g=g: nc.scalar.mul(ynb[:, g * 256:(g + 1) * 256], yg[:, g * 256:(g + 1) * 256], S5[:, 2 + g:3 + g]),
                             [b_yg, b_ss5[c]], [b_ynb])

                def w2c(i):
                    c = NT - 1 - i
                    tokc = slice(c * 128, (c + 1) * 128)
                    for j in range(4):
                        tr(ps_tr5[:, j * 128:(j + 1) * 128], ynb[:, j * 128:(j + 1) * 128], ident_b[:], [b_ynb, b_const], [b_ptr5])
                    tt(V, mixS[:, :, tokc], ps_tr5[:, 0:512].rearrange("p (j k) -> p j k", k=128), gssm[:], ALU.mult,
                       [b_ptr5, b_set, b_gssm], [b_mixS[c]])

                run_pipeline([w2x, w2a, w2b, w2b2, w2c], NT, order=[1, 0, 4, 3, 2])
        K.barrier()
        if "mixS" in dbg_d:
            with ExitStack() as phd:
                mf = phd.enter_context(nc.sbuf_tensor("s_mixSf", [128, 4, 512], F32))
                b_d = Buf("dbg4")
                cp(V, mf[:], mixS[:, :, 0:512], b_mixS, [b_d])
                dump("mixS", mf[:], [b_d])
            K.barrier()

        x1 = sb("x1", [128, NT, D])
        b_x1 = [bufs(2, f"x1_{t}_") for t in range(NT)]
        wup0 = sb("wup0", [128, 8, 512], BF16)
        wdn0 = sb("wdn0", [128, 4, D], BF16)
        b_wup = bufs(2, "wup")
        b_wdn = bufs(2, "wdn")
        dma(wup0[:], w_up_d[:, 0:512].rearrange("(c p) f -> p c f", p=128), [], [b_wup[0]], "wup0", q=P)
        dma(wdn0[:], w_dn_d[0:512, :].rearrange("(c p) n -> p c n", p=128), [], [b_wdn[0]], "wdn0", q=P)
        with ExitStack() as ph6:
            sb6 = lambda name, shape, dt=F32: ph6.enter_context(nc.sbuf_tensor("s_" + name, shape, dt))
            ps6 = lambda name, shape, dt=F32: ph6.enter_context(nc.psum_tensor("p_" + name, shape, dt))
            xt6 = [sb6(f"xt6_{i}", [128, D]) for i in range(2)]
            b_xt6 = bufs(2, "xt6")
            xn6 = [sb6(f"xn6_{i}", [128, D], BF16) for i in range(2)]
            b_xn6 = bufs(2, "xn6")
            scr6 = sb6("scr6", [128, D], BF16)
            gmlp = sb6("gmlp", [128, 8, 128])
            b_g6 = Buf("gmlp")
            ss6 = sb6("ss6", [128, NT, 2])
            b_ss6 = bufs(NT, "ss6")
            ps_o6 = [ps6(f"ps_o6_{i}", [128, D]) for i in range(2)]
            b_po6 = bufs(2, "po6")
            ps_t6 = [ps6(f"ps_t6_{i}", [128, D], BF16) for i in range(2)]
            b_pt6 = bufs(2, "pt6")
            cp(V, gmlp[:], pcol[:, C_LNMLP:C_LNMLP + 8].unsqueeze(2).to_broadcast([128, 8, 128]), [b_par], [b_g6])
            def p6a(t):
                i2 = t % 2
                tok = slice(t * 128, (t + 1) * 128)
                dma(xt6[i2][:], x_d[tok, :], [], [b_xt6[i2]], f"x6_{i2}")
                for half in range(2):
                    cs = slice(half * 512, (half + 1) * 512)
                    for j in range(4):
                        mm(ps_o6[i2][:, cs], mixA[:, j, tok], wo[:, j, cs], j == 0, False, [b_mixA[t // 4], b_wo[j]], [b_po6[i2]])
                    for j in range(4):
                        mm(ps_o6[i2][:, cs], mixS[:, j, tok], wo[:, 4 + j, cs], False, j == 3, [b_mixS[t], b_wo[4 + j]], [b_po6[i2]])

            def p6b(t):
                i2 = t % 2
                tt(V, x1[:, t, :], xt6[i2][:], ps_o6[i2][:], ALU.add, [b_xt6[i2], b_po6[i2]], b_x1[t])
                act(scr6[:], x1[:, t, :], AF.Square, b_x1[t], [b_ss6[t]], accum=ss6[:, t, 0:1])
                act(ss6[:, t, 1:2], ss6[:, t, 0:1], AF.Sqrt, [b_ss6[t]], [b_ss6[t]], bias=epsb[:, 0:1], scale=1.0 / D)
                recip(ss6[:, t, 1:2], ss6[:, t, 1:2], [b_ss6[t]], [b_ss6[t]])

            def p6c(t):
                i2 = t % 2
                K.op(A, lambda: nc.scalar.mul(xn6[i2][:], x1[:, t, :], ss6[:, t, 1:2]), b_x1[t] + [b_ss6[t]], [b_xn6[i2]])

            def p6d(t):
                i2 = t % 2
                for dc in range(8):
                    tr(ps_t6[i2][:, dc * 128:(dc + 1) * 128], xn6[i2][:, dc * 128:(dc + 1) * 128], ident_b[:], [b_xn6[i2], b_const], [b_pt6[i2]])

            def p6e(t):
                i2 = t % 2
                tok = slice(t * 128, (t + 1) * 128)
                tt(V, hT[:, :, tok], ps_t6[i2][:].rearrange("p (c k) -> p c k", k=128), gmlp[:], ALU.mult, [b_pt6[i2], b_g6], [b_hT[t]])

            run_pipeline([p6a, p6b, p6c, p6d, p6e], NT)
        K.barrier()

        with ExitStack() as ph7:
            sb7 = lambda name, shape, dt=F32: ph7.enter_context(nc.sbuf_tensor("s_" + name, shape, dt))
            ps7 = lambda name, shape, dt=F32: ph7.enter_context(nc.psum_tensor("p_" + name, shape, dt))
            NFG = 8
            wup = [wup0, sb7("wup1", [128, 8, 512], BF16)]
            wdn = [wdn0, sb7("wdn1", [128, 4, D], BF16)]
            actT = sb7("actT", [128, 4, S], BF16)
            b_actT = [bufs(4, f"actT{fc}_") for fc in range(4)]
            rl = [sb7(f"rl{i}", [128, 512], BF16) for i in range(2)]
            b_rl = bufs(2, "rl")
            psU = [ps7(f"psU{i}", [128, 512]) for i in range(4)]
            b_pU = bufs(4, "pU")
            psD = [ps7(f"psD{i}", [128, 512]) for i in range(4)]
            b_pD = bufs(4, "pD")

            def load_w(fg):
                sl = fg % 2
                dma(wup[sl][:], w_up_d[:, fg * 512:(fg + 1) * 512].rearrange("(c p) f -> p c f", p=128), [], [b_wup[sl]], f"wup{sl}", q=P)
                dma(wdn[sl][:], w_dn_d[fg * 512:(fg + 1) * 512, :].rearrange("(c p) n -> p c n", p=128), [], [b_wdn[sl]], f"wdn{sl}", q=P)

            iu = 0
            idn = 0
            for fg in range(NFG):
                sl = fg % 2
                if fg + 1 < NFG:
                    load_w(fg + 1)
                for fc in range(4):
                    for tb in range(4):
                        iU, iR = iu % 4, iu % 2
                        iu += 1
                        ts_ = slice(tb * 512, (tb + 1) * 512)
                        for dc in range(8):
                            mm(psU[iU][:], wup[sl][:, dc, fc * 128:(fc + 1) * 128], hT[:, dc, ts_], dc == 0, dc == 7,
                               [b_wup[sl]] + b_hT[tb * 4:(tb + 1) * 4], [b_pU[iU]])
                        act(rl[iR][:], psU[iU][:], AF.Relu, [b_pU[iU]], [b_rl[iR]])
                        tt(P, actT[:, fc, ts_], rl[iR][:], rl[iR][:], ALU.mult, [b_rl[iR]], [b_actT[fc][tb]])
                for t in range(NT):
                    tok = slice(t * 128, (t + 1) * 128)
                    for half in range(2):
                        iD = idn % 4
                        idn += 1
                        cs = slice(half * 512, (half + 1) * 512)
                        for fc in range(4):
                            mm(psD[iD][:], actT[:, fc, tok], wdn[sl][:, fc, cs], fc == 0, fc == 3,
                               [b_actT[fc][t // 4], b_wdn[sl]], [b_pD[iD]])
                        tt(V, x1[:, t, cs], x1[:, t, cs], psD[iD][:], ALU.add, [b_x1[t][half], b_pD[iD]], [b_x1[t][half]])
                    if fg == NFG - 1:
                        dma(out_d[tok, :], x1[:, t, :], b_x1[t], [], "out")

        nw = K.emit(es)
        print("[kernel] psum bufs:", sorted(K.psum_names))
        print(f"[kernel] ops={len(K.ops)} waits={nw} sig={K.sig_counts} dma={K.dma_count}")
    return nc


def _host_params(inp, b):
    f = np.float32
    pcol = np.zeros((128, NCOL), f)
    col = lambda v, n: np.ascontiguousarray(np.asarray(v, f).reshape(n, 128).T)
    pcol[:, C_LNMIX:C_LNMIX + 8] = col(inp["ln_mix_g"][0], 8)
    pcol[:, C_LNMLP:C_LNMLP + 8] = col(inp["ln_mlp_g"][0], 8)
    pcol[:, C_QA:C_QA + 2] = col(inp["q_a_norm_g"][0], 2)
    pcol[:, C_KVA:C_KVA + 1] = col(inp["kv_a_norm_g"][0], 1)
    pcol[:, C_GK] = 1.0
    pcol[0:64, C_GK] = inp["k_norm_g"][0][0:64]
    pcol[:, C_GQ] = 1.0
    pcol[0:64, C_GQ] = inp["q_norm_g"][0][0:64]
    pcol[0:64, C_GATT:C_GATT + 8] = np.asarray(inp["attn_out_norm_g"][0], f).reshape(8, 64).T
    pcol[:, C_GSSM:C_GSSM + 4] = col(inp["ssm_norm_g"][0], 4)
    pcol[:, C_CONVB:C_CONVB + 8] = col(inp["conv_b"][0], 8)
    cw = np.asarray(inp["conv_w"][0][:, 0, :], f)
    for k in range(5):
        pcol[:, C_CONVW + k * 8:C_CONVW + (k + 1) * 8] = col(cw[k], 8)
    row = np.zeros((NBC,), f)
    row[B_DTB:B_DTB + 8] = inp["dt_bias_fwd"][0]
    row[B_DTB + 8:B_DTB + 16] = inp["dt_bias_bwd"][0]
    row[B_ALOG:B_ALOG + 8] = inp["a_log_fwd"][0]
    row[B_ALOG + 8:B_ALOG + 16] = inp["a_log_bwd"][0]
    row[B_DSKIP:B_DSKIP + 8] = inp["d_skip"][0]
    row[B_GQR:B_GQR + 32] = inp["q_norm_g"][0][64:96]
    row[B_GKR:B_GKR + 32] = inp["k_norm_g"][0][64:96]
    inv = (1.0 / (np.float32(10000.0) ** (np.arange(0, 32, 2, dtype=f) / np.float32(32)))).astype(f)
    row[B_INVF:B_INVF + 16] = inv
    pbc = np.ascontiguousarray(np.broadcast_to(row[None, :], (128, NBC)))
    pos = np.ascontiguousarray(np.asarray(inp["positions"][b], np.int32).reshape(NT, 128).T)
    return pcol, pbc, pos


_NC_CACHE = {}


def kernel(**inputs):
    inp = {k: np.asarray(v) for k, v in inputs.items()}
    if "nc" not in _NC_CACHE:
        _NC_CACHE["nc"] = build_nc()
    nc = _NC_CACHE["nc"]
    in_maps = []
    for b in range(8):
        pcol, pbc, pos = _host_params(inp, b)
        in_maps.append({
            "x": np.ascontiguousarray(inp["x"][b], np.float32),
            "pos": pos, "pcol": pcol, "pbc": pbc,
            "w_in": np.ascontiguousarray(inp["w_in"][0], np.float32),
            "w_uq": np.ascontiguousarray(inp["w_uq"][0], np.float32),
            "w_ukv": np.ascontiguousarray(inp["w_ukv"][0], np.float32),
            "w_out": np.ascontiguousarray(inp["w_out"][0], np.float32),
            "w_up": np.ascontiguousarray(inp["w_mlp_up"][0], np.float32),
            "w_dn": np.ascontiguousarray(inp["w_mlp_down"][0], np.float32),
        })
    res = run_bass_kernel_spmd(nc, in_maps, core_ids=list(range(8)))
    return np.stack([np.asarray(r["out"], np.float32) for r in res.results], axis=0)
```

```python
import math
from contextlib import ExitStack

import numpy as np
import concourse.bass as bass
import concourse.mybir as mybir
from concourse.bass_utils import run_bass_kernel_spmd

F32 = mybir.dt.float32
BF16 = mybir.dt.bfloat16
I32 = mybir.dt.int32
AF = mybir.ActivationFunctionType
ALU = mybir.AluOpType
AX = mybir.AxisListType

S = 2048
D = 1024
NT = 16
EPS = 1e-6
IN_W = 1968
C_LNMIX, C_LNMLP, C_QA, C_KVA, C_GK, C_GQ, C_GATT, C_GSSM, C_CONVB, C_CONVW, NCOL = 0, 8, 16, 18, 19, 20, 21, 29, 33, 41, 81
B_DTB, B_ALOG, B_DSKIP, B_GQR, B_GKR, B_INVF, NBC = 0, 16, 32, 40, 72, 104, 120
NEG = -30000.0
MAGIC = 12582912.0
TWO_PI = 2.0 * math.pi
C1 = 6.28125
C2 = TWO_PI - C1


class Buf:
    __slots__ = ("w", "r", "name")

    def __init__(self, name=""):
        self.w = None
        self.r = []
        self.name = name


def bufs(n, name=""):
    return [Buf(f"{name}{i}") for i in range(n)]


def is_psum_buf(b):
    n = b.name
    return n.startswith("p") and n != "par" and not n.startswith("prev")


class Sched:
    def __init__(self, nc):
        self.nc = nc
        self.ops = []
        self.barrier_deps = set()
        self.last = {}
        self.dma_count = {}
        self.psum_names = set()

    def op(self, eng, fn, rd=(), wr=(), dma_key=None):
        idx = len(self.ops)
        deps = set(self.barrier_deps)
        for b in rd:
            if b.w is not None:
                deps.add(b.w)
            if is_psum_buf(b):
                deps.update(r for r in b.r if self.ops[r]["eng"] != eng)
                self.psum_names.add(b.name)
        for b in wr:
            if b.w is not None:
                deps.add(b.w)
            deps.update(b.r)
        deps.discard(idx)
        semval = None
        if dma_key is not None:
            self.dma_count[dma_key] = self.dma_count.get(dma_key, 0) + 16
            semval = self.dma_count[dma_key]
        self.ops.append(dict(eng=eng, fn=fn, deps=deps, dma=dma_key, semval=semval, sig=False, sigidx=0))
        for b in rd:
            b.r.append(idx)
        for b in wr:
            b.w = idx
            b.r = []
        if dma_key is None:
            self.last[eng] = idx
        else:
            self.last[("dma", dma_key)] = idx
        return idx

    def barrier(self):
        self.barrier_deps = set(self.last.values())

    def emit(self, es):
        nc = self.nc
        engs = {"pe": nc.tensor, "act": nc.scalar, "dve": nc.vector, "pool": nc.gpsimd, "sp": nc.sync}
        ops = self.ops
        for o in ops:
            for d in o["deps"]:
                p = ops[d]
                if p["dma"] is None:
                    if p["eng"] == "pe" and o["eng"] == "pe" and o["dma"] is None:
                        continue
                    p["sig"] = True
        cnt = {}
        for o in ops:
            if o["dma"] is None and o["sig"]:
                cnt[o["eng"]] = cnt.get(o["eng"], 0) + 1
                o["sigidx"] = cnt[o["eng"]]
        self.sig_counts = dict(cnt)
        esem = {e: es.enter_context(nc.semaphore("sem_" + e)) for e in ("pe", "act", "dve", "pool")}
        dsem = {k: es.enter_context(nc.semaphore("dsem_" + str(k))) for k in self.dma_count}
        waited = {e: {} for e in engs}
        nwait = 0
        for o in ops:
            E = engs[o["eng"]]
            need = {}
            for d in o["deps"]:
                p = ops[d]
                if p["dma"] is not None:
                    key, sem, val = ("d", p["dma"]), dsem[p["dma"]], p["semval"]
                else:
                    if p["eng"] == "pe" and o["eng"] == "pe" and o["dma"] is None:
                        continue
                    key, sem, val = ("e", p["eng"]), esem[p["eng"]], p["sigidx"]
                if need.get(key, (None, 0))[1] < val:
                    need[key] = (sem, val)
            for key, (sem, val) in need.items():
                if waited[o["eng"]].get(key, 0) >= val:
                    continue
                E.wait_ge(sem, val)
                waited[o["eng"]][key] = val
                nwait += 1
            ins = o["fn"]()
            if o["dma"] is not None:
                ins.then_inc(dsem[o["dma"]], 16)
            elif o["sig"]:
                ins.then_inc(esem[o["eng"]], 1)
        for k, sem in dsem.items():
            if waited["sp"].get(("d", k), 0) < self.dma_count[k]:
                nc.sync.wait_ge(sem, self.dma_count[k])
        return nwait


def build_nc(debug=None):
    nc = bass.Bass("TRN2", target_bir_lowering=False)
    x_d = nc.dram_tensor("x", [S, D], F32, kind="ExternalInput").ap()
    pos_d = nc.dram_tensor("pos", [128, NT], I32, kind="ExternalInput").ap()
    pcol_d = nc.dram_tensor("pcol", [128, NCOL], F32, kind="ExternalInput").ap()
    pbc_d = nc.dram_tensor("pbc", [128, NBC], F32, kind="ExternalInput").ap()
    w_in_d = nc.dram_tensor("w_in", [D, IN_W], F32, kind="ExternalInput").ap()
    w_uq_d = nc.dram_tensor("w_uq", [256, 768], F32, kind="ExternalInput").ap()
    w_ukv_d = nc.dram_tensor("w_ukv", [128, 1024], F32, kind="ExternalInput").ap()
    w_out_d = nc.dram_tensor("w_out", [D, D], F32, kind="ExternalInput").ap()
    w_up_d = nc.dram_tensor("w_up", [D, 4096], F32, kind="ExternalInput").ap()
    w_dn_d = nc.dram_tensor("w_dn", [4096, D], F32, kind="ExternalInput").ap()
    out_d = nc.dram_tensor("out", [S, D], F32, kind="ExternalOutput").ap()
    dbg_d = {}
    if debug:
        for name, shape in debug.items():
            dbg_d[name] = nc.dram_tensor("dbg_" + name, list(shape), F32, kind="ExternalOutput").ap()

    K = Sched(nc)
    V, A, P, T, SP = "dve", "act", "pool", "pe", "sp"

    def mm(out, lhsT, rhs, start, stop, rd, wr):
        K.op(T, lambda: nc.tensor.matmul(out, lhsT=lhsT, rhs=rhs, start=start, stop=stop), rd, wr)

    def tr(out, in_, ident, rd, wr):
        K.op(T, lambda: nc.tensor.transpose(out, in_, ident), rd, wr)

    def act(out, in_, func, rd, wr, bias=None, scale=None, accum=None):
        kw = {}
        if bias is not None:
            kw["bias"] = bias
            if not isinstance(bias, float):
                rd = list(rd) + [b_const]
        if scale is not None:
            kw["scale"] = scale
        if accum is not None:
            kw["accum_out"] = accum
        K.op(A, lambda: nc.scalar.activation(out, in_, func, **kw), rd, wr)

    def ts(eng, out, in0, s1, s2, op0, op1, rd, wr):
        e = nc.vector if eng == V else nc.gpsimd
        if op1 is None:
            K.op(eng, lambda: e.tensor_scalar(out, in0, s1, None, op0), rd, wr)
        else:
            K.op(eng, lambda: e.tensor_scalar(out, in0, s1, s2, op0, op1), rd, wr)

    def tt(eng, out, in0, in1, op, rd, wr):
        e = nc.vector if eng == V else nc.gpsimd
        K.op(eng, lambda: e.tensor_tensor(out, in0, in1, op), rd, wr)

    def stt(out, in0, scalar, in1, op0, op1, rd, wr):
        K.op(V, lambda: nc.vector.scalar_tensor_tensor(out, in0, scalar, in1, op0, op1), rd, wr)

    def cp(eng, out, in_, rd, wr):
        if eng == A:
            K.op(A, lambda: nc.scalar.copy(out, in_), rd, wr)
        else:
            e = nc.vector if eng == V else nc.gpsimd
            K.op(eng, lambda: e.tensor_copy(out, in_), rd, wr)

    def recip(out, in_, rd, wr):
        K.op(V, lambda: nc.vector.reciprocal(out, in_), rd, wr)

    def red(out, in_, rd, wr):
        K.op(V, lambda: nc.vector.tensor_reduce(out, in_, AX.X, ALU.add), rd, wr)

    def memset(eng, ap, val, wr):
        e = nc.vector if eng == V else nc.gpsimd
        K.op(eng, lambda: e.memset(ap, val), (), wr)

    def dma(out, in_, rd, wr, key, q=SP):
        e = nc.sync if q == SP else nc.gpsimd
        K.op(q, lambda: e.dma_start(out=out, in_=in_), rd, wr, dma_key=key)

    def rstd_from_ss(ss, n, rstd, b_ss, b_rstd, tmp):
        act(tmp, ss, AF.Sqrt, [b_ss], [b_rstd], bias=epsb[:ss.shape[0], 0:1], scale=1.0 / n)
        recip(rstd, tmp, [b_rstd], [b_rstd])

    def dump(name, ap, rd):
        if name in dbg_d:
            dma(dbg_d[name], ap, rd, [], "dbg_" + name)

    def dbg_copy(name, ap, rd):
        if name in dbg_d:
            tns = dbgbuf[name]
            b_ = Buf("dbgc" + name)
            cp(V, tns[:], ap, rd, [b_])
            dma(dbg_d[name], tns[:], [b_], [], "dbg_" + name)

    with ExitStack() as es:
        sb = lambda name, shape, dt=F32: es.enter_context(nc.sbuf_tensor("s_" + name, shape, dt))
        pcol = sb("pcol", [128, NCOL])
        pbc = sb("pbc", [128, NBC])
        posi = sb("posi", [128, NT], I32)
        ident_f = sb("ident_f", [128, 128])
        ident_b = sb("ident_b", [128, 128], BF16)
        ones_b = sb("ones_b", [128, 128], BF16)
        ones_f = sb("ones_f", [128, 128])
        epsb = sb("epsb", [128, 1])
        fence_t = sb("fence_t", [128, 1])
        b_const = Buf("const")
        b_par = Buf("par")
        dma(pcol[:], pcol_d, [], [b_par], "par")
        dma(pbc[:], pbc_d, [], [b_par], "par")
        dma(posi[:], pos_d, [], [b_par], "par")
        memset(P, ident_f[:], 1.0, [b_const])
        K.op(P, lambda: nc.gpsimd.affine_select(out=ident_f[:], in_=ident_f[:], pattern=[[1, 128]], compare_op=ALU.is_equal,
                                                fill=0.0, base=0, channel_multiplier=-1), [b_const], [b_const])
        cp(P, ident_b[:], ident_f[:], [b_const], [b_const])
        memset(P, ones_b[:], 1.0, [b_const])
        memset(P, ones_f[:], 1.0, [b_const])
        memset(P, epsb[:], EPS, [b_const])

        dbgbuf = {n_: sb("dbgc_" + n_, list(ap_.shape)) for n_, ap_ in dbg_d.items() if n_ in ("ssx", "xn0", "hT2")}
        mixA = sb("mixA", [128, 4, S], BF16)
        b_mixA = bufs(4, "mixA")
        hT = sb("hT", [128, 8, S], BF16)
        b_hT = bufs(NT, "hT")

        def run_pipeline(stages, n, skew=1, order=None):
            ns = len(stages)
            order = order if order is not None else list(range(ns - 1, -1, -1))
            for step in range(n + (ns - 1) * skew):
                for si in order:
                    stg = stages[si]
                    t_ = step - si * skew
                    if 0 <= t_ < n:
                        stg(t_)

        with ExitStack() as ph:
            sbp = lambda name, shape, dt=F32: ph.enter_context(nc.sbuf_tensor("s_" + name, shape, dt))
            psp = lambda name, shape, dt=F32: ph.enter_context(nc.psum_tensor("p_" + name, shape, dt))
            QT = sbp("QT", [96, 8, S], BF16)
            KT = sbp("KT", [96, 8, S], BF16)
            Vt = sbp("Vt", [128, NT, 8, 65], BF16)
            b_QT = bufs(NT, "QT")
            b_KT = bufs(NT, "KT")
            b_V = bufs(NT, "V")
            sincos = sbp("sincos", [128, 2, NT, 16])
            qsc = sbp("qsc", [128, 2, NT, 2, 16])
            b_rope = Buf("rope")
            with ExitStack() as phr:
                sbr = lambda name, shape, dt=F32: phr.enter_context(nc.sbuf_tensor("s_" + name, shape, dt))
                posf = sbr("posf", [128, NT])
                ang = sbr("ang", [128, 2, NT, 16])
                kk = sbr("kk", [128, 2, NT, 16])
                r1 = sbr("r1", [128, 2, NT, 16])
                cp(V, posf[:], posi[:], [b_par], [b_rope])
                tt(V, ang[:, 0], pbc[:, B_INVF:B_INVF + 16].unsqueeze(1).to_broadcast([128, NT, 16]),
                   posf[:].unsqueeze(2).to_broadcast([128, NT, 16]), ALU.mult, [b_par, b_rope], [b_rope])
                ts(V, ang[:, 1], ang[:, 0], math.pi / 2, None, ALU.add, None, [b_rope], [b_rope])
                ts(V, kk[:], ang[:], 1.0 / TWO_PI, MAGIC, ALU.mult, ALU.add, [b_rope], [b_rope])
                ts(V, kk[:], kk[:], MAGIC, None, ALU.subtract, None, [b_rope], [b_rope])
                stt(r1[:], kk[:], -C1, ang[:], ALU.mult, ALU.add, [b_rope], [b_rope])
                stt(r1[:], kk[:], -C2, r1[:], ALU.mult, ALU.add, [b_rope], [b_rope])
                ts(V, r1[:], r1[:], math.pi, -math.pi, ALU.min, ALU.max, [b_rope], [b_rope])
                act(sincos[:], r1[:], AF.Sin, [b_rope], [b_rope])
                for a_ in range(2):
                    for j_ in range(2):
                        tt(V, qsc[:, a_, :, j_, :], sincos[:, a_, :, :],
                           pbc[:, B_GQR + 16 * j_:B_GQR + 16 * (j_ + 1)].unsqueeze(1).to_broadcast([128, NT, 16]), ALU.mult,
                           [b_rope, b_par], [b_rope])
            K.barrier()
            sin_t = lambda t: sincos[:, 0, t, :]
            cos_t = lambda t: sincos[:, 1, t, :]
            with ExitStack() as ph2:
                sb2 = lambda name, shape, dt=F32: ph2.enter_context(nc.sbuf_tensor("s_" + name, shape, dt))
                ps2 = lambda name, shape, dt=F32: ph2.enter_context(nc.psum_tensor("p_" + name, shape, dt))
                wA = sb2("wA", [128, 8, 416], BF16)
                b_wA = bufs(8, "wA")
                for dc in range(8):
                    dma(wA[:, dc, :], w_in_d[dc * 128:(dc + 1) * 128, 0:416], [], [b_wA[dc]], f"wA{dc}", q=P)
                wuq = sb2("wuq", [128, 2, 768], BF16)
                wukv = sb2("wukv", [128, 1024], BF16)
                b_wq = Buf("wq")
                dma(wuq[:], w_uq_d.rearrange("(c p) n -> p c n", p=128), [], [b_wq], "wq", q=P)
                dma(wukv[:], w_ukv_d, [], [b_wq], "wq", q=P)
                xt = [sb2(f"xt{i}", [128, D]) for i in range(3)]
                b_xt = bufs(3, "xt")
                xn = [sb2(f"xn{i}", [128, D], BF16) for i in range(2)]
                b_xn = bufs(2, "xn")
                sq_scr = sb2("sq_scr", [128, D], BF16)
                b_sqs = Buf("sqscr")
                ssx = sb2("ssx", [128, NT, 2])
                b_ss = bufs(NT, "ss")
                gmix = sb2("gmix", [128, 8, 128])
                b_g = Buf("gmix")
                ps_trh = ps2("ps_trh", [128, D], BF16)
                b_ptrh = Buf("ptrh")
                cp(V, gmix[:], pcol[:, C_LNMIX:C_LNMIX + 8].unsqueeze(2).to_broadcast([128, 8, 128]), [b_par], [b_g])
                cqT = sb2("cqT", [128, 2, 512], BF16)
                ckvT = sb2("ckvT", [128, 512], BF16)
                b_cqT = bufs(4, "cqT")
                b_ckvT = bufs(4, "ckvT")
                psA = ps2("psA", [128, 512])
                b_psA = Buf("psA")
                ps_t = ps2("ps_t", [128, 1024], BF16)
                b_pst = Buf("pst")
                psQ = ps2("psQ", [128, 1024])
                b_psQ = Buf("psQ")
                psKV = ps2("psKV", [128, 1024])
                b_psKV = Buf("psKV")
                ps_t2 = ps2("ps_t2", [128, 1024], BF16)
                b_pst2 = Buf("pst2")
                cn = [sb2(f"cn{i}", [128, 384], BF16) for i in range(2)]
                b_cn = bufs(2, "cn")
                st = sb2("st", [128, NT, 8])
                b_st = bufs(NT, "st")
                scr = sb2("scr2", [128, 512], BF16)
                b_scr = Buf("scr2")
                scrb = sb2("scrb", [128, 416], BF16)
                b_scrb = Buf("scrb")
                kpe = sb2("kpe", [128, NT, 32])
                b_kpe = bufs(NT, "kpe")
                kpr = sb2("kpr", [128, NT, 32])
                scrq = sb2("scrq", [128, 768], BF16)
                b_scrq = Buf("scrq")
                ktmp = sb2("ktmp", [128, 4, 16])
                b_kr = Buf("kr")
                ssh = sb2("ssh", [128, 4, 8])
                b_sshk = Buf("sshk")
                b_sshq = Buf("sshq")
                qn = sb2("qn", [128, 8, 96])
                qr = sb2("qr", [128, 8, 32])
                qtmp = sb2("qtmp", [128, 4, 8, 16])
                b_qn = Buf("qn")
                b_qr = Buf("qr")
                Qtm = [sb2(f"Qtm{i}", [128, 8, 96], BF16) for i in range(2)]
                Ktm = [sb2(f"Ktm{i}", [128, 8, 96], BF16) for i in range(2)]
                b_Qtm = bufs(2, "Qtm")
                b_Ktm = bufs(2, "Ktm")
                for t in range(NT):
                    memset(P, Vt[:, t, :, 64:65], 1.0, [b_V[t]])

                def tk(t):
                    return slice(t * 128, (t + 1) * 128)

                def s0(t):
                    i3 = t % 3
                    dma(xt[i3][:], x_d[tk(t), :], [], [b_xt[i3]], f"x{i3}")
                    act(sq_scr[:], xt[i3][:], AF.Square, [b_xt[i3]], [b_sqs, b_ss[t]], accum=ssx[:, t, 0:1])
                    act(ssx[:, t, 1:2], ssx[:, t, 0:1], AF.Sqrt, [b_ss[t]], [b_ss[t]], bias=epsb[:, 0:1], scale=1.0 / D)
                    recip(ssx[:, t, 1:2], ssx[:, t, 1:2], [b_ss[t]], [b_ss[t]])

                def s1(t):
                    i3, i2 = t % 3, t % 2
                    K.op(A, lambda: nc.scalar.mul(xn[i2][:], xt[i3][:], ssx[:, t, 1:2]), [b_xt[i3], b_ss[t]], [b_xn[i2]])

                def s2(t):
                    i2 = t % 2
                    for dc in range(8):
                        tr(ps_trh[:, dc * 128:(dc + 1) * 128], xn[i2][:, dc * 128:(dc + 1) * 128], ident_b[:], [b_xn[i2], b_const], [b_ptrh])

                def s3(t):
                    tt(V, hT[:, :, tk(t)], ps_trh[:].rearrange("p (c k) -> p c k", k=128), gmix[:], ALU.mult, [b_ptrh, b_g], [b_hT[t]])

                def s4(t):
                    for dc in range(8):
                        mm(psA[:, 0:416], hT[:, dc, tk(t)], wA[:, dc, :], dc == 0, dc == 7, [b_hT[t], b_wA[dc]], [b_psA])

                def s5(t):
                    i2 = t % 2
                    S_ = st[:, t, :]
                    act(scrb[:, 0:256], psA[:, 0:256], AF.Square, [b_psA], [b_scrb, b_st[t]], accum=S_[:, 0:1])
                    act(scrb[:, 256:384], psA[:, 256:384], AF.Square, [b_psA], [b_scrb, b_st[t]], accum=S_[:, 1:2])
                    act(scrb[:, 384:416], psA[:, 384:416], AF.Square, [b_psA], [b_scrb, b_st[t]], accum=S_[:, 2:3])
                    act(S_[:, 3:4], S_[:, 0:1], AF.Sqrt, [b_st[t]], [b_st[t]], bias=epsb[:, 0:1], scale=1.0 / 256)
                    act(S_[:, 4:5], S_[:, 1:2], AF.Sqrt, [b_st[t]], [b_st[t]], bias=epsb[:, 0:1], scale=1.0 / 128)
                    recip(S_[:, 5:7], S_[:, 3:5], [b_st[t]], [b_st[t]])
                    ts(V, cn[i2][:, 0:256], psA[:, 0:256], S_[:, 5:6], None, ALU.mult, None, [b_psA, b_st[t]], [b_cn[i2]])
                    ts(V, cn[i2][:, 256:384], psA[:, 256:384], S_[:, 6:7], None, ALU.mult, None, [b_psA, b_st[t]], [b_cn[i2]])
                    tt(V, kpe[:, t, :], psA[:, 384:416], pbc[:, B_GKR:B_GKR + 32], ALU.mult, [b_psA, b_par], [b_kpe[t]])

                def s5k(t):
                    kp = kpe[:, t, :]
                    cs2 = sincos[:, :, t, :]
                    tt(P, ktmp[:, 0:2, :], kp[:, 0:16].unsqueeze(1).to_broadcast([128, 2, 16]), cs2, ALU.mult, [b_kpe[t], b_rope], [b_kr])
                    tt(P, ktmp[:, 2:4, :], kp[:, 16:32].unsqueeze(1).to_broadcast([128, 2, 16]), cs2, ALU.mult, [b_kpe[t], b_rope], [b_kr])
                    tt(P, kpr[:, t, 0:16], ktmp[:, 1, :], ktmp[:, 2, :], ALU.subtract, [b_kr], [b_kpe[t]])
                    tt(P, kpr[:, t, 16:32], ktmp[:, 3, :], ktmp[:, 0, :], ALU.add, [b_kr], [b_kpe[t]])

                def s6(t):
                    i2 = t % 2
                    for c in range(3):
                        tr(ps_t[:, c * 128:(c + 1) * 128], cn[i2][:, c * 128:(c + 1) * 128], ident_b[:], [b_cn[i2], b_const], [b_pst])

                def s7(t):
                    r4 = t % 4
                    for c in range(2):
                        K.op(A, lambda c=c: nc.scalar.mul(cqT[:, c, tk(r4)], ps_t[:, c * 128:(c + 1) * 128], pcol[:, C_QA + c:C_QA + c + 1]),
                             [b_pst, b_par], [b_cqT[r4]])
                    K.op(A, lambda: nc.scalar.mul(ckvT[:, tk(r4)], ps_t[:, 256:384], pcol[:, C_KVA:C_KVA + 1]), [b_pst, b_par], [b_ckvT[r4]])

                def s8(t):
                    r4 = t % 4
                    for (lo, hi) in ((0, 512), (512, 768)):
                        for c in range(2):
                            mm(psQ[:, lo:hi], cqT[:, c, tk(r4)], wuq[:, c, lo:hi], c == 0, c == 1, [b_cqT[r4], b_wq], [b_psQ])
                    for (lo, hi) in ((0, 512), (512, 1024)):
                        mm(psKV[:, lo:hi], ckvT[:, tk(r4)], wukv[:, lo:hi], True, True, [b_ckvT[r4], b_wq], [b_psKV])

                def s9(t):
                    i2 = t % 2
                    S_ = st[:, t, :]
                    q3 = psQ[:, 0:768].rearrange("p (h d) -> p h d", d=96)
                    kv3 = psKV[:].rearrange("p (h d) -> p h d", d=128)
                    sk = scr[:, 0:512].rearrange("p (h d) -> p h d", d=64)
                    sq_ = scrq[:, 0:768].rearrange("p (h d) -> p h d", d=96)
                    act(sk, kv3[:, :, 0:64], AF.Square, [b_psKV], [b_scr])
                    act(sq_, q3, AF.Square, [b_psQ], [b_scrq])
                    cp(A, Vt[:, t, :, 0:64], kv3[:, :, 64:128], [b_psKV], [b_V[t]])
                    red(ssh[:, 0, :], sk, [b_scr], [b_sshk])
                    red(ssh[:, 2, :], sq_, [b_scrq], [b_sshq])
                    ts(V, ssh[:, 0, :], ssh[:, 0, :], S_[:, 2:3], None, ALU.add, None, [b_sshk, b_st[t]], [b_sshk])

                def s9b(t):
                    i2 = t % 2
                    q3 = psQ[:, 0:768].rearrange("p (h d) -> p h d", d=96)
                    kv3 = psKV[:].rearrange("p (h d) -> p h d", d=128)
                    act(ssh[:, 1, :], ssh[:, 0, :], AF.Sqrt, [b_sshk], [b_sshk], bias=epsb[:, 0:1], scale=1.0 / 96)
                    act(ssh[:, 3, :], ssh[:, 2, :], AF.Sqrt, [b_sshq], [b_sshq], bias=epsb[:, 0:1], scale=1.0 / 96)
                    recip(ssh[:, 1, :], ssh[:, 1, :], [b_sshk], [b_sshk])
                    recip(ssh[:, 3, :], ssh[:, 3, :], [b_sshq], [b_sshq])
                    rk = ssh[:, 1, :]
                    rq = ssh[:, 3, :]
                    tt(V, qn[:], q3, rq.unsqueeze(2).to_broadcast([128, 8, 96]), ALU.mult, [b_psQ, b_sshq], [b_qn])
                    tt(V, Ktm[i2][:, :, 0:64], kv3[:, :, 0:64], rk.unsqueeze(2).to_broadcast([128, 8, 64]), ALU.mult,
                       [b_psKV, b_sshk], [b_Ktm[i2]])
                    cp(V, Qtm[i2][:, :, 0:64], qn[:, :, 0:64], [b_qn], [b_Qtm[i2]])
                    tt(P, Ktm[i2][:, :, 64:96], kpr[:, t, :].unsqueeze(1).to_broadcast([128, 8, 32]),
                       rk.unsqueeze(2).to_broadcast([128, 8, 32]), ALU.mult, [b_kpe[t], b_sshk], [b_Ktm[i2]])
                    t1q = qsc[:, :, t, 0, :].unsqueeze(2).to_broadcast([128, 2, 8, 16])
                    t2q = qsc[:, :, t, 1, :].unsqueeze(2).to_broadcast([128, 2, 8, 16])
                    tt(P, qtmp[:, 0:2], qn[:, :, 64:80].unsqueeze(1).to_broadcast([128, 2, 8, 16]), t1q, ALU.mult, [b_qn, b_rope], [b_qr])
                    tt(P, qtmp[:, 2:4], qn[:, :, 80:96].unsqueeze(1).to_broadcast([128, 2, 8, 16]), t2q, ALU.mult, [b_qn, b_rope], [b_qr])
                    tt(P, Qtm[i2][:, :, 64:80], qtmp[:, 1], qtmp[:, 2], ALU.subtract, [b_qr], [b_Qtm[i2]])
                    tt(P, Qtm[i2][:, :, 80:96], qtmp[:, 3], qtmp[:, 0], ALU.add, [b_qr], [b_Qtm[i2]])

                def s10k(t):
                    i2 = t % 2
                    for h in range(8):
                        tr(ps_t2[0:96, h * 128:(h + 1) * 128], Ktm[i2][:, h, :], ident_b[:], [b_Ktm[i2], b_const], [b_pst2])

                def s10q(t):
                    i2 = t % 2
                    for h in range(8):
                        tr(ps_t[0:96, h * 128:(h + 1) * 128], Qtm[i2][:, h, :], ident_b[:], [b_Qtm[i2], b_const], [b_pst])

                def s11k(t):
                    K.op(A, lambda: nc.scalar.mul(KT[:, :, tk(t)], ps_t2[0:96, :].rearrange("p (h k) -> p h k", k=128), pcol[0:96, C_GK:C_GK + 1]),
                         [b_pst2, b_par], [b_KT[t]])

                def s11q(t):
                    ts(V, QT[:, :, tk(t)], ps_t[0:96, :].rearrange("p (h k) -> p h k", k=128), pcol[0:96, C_GQ:C_GQ + 1], None,
                       ALU.mult, None, [b_pst, b_par], [b_QT[t]])

                slots = [(5, s5), (3, s3), (9, lambda t: (s10k(t), s11k(t), s10q(t), s11q(t))), (8, s9), (1, s1), (8, s9b), (5, s5k),
                         (6, lambda t: (s6(t), s7(t))), (4, s4), (2, s2), (0, s0), (7, s8)]
                for step in range(NT + 9):
                    for si, fn_ in slots:
                        t_ = step - si
                        if 0 <= t_ < NT:
                            fn_(t_)
                dbg_copy("ssx", ssx[:], b_ss)
                dbg_copy("xn0", xn[0][:, 0:256], b_xn)
                dbg_copy("hT2", hT[:, 0, 0:256], b_hT)
            K.barrier()
            if "QT" in dbg_d:
                if "hT" in dbg_d:
                    with ExitStack() as phd:
                        htf = phd.enter_context(nc.sbuf_tensor("s_htf", [128, 8, 256], F32))
                        b_dh = Buf("dbgh")
                        cp(V, htf[:], hT[:, :, 0:256], b_hT, [b_dh])
                        dump("hT", htf[:], [b_dh])
                    K.barrier()
                qtf = sbp("qtf", [96, 8, 256])
                ktf = sbp("ktf", [96, 8, 256])
                vtf = sbp("vtf", [128, 2, 8, 65])
                b_d = Buf("dbg")
                cp(V, qtf[:], QT[:, :, 0:256], b_QT, [b_d])
                cp(V, ktf[:], KT[:, :, 0:256], b_KT, [b_d])
                cp(V, vtf[:], Vt[:, 0:2], b_V, [b_d])
                dump("QT", qtf[:], [b_d])
                dump("KT", ktf[:], [b_d])
                dump("Vt", vtf[:], [b_d])

            with ExitStack() as ph3:
                sb3 = lambda name, shape, dt=F32: ph3.enter_context(nc.sbuf_tensor("s_" + name, shape, dt))
                ps3 = lambda name, shape, dt=F32: ph3.enter_context(nc.psum_tensor("p_" + name, shape, dt))
                NPL = 2
                ps_l = [ps3(f"ps_l{i}", [128, 1024]) for i in range(NPL)]
                b_psl = bufs(NPL, "psl")
                ps_o = [ps3(f"ps_o{i}", [65, 512]) for i in range(2)]
                b_pso = bufs(2, "pso")
                ps_d = ps3("ps_d", [64, 512])
                b_psd = Buf("psd")
                ps_s = ps3("ps_s", [64, 512])
                b_pss = Buf("pss")
                NPT = 3
                PT = [sb3(f"PT{i}", [128, 1024], BF16) for i in range(NPT)]
                b_PT = bufs(NPT, "PT")
                osb = [sb3(f"osb{i}", [65, 512]) for i in range(2)]
                b_osb = bufs(2, "osb")
                rden = sb3("rden", [64, 512])
                b_rden = Buf("rden")
                att2 = [sb3(f"att{i}", [64, 8, 512]) for i in range(2)]
                b_att2 = [bufs(8, f"att{i}_") for i in range(2)]
                sqa2 = [sb3(f"sqa{i}", [64, 8, 512], BF16) for i in range(2)]
                b_sqa2 = [bufs(8, f"sqa{i}_") for i in range(2)]
                rstd_a = sb3("rstd_a", [64, 512])
                b_rsa = Buf("rsa")
                modd = sb3("modd", [64, 4, 512], BF16)
                b_modd = Buf("modd")
                dh = sb3("dh", [128, 2, 512], BF16)
                sel_b = sb3("sel_b", [128, 64], BF16)
                dtmp = sb3("dtmp", [65, 512])
                b_dh = Buf("dh")
                sel65 = sb3("sel65", [65, 64])
                b_sel = Buf("sel")
                memset(P, sel65[:], 0.0, [b_sel])
                memset(P, sel65[64:65, :], 1.0, [b_sel])
                memset(P, sel_b[:], 0.0, [b_sel])
                memset(P, sel_b[64:65, :], 1.0, [b_sel])
                memset(P, dh[:], 0.0, [b_dh])
                scale = 96 ** -0.5
                iters = [(qc, h, kp) for qc in range(4) for h in range(8) for kp in range(NT // 2)]
                N_IT = len(iters)
                deferred = {}

                def defer(step, fn):
                    deferred.setdefault(step, []).append(fn)

                def qsl(qc):
                    return slice(qc * 512, (qc + 1) * 512)

                def st_L(i):
                    qc, h, kp = iters[i]
                    il = i % NPL
                    for j in range(2):
                        kt = 2 * kp + j
                        mm(ps_l[il][:, j * 512:(j + 1) * 512], KT[:, h, kt * 128:(kt + 1) * 128], QT[:, h, qsl(qc)], True, True,
                           [b_KT[kt]] + b_QT[qc * 4:(qc + 1) * 4], [b_psl[il]])

                def st_E(i):
                    il, ip = i % NPL, i % NPT
                    act(PT[ip][:], ps_l[il][:], AF.Exp, [b_psl[il]], [b_PT[ip]], scale=scale)

                def st_PV(i):
                    qc, h, kp = iters[i]
                    hh = i // (NT // 2)
                    ip = i % NPT
                    for j in range(2):
                        kt = 2 * kp + j
                        mm(ps_o[hh % 2][:], Vt[:, kt, h, :], PT[ip][:, j * 512:(j + 1) * 512], kt == 0, kt == NT - 1,
                           [b_V[kt], b_PT[ip]], [b_pso[hh % 2]])

                def tailA(hh):
                    cp(V, osb[hh % 2][:], ps_o[hh % 2][:], [b_pso[hh % 2]], [b_osb[hh % 2]])

                def tailA2(hh):
                    o64 = osb[hh % 2][64:65, :]
                    cp(V, dh[64:65, 0, :], o64, [b_osb[hh % 2]], [b_dh])
                    cp(V, dtmp[64:65, :], dh[64:65, 0, :], [b_dh], [b_dh])
                    tt(V, dh[64:65, 1, :], o64, dtmp[64:65, :], ALU.subtract, [b_osb[hh % 2], b_dh], [b_dh])

                def tailB(hh):
                    mm(ps_d[:], sel_b[:], dh[:, 0, :], True, False, [b_sel, b_dh], [b_psd])
                    mm(ps_d[:], sel_b[:], dh[:, 1, :], False, True, [b_sel, b_dh], [b_psd])

                def tailC(hh):
                    h = hh % 8
                    att, b_att, sqa, b_sqa = att2[(hh // 8) % 2], b_att2[(hh // 8) % 2], sqa2[(hh // 8) % 2], b_sqa2[(hh // 8) % 2]
                    recip(rden[:], ps_d[:], [b_psd], [b_rden])
                    tt(V, att[:, h, :], osb[hh % 2][0:64, :], rden[:], ALU.mult, [b_osb[hh % 2], b_rden], [b_att[h]])
                    tt(P, sqa[:, h, :], att[:, h, :], att[:, h, :], ALU.mult, [b_att[h]], [b_sqa[h]])

                def normA(qc):
                    sqa, b_sqa = sqa2[qc % 2], b_sqa2[qc % 2]
                    for h in range(8):
                        mm(ps_s[:], ones_b[0:64, 0:64], sqa[:, h, :], h == 0, h == 7, [b_const, b_sqa[h]], [b_pss])

                def normB(qc):
                    act(rstd_a[:], ps_s[:], AF.Ln, [b_pss], [b_rsa], bias=epsb[0:64, 0:1], scale=1.0 / 512)
                    act(rstd_a[:], rstd_a[:], AF.Exp, [b_rsa], [b_rsa], scale=-0.5)

                def normC(qc, hp):
                    att, b_att = att2[qc % 2], b_att2[qc % 2]
                    h = 2 * hp
                    stt(mixA[0:64, hp, qsl(qc)], att[:, h, :], pcol[0:64, C_GATT + h:C_GATT + h + 1], rstd_a[:], ALU.mult, ALU.mult,
                        [b_att[h], b_par, b_rsa], [b_mixA[qc]])
                    h = 2 * hp + 1
                    stt(modd[:, hp, :], att[:, h, :], pcol[0:64, C_GATT + h:C_GATT + h + 1], rstd_a[:], ALU.mult, ALU.mult,
                        [b_att[h], b_par, b_rsa], [b_modd])
                    if hp == 3:
                        dma(mixA[64:128, :, qsl(qc)], modd[:], [b_modd], [b_mixA[qc]], "modd")

                for step in range(N_IT + 36):
                    if step < N_IT:
                        st_L(step)
                    if 0 <= step - 1 < N_IT:
                        st_E(step - 1)
                    i = step - 2
                    if 0 <= i < N_IT:
                        st_PV(i)
                        qc, h, kp = iters[i]
                        if kp == NT // 2 - 1:
                            hh = i // (NT // 2)
                            defer(step + 1, lambda hh=hh: tailA(hh))
                            defer(step + 1, lambda hh=hh: tailA2(hh))
                            defer(step + 7, lambda hh=hh: tailB(hh))
                            defer(step + 8, lambda hh=hh: tailC(hh))
                            if h == 7:
                                defer(step + 18, lambda qc=qc: normA(qc))
                                defer(step + 22, lambda qc=qc: normB(qc))
                                for hp in range(4):
                                    defer(step + 23 + 2 * hp, lambda qc=qc, hp=hp: normC(qc, hp))
                    for fn in deferred.pop(step, []):
                        fn()
                assert not deferred
        if "mixA" in dbg_d:
            K.barrier()
            with ExitStack() as phd:
                mf = phd.enter_context(nc.sbuf_tensor("mixAf", [128, 4, 512], F32))
                b_d = Buf("dbg2")
                cp(V, mf[:], mixA[:, :, 0:512], b_mixA, [b_d])
                dump("mixA", mf[:], [b_d])
            K.barrier()

        mixS = sb("mixS", [128, 4, S], BF16)
        b_mixS = bufs(NT, "mixS")
        wo = sb("wo", [128, 8, D], BF16)
        b_wo = bufs(8, "wo")

        with ExitStack() as phS:
            sbS = lambda name, shape, dt=F32: phS.enter_context(nc.sbuf_tensor("s_" + name, shape, dt))
            zact = sbS("zact", [128, NT, 512], BF16)
            xbcT = sbS("xbcT", [128, 8, S], BF16)
            xs_tm = sbS("xs_tm", [128, NT, 512], BF16)
            B_tm = sbS("B_tm", [128, NT, 256], BF16)
            dtsb = sbS("dtsb", [128, 2, NT, 8])
            b_zact = bufs(NT, "zact")
            b_xbcT = [bufs(4, f"xbcT{cc}_") for cc in range(8)]
            b_xs = [bufs(2, f"xs{cc}_") for cc in range(4)]
            b_Btm = [bufs(2, f"Btm{g}_") for g in range(2)]
            b_dt = bufs(NT, "dt")
            Tm = sbS("Tm", [128, 2, 128])
            nmask = sbS("nmask", [128, 2, 128], BF16)
            dskipI = sbS("dskipI", [128, 8, 128], BF16)
            gssm = sbS("gssm", [128, 4, 128])
            a_bc = sbS("a_bc", [128, 16])
            da = sbS("da", [128, 2, NT, 8])
            acs = sbS("acs", [128, 2, NT, 8])
            negu = sbS("negu", [128, 2, NT, 8])
            eacs = sbS("eacs", [128, 2, NT, 8])
            wsb = sbS("wsb", [128, 2, NT, 8])
            cdec = sbS("cdec", [128, 2, NT, 8])
            tmp5 = sbS("tmp5", [128, 256])
            fl = lambda a: a[:].rearrange("p d c h -> p (d c h)")
            wBz_v = xbcT[:, 0:3, :].rearrange("p a b -> p (a b)")[:, 0:8 * 528].rearrange("p (c n) -> p c n", n=528)
            b_fence = Buf("fence")
            b_wBz = bufs(8, "wBz")
            b_wBd = bufs(8, "wBd")
            memset(P, fence_t[:], 0.0, b_KT + b_QT + b_V + [b_fence])
            for dc in range(8):
                dma(wBz_v[:, dc, 0:512], w_in_d[dc * 128:(dc + 1) * 128, 416:928], [b_fence], [b_wBz[dc]], f"wBz{dc}", q=P)
            dma(wBz_v[:, :, 512:528], w_in_d[:, 1952:1968].rearrange("(c p) n -> p c n", p=128), [b_fence], b_wBd, "wBd", q=P)
            K.barrier()
            b_set = Buf("ssdset")
            b_gssm = Buf("gssm")
            b_abc = Buf("abc")

            def ssd_consts():
                memset(P, Tm[:], 1.0, [b_set])
                K.op(P, lambda: nc.gpsimd.affine_select(out=Tm[:, 0, :], in_=Tm[:, 0, :], pattern=[[1, 128]], compare_op=ALU.is_ge,
                                                        fill=0.0, base=0, channel_multiplier=-1), [b_set], [b_set])
                K.op(P, lambda: nc.gpsimd.affine_select(out=Tm[:, 1, :], in_=Tm[:, 1, :], pattern=[[-1, 128]], compare_op=ALU.is_ge,
                                                        fill=0.0, base=0, channel_multiplier=1), [b_set], [b_set])
                memset(P, nmask[:], 0.0, [b_set])
                K.op(P, lambda: nc.gpsimd.affine_select(out=nmask[:, 0, :], in_=nmask[:, 0, :], pattern=[[1, 128]], compare_op=ALU.is_ge,
                                                        fill=NEG, base=0, channel_multiplier=-1), [b_set], [b_set])
                K.op(P, lambda: nc.gpsimd.affine_select(out=nmask[:, 1, :], in_=nmask[:, 1, :], pattern=[[-1, 128]], compare_op=ALU.is_ge,
                                                        fill=NEG, base=0, channel_multiplier=1), [b_set], [b_set])
                for h in range(8):
                    ts(P, dskipI[:, h, :], ident_b[:], pbc[:, B_DSKIP + h:B_DSKIP + h + 1], None, ALU.mult, None, [b_const, b_par], [b_set])
                cp(V, gssm[:], pcol[:, C_GSSM:C_GSSM + 4].unsqueeze(2).to_broadcast([128, 4, 128]), [b_par], [b_gssm])
                act(a_bc[:], pbc[:, B_ALOG:B_ALOG + 16], AF.Exp, [b_par], [b_abc])
                ts(V, a_bc[:], a_bc[:], -1.0, None, ALU.mult, None, [b_abc], [b_abc])


            with ExitStack() as ph4:
                sb4 = lambda name, shape, dt=F32: ph4.enter_context(nc.sbuf_tensor("s_" + name, shape, dt))
                ps4 = lambda name, shape, dt=F32: ph4.enter_context(nc.psum_tensor("p_" + name, shape, dt))
                wB = sb4("wB", [128, 8, 1024], BF16)
                b_wB = bufs(8, "wB")
                for dc in range(8):
                    dma(wB[:, dc, :], w_in_d[dc * 128:(dc + 1) * 128, 928:1952], b_wBz, [b_wB[dc]], f"wB{dc}", q=P)
                ssd_consts()
                diag = sb4("diag", [128, 5, 8, 128], BF16)
                b_diag = Buf("diag")
                tt(V, diag[:].rearrange("p k c n -> p (k c) n"), ident_b[:].unsqueeze(1).to_broadcast([128, 40, 128]),
                   pcol[:, C_CONVW:C_CONVW + 40].unsqueeze(2).to_broadcast([128, 40, 128]), ALU.mult, [b_const, b_par], [b_diag])
                raw = [sb4(f"raw{i}", [128, S + 4], BF16) for i in range(2)]
                b_raw = bufs(2, "raw")
                for i in range(2):
                    memset(P, raw[i][:, 0:2], 0.0, [b_raw[i]])
                    memset(P, raw[i][:, S + 2:S + 4], 0.0, [b_raw[i]])
                psZ = [ps4(f"psZ{i}", [128, 512]) for i in range(2)]
                b_psZ = bufs(2, "psZ")
                psDT = ps4("psDT", [128, 512])
                b_psDT = Buf("psDT")
                psX = [ps4(f"psX{i}", [128, 512]) for i in range(2)]
                b_psX = bufs(2, "psX")
                psC = [ps4(f"psC{i}", [128, 512]) for i in range(2)]
                b_psC = bufs(2, "psC")
                ps_tr4 = ps4("ps_tr4", [128, 1024], BF16)
                b_ptr4 = Buf("ptr4")
                dtx = sb4("dtx", [128, NT, 16])
                b_dtx = bufs(NT, "dtx")
                def p4a(t):
                    tok = slice(t * 128, (t + 1) * 128)
                    i2 = t % 2
                    for dc in range(8):
                        mm(psZ[i2][:], hT[:, dc, tok], wBz_v[:, dc, 0:512], dc == 0, dc == 7, [b_hT[t], b_wBz[dc]], [b_psZ[i2]])
                    for dc in range(8):
                        mm(psDT[:, 0:16], hT[:, dc, tok], wBz_v[:, dc, 512:528], dc == 0, dc == 7, [b_hT[t], b_wBd[dc]], [b_psDT])

                def p4b(t):
                    i2 = t % 2
                    act(zact[:, t, :], psZ[i2][:], AF.Silu, [b_psZ[i2]], [b_zact[t]])
                    tt(V, dtx[:, t, :], psDT[:, 0:16], pbc[:, B_DTB:B_DTB + 16], ALU.add, [b_psDT, b_par], [b_dtx[t]])

                run_pipeline([p4a, p4b], NT)
                act(dtx[:], dtx[:], AF.Exp, b_dtx, b_dtx)
                act(dtsb[:].rearrange("p d c h -> p c d h"), dtx[:].rearrange("p c (d h) -> p c d h", h=8), AF.Ln, b_dtx, b_dt, bias=1.0)

                ps_acs = psDT[:, 0:256]
                ps_tot = psDT[:, 256:512]
                tt(V, da[:], dtsb[:], a_bc[:].rearrange("p (d h) -> p d h", h=8).unsqueeze(2).to_broadcast([128, 2, NT, 8]), ALU.mult,
                   [b_set, b_abc] + b_dt, [b_set])
                for d_ in range(2):
                    mm(ps_acs[:, d_ * 128:(d_ + 1) * 128], Tm[:, d_, :], da[:, d_].rearrange("p c h -> p (c h)"), True, True,
                       [b_set], [b_psDT])
                mm(ps_tot, ones_f[:], fl(da), True, True, [b_set, b_const], [b_psDT])
                cp(V, fl(acs), ps_acs, [b_psDT], [b_set])
                act(fl(negu), fl(dtsb), AF.Ln, b_dt, [b_set])
                tt(V, fl(negu), fl(negu), fl(acs), ALU.subtract, [b_set], [b_set])
                act(fl(eacs), fl(acs), AF.Exp, [b_set], [b_set])
                tt(V, tmp5[:], ps_tot, fl(acs), ALU.subtract, [b_psDT, b_set], [b_set])
                act(tmp5[:], tmp5[:], AF.Exp, [b_set], [b_set])
                tt(V, fl(wsb), tmp5[:], fl(dtsb), ALU.mult, [b_set] + b_dt, [b_set])
                act(fl(cdec), ps_tot, AF.Exp, [b_psDT], [b_set])

                cnt4 = [0, 0]

                def p4x(cc):
                    rw, brw = raw[cc % 2], b_raw[cc % 2]
                    for j in range(4):
                        i2 = cnt4[0] % 2
                        cnt4[0] += 1
                        for dc in range(8):
                            mm(psX[i2][:], wB[:, dc, cc * 128:(cc + 1) * 128], hT[:, dc, j * 512:(j + 1) * 512],
                               dc == 0, dc == 7, b_hT[j * 4:(j + 1) * 4] + [b_wB[dc]], [b_psX[i2]])
                        cp(V, rw[:, 2 + j * 512:2 + (j + 1) * 512], psX[i2][:], [b_psX[i2]], [brw])

                def p4c(cc):
                    rw, brw = raw[cc % 2], b_raw[cc % 2]
                    for j in range(4):
                        i2 = cnt4[1] % 2
                        cnt4[1] += 1
                        for k in range(5):
                            mm(psC[i2][:], diag[:, k, cc, :], rw[:, j * 512 + k:j * 512 + k + 512], k == 0, k == 4,
                               [b_diag, brw], [b_psC[i2]])
                        act(xbcT[:, cc, j * 512:(j + 1) * 512], psC[i2][:], AF.Silu, [b_psC[i2], b_par],
                            [b_xbcT[cc][j]] + ((b_wBz + b_wBd) if cc < 3 else []), bias=pcol[:, C_CONVB + cc:C_CONVB + cc + 1])

                def p4t(cc):
                    if cc >= 6:
                        return
                    for half in range(2):
                        for tl in range(8):
                            t = half * 8 + tl
                            tr(ps_tr4[:, tl * 128:(tl + 1) * 128], xbcT[:, cc, t * 128:(t + 1) * 128], ident_b[:],
                               [b_xbcT[cc][t // 4], b_const], [b_ptr4])
                        src = ps_tr4[:].rearrange("p (t k) -> p t k", k=128)
                        if cc < 4:
                            cp(V, xs_tm[:, half * 8:(half + 1) * 8, cc * 128:(cc + 1) * 128], src, [b_ptr4], [b_xs[cc][half]])
                        else:
                            cp(V, B_tm[:, half * 8:(half + 1) * 8, (cc - 4) * 128:(cc - 3) * 128], src, [b_ptr4], [b_Btm[cc - 4][half]])

                run_pipeline([p4x, p4c, p4t], 8)
            K.barrier()
            if "xbcT" in dbg_d:
                with ExitStack() as phd:
                    xf = phd.enter_context(nc.sbuf_tensor("s_xbcTf", [128, 8, 512], F32))
                    xsf = phd.enter_context(nc.sbuf_tensor("s_xsf", [128, 2, 512], F32))
                    b_d = Buf("dbg3")
                    cp(V, xf[:], xbcT[:, :, 0:512], [], [b_d])
                    cp(V, xsf[:], xs_tm[:, 0:2, :], [], [b_d])
                    dump("xbcT", xf[:], [b_d])
                    dump("xs_tm", xsf[:], [b_d])
                    dump("dtsb", dtsb[:], [b_d])
                K.barrier()

            for c_ in range(8):
                dma(wo[:, c_, :], w_out_d[c_ * 128:(c_ + 1) * 128, :], [], [b_wo[c_]], f"wo{c_}", q=P)
            with ExitStack() as ph5:
                sb5 = lambda name, shape, dt=F32: ph5.enter_context(nc.sbuf_tensor("s_" + name, shape, dt))
                ps5 = lambda name, shape, dt=F32: ph5.enter_context(nc.psum_tensor("p_" + name, shape, dt))
                ypart = hT[:].bitcast(F32).rearrange("p a (b c) -> p (a b) c", c=512)
                b_yp = bufs(NT, "ypart")
                state = [sb5(f"state{d_}", [128, 512]) for d_ in range(2)]
                prevbf = [sb5(f"prevbf{d_}", [128, 512], BF16) for d_ in range(2)]
                b_state = bufs(2, "state")
                b_prev = bufs(2, "prev")
                Et = [sb5(f"Et{i}", [128, 8, 128], BF16) for i in range(2)]
                Mt = [sb5(f"Mt{i}", [128, 8, 128], BF16) for i in range(2)]
                b_Et = bufs(2, "Et")
                b_Mt = bufs(2, "Mt")
                xsw = [sb5(f"xsw{i}", [128, 512], BF16) for i in range(2)]
                b_xsw = bufs(2, "xsw")
                t1s = [sb5(f"t1_{i}", [128, 512]) for i in range(2)]
                b_t1s = bufs(2, "t1")
                t1, b_t1 = t1s[0], b_t1s[0]
                print("[kernel] ph5 sbuf remaining", nc.sbuf_bytes_remaining)
                yg = sb5("yg", [128, 512])
                b_yg = Buf("yg")
                ynb = sb5("ynb", [128, 512], BF16)
                b_ynb = Buf("ynb")
                scr5 = sb5("scr5", [128, 512], BF16)
                ss5 = sb5("ss5", [128, NT, 4])
                b_ss5 = bufs(NT, "ss5")
                ps_zz = [ps5(f"ps_zz{i}", [128, 512]) for i in range(4)]
                b_pzz = bufs(4, "pzz")
                ps_cbtr = ps5("ps_cbtr", [128, 512])
                ps_cb = ps_cbtr
                ps_tr5 = ps_cbtr[:].bitcast(BF16)
                b_pcb = Buf("pcb")
                b_ptr5 = b_pcb
                ps_y = ps5("ps_y", [128, 512])
                b_py = Buf("py")
                ps_st = ps5("ps_st", [128, 512])
                b_pst5 = Buf("pst5")
                ps_yo = ps5("ps_yo", [128, 512])
                b_pyo = Buf("pyo")
                h8 = lambda ap: ap.rearrange("p (h d) -> p h d", d=64)
                da_hi = sb5("da_hi", [128, 2, NT, 8], BF16)
                da_lo = sb5("da_lo", [128, 2, NT, 8], BF16)
                Tb16 = sb5("Tb16", [128, 2, 128], BF16)
                cp(V, da_hi[:], da[:], [b_set], [b_set])
                cp(V, t1[:, 0:256], fl(da_hi), [b_set], [b_t1])
                tt(V, fl(da_lo), fl(da), t1[:, 0:256], ALU.subtract, [b_set, b_t1], [b_set])
                cp(V, Tb16[:], Tm[:], [b_set], [b_set])

                def make_xsw(c, d_):
                    i2 = c % 2
                    tt(P, h8(xsw[i2][:]), h8(xs_tm[:, c, :]), wsb[:, d_, c, :].unsqueeze(2).to_broadcast([128, 8, 64]), ALU.mult,
                       [b_xs[cc_][c // 8] for cc_ in range(4)] + [b_set], [b_xsw[i2]])

                def states_and_yoff(c, d_, first, ti=0, cast_eng=P):
                    tokc = slice(c * 128, (c + 1) * 128)
                    i2 = c % 2
                    for g in range(2):
                        mm(ps_st[:, g * 256:(g + 1) * 256], B_tm[:, c, g * 128:(g + 1) * 128], xsw[i2][:, g * 256:(g + 1) * 256], True, True,
                           [b_Btm[g][c // 8], b_xsw[i2]], [b_pst5])
                    if not first:
                        for g in range(2):
                            mm(ps_yo[:, g * 256:(g + 1) * 256], xbcT[:, 6 + g, tokc], prevbf[d_][:, g * 256:(g + 1) * 256], True, True,
                               [b_xbcT[6 + g][c // 4], b_prev[d_]], [b_pyo])
                        tt(V, h8(state[d_][:]), h8(state[d_][:]), cdec[:, d_, c, :].unsqueeze(2).to_broadcast([128, 8, 64]), ALU.mult,
                           [b_state[d_], b_set], [b_state[d_]])
                        tt(V, state[d_][:], state[d_][:], ps_st[:], ALU.add, [b_state[d_], b_pst5], [b_state[d_]])
                        tt(V, h8(t1s[ti][:]), h8(ps_yo[:]), eacs[:, d_, c, :].unsqueeze(2).to_broadcast([128, 8, 64]), ALU.mult,
                           [b_pyo, b_set], [b_t1s[ti]])
                    else:
                        cp(V, state[d_][:], ps_st[:], [b_pst5], [b_state[d_]])
                    cp(cast_eng, prevbf[d_][:], state[d_][:], [b_state[d_]], [b_prev[d_]])

                units = [(c, g) for c in range(NT) for g in range(2)]

                def w1A(u):
                    c, g = units[u]
                    for d_ in range(2):
                        zb = (u % 2) * 2 + d_
                        for hl in range(4):
                            h = g * 4 + hl
                            o_ = ps_zz[zb][:, hl * 128:(hl + 1) * 128]
                            mm(o_, da_hi[:, d_, c, h:h + 1].to_broadcast([128, 128]), Tb16[:, d_, :], True, False, [b_set], [b_pzz[zb]])
                            mm(o_, da_lo[:, d_, c, h:h + 1].to_broadcast([128, 128]), Tb16[:, d_, :], False, False, [b_set], [b_pzz[zb]])
                            mm(o_, ident_b[:], nmask[:, d_, :], False, True, [b_set, b_const], [b_pzz[zb]])

                def w1B(u):
                    c, g = units[u]
                    i2 = u % 2
                    for d_ in range(2):
                        zb = (u % 2) * 2 + d_
                        for hl in range(4):
                            h = g * 4 + hl
                            act(Et[i2][:, d_ * 4 + hl, :], ps_zz[zb][:, hl * 128:(hl + 1) * 128], AF.Exp, [b_pzz[zb], b_set], [b_Et[i2]],
                                bias=negu[:, d_, c, h:h + 1])

                def w1C(u):
                    c, g = units[u]
                    i2 = u % 2
                    tokc = slice(c * 128, (c + 1) * 128)
                    mm(ps_cb[:, 0:128], xbcT[:, 4 + g, tokc], xbcT[:, 6 + g, tokc], True, True,
                       [b_xbcT[4 + g][c // 4], b_xbcT[6 + g][c // 4]], [b_pcb])
                    tt(V, Mt[i2][:], Et[i2][:], ps_cb[:, 0:128].unsqueeze(1).to_broadcast([128, 8, 128]), ALU.mult,
                       [b_Et[i2], b_pcb], [b_Mt[i2]])

                def w1D(u):
                    c, g = units[u]
                    i2 = u % 2
                    for hl in range(4):
                        h = g * 4 + hl
                        o_ = ps_y[:, h * 64:(h + 1) * 64]
                        r_ = xs_tm[:, c, h * 64:(h + 1) * 64]
                        brd = [b_xs[h // 2][c // 8]]
                        mm(o_, Mt[i2][:, hl, :], r_, True, False, [b_Mt[i2]] + brd, [b_py])
                        mm(o_, Mt[i2][:, 4 + hl, :], r_, False, False, [b_Mt[i2]] + brd, [b_py])
                        mm(o_, dskipI[:, h, :], r_, False, True, [b_set] + brd, [b_py])
                    if g == 1:
                        make_xsw(c, 0)

                def w1E(u):
                    c, g = units[u]
                    if g == 1:
                        cp(V, ypart[:, c, :], ps_y[:], [b_py], [b_yp[c]])
                        states_and_yoff(c, 0, c == 0)
                        if c > 0:
                            tt(P, ypart[:, c, :], ypart[:, c, :], t1[:], ALU.add, [b_yp[c], b_t1], [b_yp[c]])

                run_pipeline([w1A, w1B, w1C, w1D, w1E], len(units), order=[2, 4, 3, 1, 0])

                def w2x(i):
                    make_xsw(NT - 1 - i, 1)

                def w2a(i):
                    c = NT - 1 - i
                    states_and_yoff(c, 1, i == 0, ti=i % 2, cast_eng=A)

                def w2b(i):
                    c = NT - 1 - i
                    if i == 0:
                        tt(P, yg[:], ypart[:, c, :], zact[:, c, :], ALU.mult, [b_yp[c], b_zact[c]], [b_yg])
                    else:
                        tt(V, yg[:], ypart[:, c, :], t1s[i % 2][:], ALU.add, [b_yp[c], b_t1s[i % 2]], [b_yg])
                        tt(V, yg[:], yg[:], zact[:, c, :], ALU.mult, [b_yg, b_zact[c]], [b_yg])
                    S5 = ss5[:, c, :]
                    for g in range(2):
                        act(scr5[:, g * 256:(g + 1) * 256], yg[:, g * 256:(g + 1) * 256], AF.Square, [b_yg], [b_ss5[c]], accum=S5[:, g:g + 1])
                    act(S5[:, 2:4], S5[:, 0:2], AF.Sqrt, [b_ss5[c]], [b_ss5[c]], bias=epsb[:, 0:1], scale=1.0 / 256)

                def w2b2(i):
                    c = NT - 1 - i
                    S5 = ss5[:, c, :]
                    recip(S5[:, 2:4], S5[:, 2:4], [b_ss5[c]], [b_ss5[c]])
                    for g in range(2):
                        K.op(A, lambda g=g: nc.scalar.mul(ynb[:, g * 256:(g + 1) * 256], yg[:, g * 256:(g + 1) * 256], S5[:, 2 + g:3 + g]),
                             [b_yg, b_ss5[c]], [b_ynb])

                def w2c(i):
                    c = NT - 1 - i
                    tokc = slice(c * 128, (c + 1) * 128)
                    for j in range(4):
                        tr(ps_tr5[:, j * 128:(j + 1) * 128], ynb[:, j * 128:(j + 1) * 128], ident_b[:], [b_ynb, b_const], [b_ptr5])
                    tt(V, mixS[:, :, tokc], ps_tr5[:, 0:512].rearrange("p (j k) -> p j k", k=128), gssm[:], ALU.mult,
                       [b_ptr5, b_set, b_gssm], [b_mixS[c]])

                run_pipeline([w2x, w2a, w2b, w2b2, w2c], NT, order=[4, 3, 2, 1, 0])
        K.barrier()
        if "mixS" in dbg_d:
            with ExitStack() as phd:
                mf = phd.enter_context(nc.sbuf_tensor("s_mixSf", [128, 4, 512], F32))
                b_d = Buf("dbg4")
                cp(V, mf[:], mixS[:, :, 0:512], b_mixS, [b_d])
                dump("mixS", mf[:], [b_d])
            K.barrier()

        x1 = sb("x1", [128, NT, D])
        b_x1 = [bufs(2, f"x1_{t}_") for t in range(NT)]
        wup0 = sb("wup0", [128, 8, 512], BF16)
        wdn0 = sb("wdn0", [128, 4, D], BF16)
        b_wup = bufs(2, "wup")
        b_wdn = bufs(2, "wdn")
        dma(wup0[:], w_up_d[:, 0:512].rearrange("(c p) f -> p c f", p=128), [], [b_wup[0]], "wup0", q=P)
        dma(wdn0[:], w_dn_d[0:512, :].rearrange("(c p) n -> p c n", p=128), [], [b_wdn[0]], "wdn0", q=P)
        with ExitStack() as ph6:
            sb6 = lambda name, shape, dt=F32: ph6.enter_context(nc.sbuf_tensor("s_" + name, shape, dt))
            ps6 = lambda name, shape, dt=F32: ph6.enter_context(nc.psum_tensor("p_" + name, shape, dt))
            xt6 = [sb6(f"xt6_{i}", [128, D]) for i in range(2)]
            b_xt6 = bufs(2, "xt6")
            xn6 = [sb6(f"xn6_{i}", [128, D], BF16) for i in range(2)]
            b_xn6 = bufs(2, "xn6")
            scr6 = sb6("scr6", [128, D], BF16)
            gmlp = sb6("gmlp", [128, 8, 128])
            b_g6 = Buf("gmlp")
            ss6 = sb6("ss6", [128, NT, 2])
            b_ss6 = bufs(NT, "ss6")
            ps_o6 = [ps6(f"ps_o6_{i}", [128, D]) for i in range(2)]
            b_po6 = bufs(2, "po6")
            ps_t6 = [ps6(f"ps_t6_{i}", [128, D], BF16) for i in range(2)]
            b_pt6 = bufs(2, "pt6")
            cp(V, gmlp[:], pcol[:, C_LNMLP:C_LNMLP + 8].unsqueeze(2).to_broadcast([128, 8, 128]), [b_par], [b_g6])
            def p6a(t):
                i2 = t % 2
                tok = slice(t * 128, (t + 1) * 128)
                dma(xt6[i2][:], x_d[tok, :], [], [b_xt6[i2]], f"x6_{i2}")
                for half in range(2):
                    cs = slice(half * 512, (half + 1) * 512)
                    for j in range(4):
                        mm(ps_o6[i2][:, cs], mixA[:, j, tok], wo[:, j, cs], j == 0, False, [b_mixA[t // 4], b_wo[j]], [b_po6[i2]])
                    for j in range(4):
                        mm(ps_o6[i2][:, cs], mixS[:, j, tok], wo[:, 4 + j, cs], False, j == 3, [b_mixS[t], b_wo[4 + j]], [b_po6[i2]])

            def p6b(t):
                i2 = t % 2
                tt(V, x1[:, t, :], xt6[i2][:], ps_o6[i2][:], ALU.add, [b_xt6[i2], b_po6[i2]], b_x1[t])
                act(scr6[:], x1[:, t, :], AF.Square, b_x1[t], [b_ss6[t]], accum=ss6[:, t, 0:1])
                act(ss6[:, t, 1:2], ss6[:, t, 0:1], AF.Sqrt, [b_ss6[t]], [b_ss6[t]], bias=epsb[:, 0:1], scale=1.0 / D)
                recip(ss6[:, t, 1:2], ss6[:, t, 1:2], [b_ss6[t]], [b_ss6[t]])

            def p6c(t):
                i2 = t % 2
                K.op(A, lambda: nc.scalar.mul(xn6[i2][:], x1[:, t, :], ss6[:, t, 1:2]), b_x1[t] + [b_ss6[t]], [b_xn6[i2]])

            def p6d(t):
                i2 = t % 2
                for dc in range(8):
                    tr(ps_t6[i2][:, dc * 128:(dc + 1) * 128], xn6[i2][:, dc * 128:(dc + 1) * 128], ident_b[:], [b_xn6[i2], b_const], [b_pt6[i2]])

            def p6e(t):
                i2 = t % 2
                tok = slice(t * 128, (t + 1) * 128)
                tt(V, hT[:, :, tok], ps_t6[i2][:].rearrange("p (c k) -> p c k", k=128), gmlp[:], ALU.mult, [b_pt6[i2], b_g6], [b_hT[t]])

            run_pipeline([p6a, p6b, p6c, p6d, p6e], NT)
        K.barrier()

        with ExitStack() as ph7:
            sb7 = lambda name, shape, dt=F32: ph7.enter_context(nc.sbuf_tensor("s_" + name, shape, dt))
            ps7 = lambda name, shape, dt=F32: ph7.enter_context(nc.psum_tensor("p_" + name, shape, dt))
            NFG = 8
            wup = [wup0, sb7("wup1", [128, 8, 512], BF16)]
            wdn = [wdn0, sb7("wdn1", [128, 4, D], BF16)]
            actT = sb7("actT", [128, 4, S], BF16)
            b_actT = [bufs(4, f"actT{fc}_") for fc in range(4)]
            rl = [sb7(f"rl{i}", [128, 512], BF16) for i in range(2)]
            b_rl = bufs(2, "rl")
            psU = [ps7(f"psU{i}", [128, 512]) for i in range(4)]
            b_pU = bufs(4, "pU")
            psD = [ps7(f"psD{i}", [128, 512]) for i in range(4)]
            b_pD = bufs(4, "pD")

            def load_w(fg):
                sl = fg % 2
                dma(wup[sl][:], w_up_d[:, fg * 512:(fg + 1) * 512].rearrange("(c p) f -> p c f", p=128), [], [b_wup[sl]], f"wup{sl}", q=P)
                dma(wdn[sl][:], w_dn_d[fg * 512:(fg + 1) * 512, :].rearrange("(c p) n -> p c n", p=128), [], [b_wdn[sl]], f"wdn{sl}", q=P)

            iu = 0
            idn = 0
            for fg in range(NFG):
                sl = fg % 2
                if fg + 1 < NFG:
                    load_w(fg + 1)
                for fc in range(4):
                    for tb in range(4):
                        iU, iR = iu % 4, iu % 2
                        iu += 1
                        ts_ = slice(tb * 512, (tb + 1) * 512)
                        for dc in range(8):
                            mm(psU[iU][:], wup[sl][:, dc, fc * 128:(fc + 1) * 128], hT[:, dc, ts_], dc == 0, dc == 7,
                               [b_wup[sl]] + b_hT[tb * 4:(tb + 1) * 4], [b_pU[iU]])
                        act(rl[iR][:], psU[iU][:], AF.Relu, [b_pU[iU]], [b_rl[iR]])
                        tt(P, actT[:, fc, ts_], rl[iR][:], rl[iR][:], ALU.mult, [b_rl[iR]], [b_actT[fc][tb]])
                for t in range(NT):
                    tok = slice(t * 128, (t + 1) * 128)
                    for half in range(2):
                        iD = idn % 4
                        idn += 1
                        cs = slice(half * 512, (half + 1) * 512)
                        for fc in range(4):
                            mm(psD[iD][:], actT[:, fc, tok], wdn[sl][:, fc, cs], fc == 0, fc == 3,
                               [b_actT[fc][t // 4], b_wdn[sl]], [b_pD[iD]])
                        tt(V, x1[:, t, cs], x1[:, t, cs], psD[iD][:], ALU.add, [b_x1[t][half], b_pD[iD]], [b_x1[t][half]])
                    if fg == NFG - 1:
                        dma(out_d[tok, :], x1[:, t, :], b_x1[t], [], "out")

        nw = K.emit(es)
        print("[kernel] psum bufs:", sorted(K.psum_names))
        print(f"[kernel] ops={len(K.ops)} waits={nw} sig={K.sig_counts} dma={K.dma_count}")
    return nc


def _host_params(inp, b):
    f = np.float32
    pcol = np.zeros((128, NCOL), f)
    col = lambda v, n: np.ascontiguousarray(np.asarray(v, f).reshape(n, 128).T)
    pcol[:, C_LNMIX:C_LNMIX + 8] = col(inp["ln_mix_g"][0], 8)
    pcol[:, C_LNMLP:C_LNMLP + 8] = col(inp["ln_mlp_g"][0], 8)
    pcol[:, C_QA:C_QA + 2] = col(inp["q_a_norm_g"][0], 2)
    pcol[:, C_KVA:C_KVA + 1] = col(inp["kv_a_norm_g"][0], 1)
    pcol[:, C_GK] = 1.0
    pcol[0:64, C_GK] = inp["k_norm_g"][0][0:64]
    pcol[:, C_GQ] = 1.0
    pcol[0:64, C_GQ] = inp["q_norm_g"][0][0:64]
    pcol[0:64, C_GATT:C_GATT + 8] = np.asarray(inp["attn_out_norm_g"][0], f).reshape(8, 64).T
    pcol[:, C_GSSM:C_GSSM + 4] = col(inp["ssm_norm_g"][0], 4)
    pcol[:, C_CONVB:C_CONVB + 8] = col(inp["conv_b"][0], 8)
    cw = np.asarray(inp["conv_w"][0][:, 0, :], f)
    for k in range(5):
        pcol[:, C_CONVW + k * 8:C_CONVW + (k + 1) * 8] = col(cw[k], 8)
    row = np.zeros((NBC,), f)
    row[B_DTB:B_DTB + 8] = inp["dt_bias_fwd"][0]
    row[B_DTB + 8:B_DTB + 16] = inp["dt_bias_bwd"][0]
    row[B_ALOG:B_ALOG + 8] = inp["a_log_fwd"][0]
    row[B_ALOG + 8:B_ALOG + 16] = inp["a_log_bwd"][0]
    row[B_DSKIP:B_DSKIP + 8] = inp["d_skip"][0]
    row[B_GQR:B_GQR + 32] = inp["q_norm_g"][0][64:96]
    row[B_GKR:B_GKR + 32] = inp["k_norm_g"][0][64:96]
    inv = (1.0 / (np.float32(10000.0) ** (np.arange(0, 32, 2, dtype=f) / np.float32(32)))).astype(f)
    row[B_INVF:B_INVF + 16] = inv
    pbc = np.ascontiguousarray(np.broadcast_to(row[None, :], (128, NBC)))
    pos = np.ascontiguousarray(np.asarray(inp["positions"][b], np.int32).reshape(NT, 128).T)
    return pcol, pbc, pos


_NC_CACHE = {}


def kernel(**inputs):
    inp = {k: np.asarray(v) for k, v in inputs.items()}
    if "nc" not in _NC_CACHE:
        _NC_CACHE["nc"] = build_nc()
    nc = _NC_CACHE["nc"]
    in_maps = []
    for b in range(8):
        pcol, pbc, pos = _host_params(inp, b)
        in_maps.append({
            "x": np.ascontiguousarray(inp["x"][b], np.float32),
            "pos": pos, "pcol": pcol, "pbc": pbc,
            "w_in": np.ascontiguousarray(inp["w_in"][0], np.float32),
            "w_uq": np.ascontiguousarray(inp["w_uq"][0], np.float32),
            "w_ukv": np.ascontiguousarray(inp["w_ukv"][0], np.float32),
            "w_out": np.ascontiguousarray(inp["w_out"][0], np.float32),
            "w_up": np.ascontiguousarray(inp["w_mlp_up"][0], np.float32),
            "w_dn": np.ascontiguousarray(inp["w_mlp_down"][0], np.float32),
        })
    res = run_bass_kernel_spmd(nc, in_maps, core_ids=list(range(8)))
    return np.stack([np.asarray(r["out"], np.float32) for r in res.results], axis=0)
```

```python
import math
from contextlib import ExitStack

import numpy as np
import concourse.bass as bass
import concourse.mybir as mybir
from concourse.bass_utils import run_bass_kernel_spmd

F32 = mybir.dt.float32
BF16 = mybir.dt.bfloat16
I32 = mybir.dt.int32
AF = mybir.ActivationFunctionType
ALU = mybir.AluOpType
AX = mybir.AxisListType

S = 2048
D = 1024
NT = 16
EPS = 1e-6
IN_W = 1968
C_LNMIX, C_LNMLP, C_QA, C_KVA, C_GK, C_GQ, C_GATT, C_GSSM, C_CONVB, C_CONVW, NCOL = 0, 8, 16, 18, 19, 20, 21, 29, 33, 41, 81
B_DTB, B_ALOG, B_DSKIP, B_GQR, B_GKR, B_INVF, NBC = 0, 16, 32, 40, 72, 104, 120
NEG = -30000.0
MAGIC = 12582912.0
TWO_PI = 2.0 * math.pi
C1 = 6.28125
C2 = TWO_PI - C1


class Buf:
    __slots__ = ("w", "r", "name")

    def __init__(self, name=""):
        self.w = None
        self.r = []
        self.name = name


def bufs(n, name=""):
    return [Buf(f"{name}{i}") for i in range(n)]


def is_psum_buf(b):
    n = b.name
    return n.startswith("p") and n != "par" and not n.startswith("prev")


class Sched:
    def __init__(self, nc):
        self.nc = nc
        self.ops = []
        self.barrier_deps = set()
        self.last = {}
        self.dma_count = {}
        self.psum_names = set()

    def op(self, eng, fn, rd=(), wr=(), dma_key=None):
        idx = len(self.ops)
        deps = set(self.barrier_deps)
        for b in rd:
            if b.w is not None:
                deps.add(b.w)
            if is_psum_buf(b):
                deps.update(r for r in b.r if self.ops[r]["eng"] != eng)
                self.psum_names.add(b.name)
        for b in wr:
            if b.w is not None:
                deps.add(b.w)
            deps.update(b.r)
        deps.discard(idx)
        semval = None
        if dma_key is not None:
            self.dma_count[dma_key] = self.dma_count.get(dma_key, 0) + 16
            semval = self.dma_count[dma_key]
        self.ops.append(dict(eng=eng, fn=fn, deps=deps, dma=dma_key, semval=semval, sig=False, sigidx=0))
        for b in rd:
            b.r.append(idx)
        for b in wr:
            b.w = idx
            b.r = []
        if dma_key is None:
            self.last[eng] = idx
        else:
            self.last[("dma", dma_key)] = idx
        return idx

    def barrier(self):
        self.barrier_deps = set(self.last.values())

    def emit(self, es):
        nc = self.nc
        engs = {"pe": nc.tensor, "act": nc.scalar, "dve": nc.vector, "pool": nc.gpsimd, "sp": nc.sync}
        ops = self.ops
        for o in ops:
            for d in o["deps"]:
                p = ops[d]
                if p["dma"] is None:
                    if p["eng"] == "pe" and o["eng"] == "pe" and o["dma"] is None:
                        continue
                    p["sig"] = True
        cnt = {}
        for o in ops:
            if o["dma"] is None and o["sig"]:
                cnt[o["eng"]] = cnt.get(o["eng"], 0) + 1
                o["sigidx"] = cnt[o["eng"]]
        self.sig_counts = dict(cnt)
        esem = {e: es.enter_context(nc.semaphore("sem_" + e)) for e in ("pe", "act", "dve", "pool")}
        dsem = {k: es.enter_context(nc.semaphore("dsem_" + str(k))) for k in self.dma_count}
        waited = {e: {} for e in engs}
        nwait = 0
        for o in ops:
            E = engs[o["eng"]]
            need = {}
            for d in o["deps"]:
                p = ops[d]
                if p["dma"] is not None:
                    key, sem, val = ("d", p["dma"]), dsem[p["dma"]], p["semval"]
                else:
                    if p["eng"] == "pe" and o["eng"] == "pe" and o["dma"] is None:
                        continue
                    key, sem, val = ("e", p["eng"]), esem[p["eng"]], p["sigidx"]
                if need.get(key, (None, 0))[1] < val:
                    need[key] = (sem, val)
            for key, (sem, val) in need.items():
                if waited[o["eng"]].get(key, 0) >= val:
                    continue
                E.wait_ge(sem, val)
                waited[o["eng"]][key] = val
                nwait += 1
            ins = o["fn"]()
            if o["dma"] is not None:
                ins.then_inc(dsem[o["dma"]], 16)
            elif o["sig"]:
                ins.then_inc(esem[o["eng"]], 1)
        for k, sem in dsem.items():
            if waited["sp"].get(("d", k), 0) < self.dma_count[k]:
                nc.sync.wait_ge(sem, self.dma_count[k])
        return nwait


def build_nc(debug=None):
    nc = bass.Bass("TRN2", target_bir_lowering=False)
    x_d = nc.dram_tensor("x", [S, D], F32, kind="ExternalInput").ap()
    pos_d = nc.dram_tensor("pos", [128, NT], I32, kind="ExternalInput").ap()
    pcol_d = nc.dram_tensor("pcol", [128, NCOL], F32, kind="ExternalInput").ap()
    pbc_d = nc.dram_tensor("pbc", [128, NBC], F32, kind="ExternalInput").ap()
    w_in_d = nc.dram_tensor("w_in", [D, IN_W], F32, kind="ExternalInput").ap()
    w_uq_d = nc.dram_tensor("w_uq", [256, 768], F32, kind="ExternalInput").ap()
    w_ukv_d = nc.dram_tensor("w_ukv", [128, 1024], F32, kind="ExternalInput").ap()
    w_out_d = nc.dram_tensor("w_out", [D, D], F32, kind="ExternalInput").ap()
    w_up_d = nc.dram_tensor("w_up", [D, 4096], F32, kind="ExternalInput").ap()
    w_dn_d = nc.dram_tensor("w_dn", [4096, D], F32, kind="ExternalInput").ap()
    out_d = nc.dram_tensor("out", [S, D], F32, kind="ExternalOutput").ap()
    dbg_d = {}
    if debug:
        for name, shape in debug.items():
            dbg_d[name] = nc.dram_tensor("dbg_" + name, list(shape), F32, kind="ExternalOutput").ap()

    K = Sched(nc)
    V, A, P, T, SP = "dve", "act", "pool", "pe", "sp"

    def mm(out, lhsT, rhs, start, stop, rd, wr):
        K.op(T, lambda: nc.tensor.matmul(out, lhsT=lhsT, rhs=rhs, start=start, stop=stop), rd, wr)

    def tr(out, in_, ident, rd, wr):
        K.op(T, lambda: nc.tensor.transpose(out, in_, ident), rd, wr)

    def act(out, in_, func, rd, wr, bias=None, scale=None, accum=None):
        kw = {}
        if bias is not None:
            kw["bias"] = bias
            if not isinstance(bias, float):
                rd = list(rd) + [b_const]
        if scale is not None:
            kw["scale"] = scale
        if accum is not None:
            kw["accum_out"] = accum
        K.op(A, lambda: nc.scalar.activation(out, in_, func, **kw), rd, wr)

    def ts(eng, out, in0, s1, s2, op0, op1, rd, wr):
        e = nc.vector if eng == V else nc.gpsimd
        if op1 is None:
            K.op(eng, lambda: e.tensor_scalar(out, in0, s1, None, op0), rd, wr)
        else:
            K.op(eng, lambda: e.tensor_scalar(out, in0, s1, s2, op0, op1), rd, wr)

    def tt(eng, out, in0, in1, op, rd, wr):
        e = nc.vector if eng == V else nc.gpsimd
        K.op(eng, lambda: e.tensor_tensor(out, in0, in1, op), rd, wr)

    def stt(out, in0, scalar, in1, op0, op1, rd, wr):
        K.op(V, lambda: nc.vector.scalar_tensor_tensor(out, in0, scalar, in1, op0, op1), rd, wr)

    def cp(eng, out, in_, rd, wr):
        if eng == A:
            K.op(A, lambda: nc.scalar.copy(out, in_), rd, wr)
        else:
            e = nc.vector if eng == V else nc.gpsimd
            K.op(eng, lambda: e.tensor_copy(out, in_), rd, wr)

    def recip(out, in_, rd, wr):
        K.op(V, lambda: nc.vector.reciprocal(out, in_), rd, wr)

    def red(out, in_, rd, wr):
        K.op(V, lambda: nc.vector.tensor_reduce(out, in_, AX.X, ALU.add), rd, wr)

    def memset(eng, ap, val, wr):
        e = nc.vector if eng == V else nc.gpsimd
        K.op(eng, lambda: e.memset(ap, val), (), wr)

    def dma(out, in_, rd, wr, key, q=SP):
        e = nc.sync if q == SP else nc.gpsimd
        K.op(q, lambda: e.dma_start(out=out, in_=in_), rd, wr, dma_key=key)

    def rstd_from_ss(ss, n, rstd, b_ss, b_rstd, tmp):
        act(tmp, ss, AF.Sqrt, [b_ss], [b_rstd], bias=epsb[:ss.shape[0], 0:1], scale=1.0 / n)
        recip(rstd, tmp, [b_rstd], [b_rstd])

    def dump(name, ap, rd):
        if name in dbg_d:
            dma(dbg_d[name], ap, rd, [], "dbg_" + name)

    def dbg_copy(name, ap, rd):
        if name in dbg_d:
            tns = dbgbuf[name]
            b_ = Buf("dbgc" + name)
            cp(V, tns[:], ap, rd, [b_])
            dma(dbg_d[name], tns[:], [b_], [], "dbg_" + name)

    with ExitStack() as es:
        sb = lambda name, shape, dt=F32: es.enter_context(nc.sbuf_tensor("s_" + name, shape, dt))
        pcol = sb("pcol", [128, NCOL])
        pbc = sb("pbc", [128, NBC])
        posi = sb("posi", [128, NT], I32)
        ident_f = sb("ident_f", [128, 128])
        ident_b = sb("ident_b", [128, 128], BF16)
        ones_b = sb("ones_b", [128, 128], BF16)
        ones_f = sb("ones_f", [128, 128])
        epsb = sb("epsb", [128, 1])
        fence_t = sb("fence_t", [128, 1])
        b_const = Buf("const")
        b_par = Buf("par")
        dma(pcol[:], pcol_d, [], [b_par], "par")
        dma(pbc[:], pbc_d, [], [b_par], "par")
        dma(posi[:], pos_d, [], [b_par], "par")
        memset(P, ident_f[:], 1.0, [b_const])
        K.op(P, lambda: nc.gpsimd.affine_select(out=ident_f[:], in_=ident_f[:], pattern=[[1, 128]], compare_op=ALU.is_equal,
                                                fill=0.0, base=0, channel_multiplier=-1), [b_const], [b_const])
        cp(P, ident_b[:], ident_f[:], [b_const], [b_const])
        memset(P, ones_b[:], 1.0, [b_const])
        memset(P, ones_f[:], 1.0, [b_const])
        memset(P, epsb[:], EPS, [b_const])

        dbgbuf = {n_: sb("dbgc_" + n_, list(ap_.shape)) for n_, ap_ in dbg_d.items() if n_ in ("ssx", "xn0", "hT2")}
        mixA = sb("mixA", [128, 4, S], BF16)
        b_mixA = bufs(4, "mixA")
        hT = sb("hT", [128, 8, S], BF16)
        b_hT = bufs(NT, "hT")

        def run_pipeline(stages, n, skew=1, order=None):
            ns = len(stages)
            order = order if order is not None else list(range(ns - 1, -1, -1))
            for step in range(n + (ns - 1) * skew):
                for si in order:
                    stg = stages[si]
                    t_ = step - si * skew
                    if 0 <= t_ < n:
                        stg(t_)

        with ExitStack() as ph:
            sbp = lambda name, shape, dt=F32: ph.enter_context(nc.sbuf_tensor("s_" + name, shape, dt))
            psp = lambda name, shape, dt=F32: ph.enter_context(nc.psum_tensor("p_" + name, shape, dt))
            QT = sbp("QT", [96, 8, S], BF16)
            KT = sbp("KT", [96, 8, S], BF16)
            Vt = sbp("Vt", [128, NT, 8, 65], BF16)
            b_QT = bufs(NT, "QT")
            b_KT = bufs(NT, "KT")
            b_V = bufs(NT, "V")
            sincos = sbp("sincos", [128, 2, NT, 16])
            qsc = sbp("qsc", [128, 2, NT, 2, 16])
            b_rope = Buf("rope")
            with ExitStack() as phr:
                sbr = lambda name, shape, dt=F32: phr.enter_context(nc.sbuf_tensor("s_" + name, shape, dt))
                posf = sbr("posf", [128, NT])
                ang = sbr("ang", [128, 2, NT, 16])
                kk = sbr("kk", [128, 2, NT, 16])
                r1 = sbr("r1", [128, 2, NT, 16])
                cp(V, posf[:], posi[:], [b_par], [b_rope])
                tt(V, ang[:, 0], pbc[:, B_INVF:B_INVF + 16].unsqueeze(1).to_broadcast([128, NT, 16]),
                   posf[:].unsqueeze(2).to_broadcast([128, NT, 16]), ALU.mult, [b_par, b_rope], [b_rope])
                ts(V, ang[:, 1], ang[:, 0], math.pi / 2, None, ALU.add, None, [b_rope], [b_rope])
                ts(V, kk[:], ang[:], 1.0 / TWO_PI, MAGIC, ALU.mult, ALU.add, [b_rope], [b_rope])
                ts(V, kk[:], kk[:], MAGIC, None, ALU.subtract, None, [b_rope], [b_rope])
                stt(r1[:], kk[:], -C1, ang[:], ALU.mult, ALU.add, [b_rope], [b_rope])
                stt(r1[:], kk[:], -C2, r1[:], ALU.mult, ALU.add, [b_rope], [b_rope])
                ts(V, r1[:], r1[:], math.pi, -math.pi, ALU.min, ALU.max, [b_rope], [b_rope])
                act(sincos[:], r1[:], AF.Sin, [b_rope], [b_rope])
                for a_ in range(2):
                    for j_ in range(2):
                        tt(V, qsc[:, a_, :, j_, :], sincos[:, a_, :, :],
                           pbc[:, B_GQR + 16 * j_:B_GQR + 16 * (j_ + 1)].unsqueeze(1).to_broadcast([128, NT, 16]), ALU.mult,
                           [b_rope, b_par], [b_rope])
            K.barrier()
            sin_t = lambda t: sincos[:, 0, t, :]
            cos_t = lambda t: sincos[:, 1, t, :]
            with ExitStack() as ph2:
                sb2 = lambda name, shape, dt=F32: ph2.enter_context(nc.sbuf_tensor("s_" + name, shape, dt))
                ps2 = lambda name, shape, dt=F32: ph2.enter_context(nc.psum_tensor("p_" + name, shape, dt))
                wA = sb2("wA", [128, 8, 416], BF16)
                b_wA = bufs(8, "wA")
                for dc in range(8):
                    dma(wA[:, dc, :], w_in_d[dc * 128:(dc + 1) * 128, 0:416], [], [b_wA[dc]], f"wA{dc}", q=P)
                wuq = sb2("wuq", [128, 2, 768], BF16)
                wukv = sb2("wukv", [128, 1024], BF16)
                b_wq = Buf("wq")
                dma(wuq[:], w_uq_d.rearrange("(c p) n -> p c n", p=128), [], [b_wq], "wq", q=P)
                dma(wukv[:], w_ukv_d, [], [b_wq], "wq", q=P)
                xt = [sb2(f"xt{i}", [128, D]) for i in range(3)]
                b_xt = bufs(3, "xt")
                xn = [sb2(f"xn{i}", [128, D], BF16) for i in range(2)]
                b_xn = bufs(2, "xn")
                sq_scr = sb2("sq_scr", [128, D], BF16)
                b_sqs = Buf("sqscr")
                ssx = sb2("ssx", [128, NT, 2])
                b_ss = bufs(NT, "ss")
                gmix = sb2("gmix", [128, 8, 128])
                b_g = Buf("gmix")
                ps_trh = ps2("ps_trh", [128, D], BF16)
                b_ptrh = Buf("ptrh")
                cp(V, gmix[:], pcol[:, C_LNMIX:C_LNMIX + 8].unsqueeze(2).to_broadcast([128, 8, 128]), [b_par], [b_g])
                cqT = sb2("cqT", [128, 2, 512], BF16)
                ckvT = sb2("ckvT", [128, 512], BF16)
                b_cqT = bufs(4, "cqT")
                b_ckvT = bufs(4, "ckvT")
                psA = ps2("psA", [128, 512])
                b_psA = Buf("psA")
                ps_t = ps2("ps_t", [128, 1024], BF16)
                b_pst = Buf("pst")
                psQ = ps2("psQ", [128, 1024])
                b_psQ = Buf("psQ")
                psKV = ps2("psKV", [128, 1024])
                b_psKV = Buf("psKV")
                ps_t2 = ps2("ps_t2", [128, 1024], BF16)
                b_pst2 = Buf("pst2")
                cn = [sb2(f"cn{i}", [128, 384], BF16) for i in range(2)]
                b_cn = bufs(2, "cn")
                st = sb2("st", [128, NT, 8])
                b_st = bufs(NT, "st")
                scr = sb2("scr2", [128, 512], BF16)
                b_scr = Buf("scr2")
                scrb = sb2("scrb", [128, 416], BF16)
                b_scrb = Buf("scrb")
                kpe = sb2("kpe", [128, NT, 32])
                b_kpe = bufs(NT, "kpe")
                kpr = sb2("kpr", [128, NT, 32])
                scrq = sb2("scrq", [128, 768], BF16)
                b_scrq = Buf("scrq")
                ktmp = sb2("ktmp", [128, 4, 16])
                b_kr = Buf("kr")
                ssh = sb2("ssh", [128, 4, 8])
                b_sshk = Buf("sshk")
                b_sshq = Buf("sshq")
                qn = sb2("qn", [128, 8, 96])
                qr = sb2("qr", [128, 8, 32])
                qtmp = sb2("qtmp", [128, 4, 8, 16])
                b_qn = Buf("qn")
                b_qr = Buf("qr")
                Qtm = [sb2(f"Qtm{i}", [128, 8, 96], BF16) for i in range(2)]
                Ktm = [sb2(f"Ktm{i}", [128, 8, 96], BF16) for i in range(2)]
                b_Qtm = bufs(2, "Qtm")
                b_Ktm = bufs(2, "Ktm")
                for t in range(NT):
                    memset(P, Vt[:, t, :, 64:65], 1.0, [b_V[t]])

                def tk(t):
                    return slice(t * 128, (t + 1) * 128)

                def s0(t):
                    i3 = t % 3
                    dma(xt[i3][:], x_d[tk(t), :], [], [b_xt[i3]], f"x{i3}")
                    act(sq_scr[:], xt[i3][:], AF.Square, [b_xt[i3]], [b_sqs, b_ss[t]], accum=ssx[:, t, 0:1])
                    act(ssx[:, t, 1:2], ssx[:, t, 0:1], AF.Sqrt, [b_ss[t]], [b_ss[t]], bias=epsb[:, 0:1], scale=1.0 / D)
                    recip(ssx[:, t, 1:2], ssx[:, t, 1:2], [b_ss[t]], [b_ss[t]])

                def s1(t):
                    i3, i2 = t % 3, t % 2
                    K.op(A, lambda: nc.scalar.mul(xn[i2][:], xt[i3][:], ssx[:, t, 1:2]), [b_xt[i3], b_ss[t]], [b_xn[i2]])

                def s2(t):
                    i2 = t % 2
                    for dc in range(8):
                        tr(ps_trh[:, dc * 128:(dc + 1) * 128], xn[i2][:, dc * 128:(dc + 1) * 128], ident_b[:], [b_xn[i2], b_const], [b_ptrh])

                def s3(t):
                    tt(V, hT[:, :, tk(t)], ps_trh[:].rearrange("p (c k) -> p c k", k=128), gmix[:], ALU.mult, [b_ptrh, b_g], [b_hT[t]])

                def s4(t):
                    for dc in range(8):
                        mm(psA[:, 0:416], hT[:, dc, tk(t)], wA[:, dc, :], dc == 0, dc == 7, [b_hT[t], b_wA[dc]], [b_psA])

                def s5(t):
                    i2 = t % 2
                    S_ = st[:, t, :]
                    act(scrb[:, 0:256], psA[:, 0:256], AF.Square, [b_psA], [b_scrb, b_st[t]], accum=S_[:, 0:1])
                    act(scrb[:, 256:384], psA[:, 256:384], AF.Square, [b_psA], [b_scrb, b_st[t]], accum=S_[:, 1:2])
                    act(scrb[:, 384:416], psA[:, 384:416], AF.Square, [b_psA], [b_scrb, b_st[t]], accum=S_[:, 2:3])
                    act(S_[:, 3:4], S_[:, 0:1], AF.Sqrt, [b_st[t]], [b_st[t]], bias=epsb[:, 0:1], scale=1.0 / 256)
                    act(S_[:, 4:5], S_[:, 1:2], AF.Sqrt, [b_st[t]], [b_st[t]], bias=epsb[:, 0:1], scale=1.0 / 128)
                    recip(S_[:, 5:7], S_[:, 3:5], [b_st[t]], [b_st[t]])
                    ts(V, cn[i2][:, 0:256], psA[:, 0:256], S_[:, 5:6], None, ALU.mult, None, [b_psA, b_st[t]], [b_cn[i2]])
                    ts(V, cn[i2][:, 256:384], psA[:, 256:384], S_[:, 6:7], None, ALU.mult, None, [b_psA, b_st[t]], [b_cn[i2]])
                    tt(V, kpe[:, t, :], psA[:, 384:416], pbc[:, B_GKR:B_GKR + 32], ALU.mult, [b_psA, b_par], [b_kpe[t]])

                def s5k(t):
                    kp = kpe[:, t, :]
                    cs2 = sincos[:, :, t, :]
                    tt(P, ktmp[:, 0:2, :], kp[:, 0:16].unsqueeze(1).to_broadcast([128, 2, 16]), cs2, ALU.mult, [b_kpe[t], b_rope], [b_kr])
                    tt(P, ktmp[:, 2:4, :], kp[:, 16:32].unsqueeze(1).to_broadcast([128, 2, 16]), cs2, ALU.mult, [b_kpe[t], b_rope], [b_kr])
                    tt(P, kpr[:, t, 0:16], ktmp[:, 1, :], ktmp[:, 2, :], ALU.subtract, [b_kr], [b_kpe[t]])
                    tt(P, kpr[:, t, 16:32], ktmp[:, 3, :], ktmp[:, 0, :], ALU.add, [b_kr], [b_kpe[t]])

                def s6(t):
                    i2 = t % 2
                    for c in range(3):
                        tr(ps_t[:, c * 128:(c + 1) * 128], cn[i2][:, c * 128:(c + 1) * 128], ident_b[:], [b_cn[i2], b_const], [b_pst])

                def s7(t):
                    r4 = t % 4
                    for c in range(2):
                        K.op(A, lambda c=c: nc.scalar.mul(cqT[:, c, tk(r4)], ps_t[:, c * 128:(c + 1) * 128], pcol[:, C_QA + c:C_QA + c + 1]),
                             [b_pst, b_par], [b_cqT[r4]])
                    K.op(A, lambda: nc.scalar.mul(ckvT[:, tk(r4)], ps_t[:, 256:384], pcol[:, C_KVA:C_KVA + 1]), [b_pst, b_par], [b_ckvT[r4]])

                def s8(t):
                    r4 = t % 4
                    for (lo, hi) in ((0, 512), (512, 768)):
                        for c in range(2):
                            mm(psQ[:, lo:hi], cqT[:, c, tk(r4)], wuq[:, c, lo:hi], c == 0, c == 1, [b_cqT[r4], b_wq], [b_psQ])
                    for (lo, hi) in ((0, 512), (512, 1024)):
                        mm(psKV[:, lo:hi], ckvT[:, tk(r4)], wukv[:, lo:hi], True, True, [b_ckvT[r4], b_wq], [b_psKV])

                def s9(t):
                    i2 = t % 2
                    S_ = st[:, t, :]
                    q3 = psQ[:, 0:768].rearrange("p (h d) -> p h d", d=96)
                    kv3 = psKV[:].rearrange("p (h d) -> p h d", d=128)
                    sk = scr[:, 0:512].rearrange("p (h d) -> p h d", d=64)
                    sq_ = scrq[:, 0:768].rearrange("p (h d) -> p h d", d=96)
                    act(sk, kv3[:, :, 0:64], AF.Square, [b_psKV], [b_scr])
                    act(sq_, q3, AF.Square, [b_psQ], [b_scrq])
                    cp(A, Vt[:, t, :, 0:64], kv3[:, :, 64:128], [b_psKV], [b_V[t]])
                    red(ssh[:, 0, :], sk, [b_scr], [b_sshk])
                    red(ssh[:, 2, :], sq_, [b_scrq], [b_sshq])
                    ts(V, ssh[:, 0, :], ssh[:, 0, :], S_[:, 2:3], None, ALU.add, None, [b_sshk, b_st[t]], [b_sshk])

                def s9b(t):
                    i2 = t % 2
                    q3 = psQ[:, 0:768].rearrange("p (h d) -> p h d", d=96)
                    kv3 = psKV[:].rearrange("p (h d) -> p h d", d=128)
                    act(ssh[:, 1, :], ssh[:, 0, :], AF.Sqrt, [b_sshk], [b_sshk], bias=epsb[:, 0:1], scale=1.0 / 96)
                    act(ssh[:, 3, :], ssh[:, 2, :], AF.Sqrt, [b_sshq], [b_sshq], bias=epsb[:, 0:1], scale=1.0 / 96)
                    recip(ssh[:, 1, :], ssh[:, 1, :], [b_sshk], [b_sshk])
                    recip(ssh[:, 3, :], ssh[:, 3, :], [b_sshq], [b_sshq])
                    rk = ssh[:, 1, :]
                    rq = ssh[:, 3, :]
                    tt(V, qn[:], q3, rq.unsqueeze(2).to_broadcast([128, 8, 96]), ALU.mult, [b_psQ, b_sshq], [b_qn])
                    tt(V, Ktm[i2][:, :, 0:64], kv3[:, :, 0:64], rk.unsqueeze(2).to_broadcast([128, 8, 64]), ALU.mult,
                       [b_psKV, b_sshk], [b_Ktm[i2]])
                    cp(V, Qtm[i2][:, :, 0:64], qn[:, :, 0:64], [b_qn], [b_Qtm[i2]])
                    tt(P, Ktm[i2][:, :, 64:96], kpr[:, t, :].unsqueeze(1).to_broadcast([128, 8, 32]),
                       rk.unsqueeze(2).to_broadcast([128, 8, 32]), ALU.mult, [b_kpe[t], b_sshk], [b_Ktm[i2]])
                    t1q = qsc[:, :, t, 0, :].unsqueeze(2).to_broadcast([128, 2, 8, 16])
                    t2q = qsc[:, :, t, 1, :].unsqueeze(2).to_broadcast([128, 2, 8, 16])
                    tt(P, qtmp[:, 0:2], qn[:, :, 64:80].unsqueeze(1).to_broadcast([128, 2, 8, 16]), t1q, ALU.mult, [b_qn, b_rope], [b_qr])
                    tt(P, qtmp[:, 2:4], qn[:, :, 80:96].unsqueeze(1).to_broadcast([128, 2, 8, 16]), t2q, ALU.mult, [b_qn, b_rope], [b_qr])
                    tt(P, Qtm[i2][:, :, 64:80], qtmp[:, 1], qtmp[:, 2], ALU.subtract, [b_qr], [b_Qtm[i2]])
                    tt(P, Qtm[i2][:, :, 80:96], qtmp[:, 3], qtmp[:, 0], ALU.add, [b_qr], [b_Qtm[i2]])

                def s10k(t):
                    i2 = t % 2
                    for h in range(8):
                        tr(ps_t2[0:96, h * 128:(h + 1) * 128], Ktm[i2][:, h, :], ident_b[:], [b_Ktm[i2], b_const], [b_pst2])

                def s10q(t):
                    i2 = t % 2
                    for h in range(8):
                        tr(ps_t[0:96, h * 128:(h + 1) * 128], Qtm[i2][:, h, :], ident_b[:], [b_Qtm[i2], b_const], [b_pst])

                def s11k(t):
                    K.op(A, lambda: nc.scalar.mul(KT[:, :, tk(t)], ps_t2[0:96, :].rearrange("p (h k) -> p h k", k=128), pcol[0:96, C_GK:C_GK + 1]),
                         [b_pst2, b_par], [b_KT[t]])

                def s11q(t):
                    ts(V, QT[:, :, tk(t)], ps_t[0:96, :].rearrange("p (h k) -> p h k", k=128), pcol[0:96, C_GQ:C_GQ + 1], None,
                       ALU.mult, None, [b_pst, b_par], [b_QT[t]])

                slots = [(5, s5), (3, s3), (9, lambda t: (s10k(t), s11k(t), s10q(t), s11q(t))), (8, s9), (1, s1), (8, s9b), (5, s5k),
                         (6, lambda t: (s6(t), s7(t))), (4, s4), (2, s2), (0, s0), (7, s8)]
                for step in range(NT + 9):
                    for si, fn_ in slots:
                        t_ = step - si
                        if 0 <= t_ < NT:
                            fn_(t_)
                dbg_copy("ssx", ssx[:], b_ss)
                dbg_copy("xn0", xn[0][:, 0:256], b_xn)
                dbg_copy("hT2", hT[:, 0, 0:256], b_hT)
            K.barrier()
            if "QT" in dbg_d:
                if "hT" in dbg_d:
                    with ExitStack() as phd:
                        htf = phd.enter_context(nc.sbuf_tensor("s_htf", [128, 8, 256], F32))
                        b_dh = Buf("dbgh")
                        cp(V, htf[:], hT[:, :, 0:256], b_hT, [b_dh])
                        dump("hT", htf[:], [b_dh])
                    K.barrier()
                qtf = sbp("qtf", [96, 8, 256])
                ktf = sbp("ktf", [96, 8, 256])
                vtf = sbp("vtf", [128, 2, 8, 65])
                b_d = Buf("dbg")
                cp(V, qtf[:], QT[:, :, 0:256], b_QT, [b_d])
                cp(V, ktf[:], KT[:, :, 0:256], b_KT, [b_d])
                cp(V, vtf[:], Vt[:, 0:2], b_V, [b_d])
                dump("QT", qtf[:], [b_d])
                dump("KT", ktf[:], [b_d])
                dump("Vt", vtf[:], [b_d])

            with ExitStack() as ph3:
                sb3 = lambda name, shape, dt=F32: ph3.enter_context(nc.sbuf_tensor("s_" + name, shape, dt))
                ps3 = lambda name, shape, dt=F32: ph3.enter_context(nc.psum_tensor("p_" + name, shape, dt))
                NPL = 2
                ps_l = [ps3(f"ps_l{i}", [128, 1024]) for i in range(NPL)]
                b_psl = bufs(NPL, "psl")
                ps_o = [ps3(f"ps_o{i}", [65, 512]) for i in range(2)]
                b_pso = bufs(2, "pso")
                ps_d = ps3("ps_d", [64, 512])
                b_psd = Buf("psd")
                ps_s = ps3("ps_s", [64, 512])
                b_pss = Buf("pss")
                NPT = 3
                PT = [sb3(f"PT{i}", [128, 1024], BF16) for i in range(NPT)]
                b_PT = bufs(NPT, "PT")
                osb = [sb3(f"osb{i}", [65, 512]) for i in range(2)]
                b_osb = bufs(2, "osb")
                rden = sb3("rden", [64, 512])
                b_rden = Buf("rden")
                att2 = [sb3(f"att{i}", [64, 8, 512]) for i in range(2)]
                b_att2 = [bufs(8, f"att{i}_") for i in range(2)]
                sqa2 = [sb3(f"sqa{i}", [64, 8, 512], BF16) for i in range(2)]
                b_sqa2 = [bufs(8, f"sqa{i}_") for i in range(2)]
                rstd_a = sb3("rstd_a", [64, 512])
                b_rsa = Buf("rsa")
                modd = sb3("modd", [64, 4, 512], BF16)
                b_modd = Buf("modd")
                dh = sb3("dh", [128, 2, 512], BF16)
                sel_b = sb3("sel_b", [128, 64], BF16)
                dtmp = sb3("dtmp", [65, 512])
                b_dh = Buf("dh")
                sel65 = sb3("sel65", [65, 64])
                b_sel = Buf("sel")
                memset(P, sel65[:], 0.0, [b_sel])
                memset(P, sel65[64:65, :], 1.0, [b_sel])
                memset(P, sel_b[:], 0.0, [b_sel])
                memset(P, sel_b[64:65, :], 1.0, [b_sel])
                memset(P, dh[:], 0.0, [b_dh])
                scale = 96 ** -0.5
                iters = [(qc, h, kp) for qc in range(4) for h in range(8) for kp in range(NT // 2)]
                N_IT = len(iters)
                deferred = {}

                def defer(step, fn):
                    deferred.setdefault(step, []).append(fn)

                def qsl(qc):
                    return slice(qc * 512, (qc + 1) * 512)

                def st_L(i):
                    qc, h, kp = iters[i]
                    il = i % NPL
                    for j in range(2):
                        kt = 2 * kp + j
                        mm(ps_l[il][:, j * 512:(j + 1) * 512], KT[:, h, kt * 128:(kt + 1) * 128], QT[:, h, qsl(qc)], True, True,
                           [b_KT[kt]] + b_QT[qc * 4:(qc + 1) * 4], [b_psl[il]])

                def st_E(i):
                    il, ip = i % NPL, i % NPT
                    act(PT[ip][:], ps_l[il][:], AF.Exp, [b_psl[il]], [b_PT[ip]], scale=scale)

                def st_PV(i):
                    qc, h, kp = iters[i]
                    hh = i // (NT // 2)
                    ip = i % NPT
                    for j in range(2):
                        kt = 2 * kp + j
                        mm(ps_o[hh % 2][:], Vt[:, kt, h, :], PT[ip][:, j * 512:(j + 1) * 512], kt == 0, kt == NT - 1,
                           [b_V[kt], b_PT[ip]], [b_pso[hh % 2]])

                def tailA(hh):
                    cp(V, osb[hh % 2][:], ps_o[hh % 2][:], [b_pso[hh % 2]], [b_osb[hh % 2]])

                def tailA2(hh):
                    o64 = osb[hh % 2][64:65, :]
                    cp(V, dh[64:65, 0, :], o64, [b_osb[hh % 2]], [b_dh])
                    cp(V, dtmp[64:65, :], dh[64:65, 0, :], [b_dh], [b_dh])
                    tt(V, dh[64:65, 1, :], o64, dtmp[64:65, :], ALU.subtract, [b_osb[hh % 2], b_dh], [b_dh])

                def tailB(hh):
                    mm(ps_d[:], sel_b[:], dh[:, 0, :], True, False, [b_sel, b_dh], [b_psd])
                    mm(ps_d[:], sel_b[:], dh[:, 1, :], False, True, [b_sel, b_dh], [b_psd])

                def tailC(hh):
                    h = hh % 8
                    att, b_att, sqa, b_sqa = att2[(hh // 8) % 2], b_att2[(hh // 8) % 2], sqa2[(hh // 8) % 2], b_sqa2[(hh // 8) % 2]
                    recip(rden[:], ps_d[:], [b_psd], [b_rden])
                    tt(V, att[:, h, :], osb[hh % 2][0:64, :], rden[:], ALU.mult, [b_osb[hh % 2], b_rden], [b_att[h]])
                    tt(P, sqa[:, h, :], att[:, h, :], att[:, h, :], ALU.mult, [b_att[h]], [b_sqa[h]])

                def normA(qc):
                    sqa, b_sqa = sqa2[qc % 2], b_sqa2[qc % 2]
                    for h in range(8):
                        mm(ps_s[:], ones_b[0:64, 0:64], sqa[:, h, :], h == 0, h == 7, [b_const, b_sqa[h]], [b_pss])

                def normB(qc):
                    act(rstd_a[:], ps_s[:], AF.Ln, [b_pss], [b_rsa], bias=epsb[0:64, 0:1], scale=1.0 / 512)
                    act(rstd_a[:], rstd_a[:], AF.Exp, [b_rsa], [b_rsa], scale=-0.5)

                def normC(qc, hp):
                    att, b_att = att2[qc % 2], b_att2[qc % 2]
                    h = 2 * hp
                    stt(mixA[0:64, hp, qsl(qc)], att[:, h, :], pcol[0:64, C_GATT + h:C_GATT + h + 1], rstd_a[:], ALU.mult, ALU.mult,
                        [b_att[h], b_par, b_rsa], [b_mixA[qc]])
                    h = 2 * hp + 1
                    stt(modd[:, hp, :], att[:, h, :], pcol[0:64, C_GATT + h:C_GATT + h + 1], rstd_a[:], ALU.mult, ALU.mult,
                        [b_att[h], b_par, b_rsa], [b_modd])
                    if hp == 3:
                        dma(mixA[64:128, :, qsl(qc)], modd[:], [b_modd], [b_mixA[qc]], "modd")

                for step in range(N_IT + 36):
                    if step < N_IT:
                        st_L(step)
                    if 0 <= step - 1 < N_IT:
                        st_E(step - 1)
                    i = step - 2
                    if 0 <= i < N_IT:
                        st_PV(i)
                        qc, h, kp = iters[i]
                        if kp == NT // 2 - 1:
                            hh = i // (NT // 2)
                            defer(step + 1, lambda hh=hh: tailA(hh))
                            defer(step + 1, lambda hh=hh: tailA2(hh))
                            defer(step + 7, lambda hh=hh: tailB(hh))
                            defer(step + 8, lambda hh=hh: tailC(hh))
                            if h == 7:
                                defer(step + 18, lambda qc=qc: normA(qc))
                                defer(step + 22, lambda qc=qc: normB(qc))
                                for hp in range(4):
                                    defer(step + 23 + 2 * hp, lambda qc=qc, hp=hp: normC(qc, hp))
                    for fn in deferred.pop(step, []):
                        fn()
                assert not deferred
        if "mixA" in dbg_d:
            K.barrier()
            with ExitStack() as phd:
                mf = phd.enter_context(nc.sbuf_tensor("mixAf", [128, 4, 512], F32))
                b_d = Buf("dbg2")
                cp(V, mf[:], mixA[:, :, 0:512], b_mixA, [b_d])
                dump("mixA", mf[:], [b_d])
            K.barrier()

        mixS = sb("mixS", [128, 4, S], BF16)
        b_mixS = bufs(NT, "mixS")
        wo = sb("wo", [128, 8, D], BF16)
        b_wo = bufs(8, "wo")

        with ExitStack() as phS:
            sbS = lambda name, shape, dt=F32: phS.enter_context(nc.sbuf_tensor("s_" + name, shape, dt))
            zact = sbS("zact", [128, NT, 512], BF16)
            xbcT = sbS("xbcT", [128, 8, S], BF16)
            xs_tm = sbS("xs_tm", [128, NT, 512], BF16)
            B_tm = sbS("B_tm", [128, NT, 256], BF16)
            dtsb = sbS("dtsb", [128, 2, NT, 8])
            b_zact = bufs(NT, "zact")
            b_xbcT = [bufs(4, f"xbcT{cc}_") for cc in range(8)]
            b_xs = [bufs(2, f"xs{cc}_") for cc in range(4)]
            b_Btm = [bufs(2, f"Btm{g}_") for g in range(2)]
            b_dt = bufs(NT, "dt")
            Tm = sbS("Tm", [128, 2, 128])
            nmask = sbS("nmask", [128, 2, 128], BF16)
            dskipI = sbS("dskipI", [128, 8, 128], BF16)
            gssm = sbS("gssm", [128, 4, 128])
            a_bc = sbS("a_bc", [128, 16])
            da = sbS("da", [128, 2, NT, 8])
            acs = sbS("acs", [128, 2, NT, 8])
            negu = sbS("negu", [128, 2, NT, 8])
            eacs = sbS("eacs", [128, 2, NT, 8])
            wsb = sbS("wsb", [128, 2, NT, 8])
            cdec = sbS("cdec", [128, 2, NT, 8])
            tmp5 = sbS("tmp5", [128, 256])
            fl = lambda a: a[:].rearrange("p d c h -> p (d c h)")
            wBz_v = xbcT[:, 0:3, :].rearrange("p a b -> p (a b)")[:, 0:8 * 528].rearrange("p (c n) -> p c n", n=528)
            b_fence = Buf("fence")
            b_wBz = bufs(8, "wBz")
            b_wBd = bufs(8, "wBd")
            memset(P, fence_t[:], 0.0, b_KT + b_QT + b_V + [b_fence])
            for dc in range(8):
                dma(wBz_v[:, dc, 0:512], w_in_d[dc * 128:(dc + 1) * 128, 416:928], [b_fence], [b_wBz[dc]], f"wBz{dc}", q=P)
            dma(wBz_v[:, :, 512:528], w_in_d[:, 1952:1968].rearrange("(c p) n -> p c n", p=128), [b_fence], b_wBd, "wBd", q=P)
            K.barrier()
            b_set = Buf("ssdset")
            b_gssm = Buf("gssm")
            b_abc = Buf("abc")

            def ssd_consts():
                memset(P, Tm[:], 1.0, [b_set])
                K.op(P, lambda: nc.gpsimd.affine_select(out=Tm[:, 0, :], in_=Tm[:, 0, :], pattern=[[1, 128]], compare_op=ALU.is_ge,
                                                        fill=0.0, base=0, channel_multiplier=-1), [b_set], [b_set])
                K.op(P, lambda: nc.gpsimd.affine_select(out=Tm[:, 1, :], in_=Tm[:, 1, :], pattern=[[-1, 128]], compare_op=ALU.is_ge,
                                                        fill=0.0, base=0, channel_multiplier=1), [b_set], [b_set])
                memset(P, nmask[:], 0.0, [b_set])
                K.op(P, lambda: nc.gpsimd.affine_select(out=nmask[:, 0, :], in_=nmask[:, 0, :], pattern=[[1, 128]], compare_op=ALU.is_ge,
                                                        fill=NEG, base=0, channel_multiplier=-1), [b_set], [b_set])
                K.op(P, lambda: nc.gpsimd.affine_select(out=nmask[:, 1, :], in_=nmask[:, 1, :], pattern=[[-1, 128]], compare_op=ALU.is_ge,
                                                        fill=NEG, base=0, channel_multiplier=1), [b_set], [b_set])
                for h in range(8):
                    ts(P, dskipI[:, h, :], ident_b[:], pbc[:, B_DSKIP + h:B_DSKIP + h + 1], None, ALU.mult, None, [b_const, b_par], [b_set])
                cp(V, gssm[:], pcol[:, C_GSSM:C_GSSM + 4].unsqueeze(2).to_broadcast([128, 4, 128]), [b_par], [b_gssm])
                act(a_bc[:], pbc[:, B_ALOG:B_ALOG + 16], AF.Exp, [b_par], [b_abc])
                ts(V, a_bc[:], a_bc[:], -1.0, None, ALU.mult, None, [b_abc], [b_abc])


            with ExitStack() as ph4:
                sb4 = lambda name, shape, dt=F32: ph4.enter_context(nc.sbuf_tensor("s_" + name, shape, dt))
                ps4 = lambda name, shape, dt=F32: ph4.enter_context(nc.psum_tensor("p_" + name, shape, dt))
                wB = sb4("wB", [128, 8, 1024], BF16)
                b_wB = bufs(8, "wB")
                for dc in range(8):
                    dma(wB[:, dc, :], w_in_d[dc * 128:(dc + 1) * 128, 928:1952], b_wBz, [b_wB[dc]], f"wB{dc}", q=P)
                ssd_consts()
                diag = sb4("diag", [128, 5, 8, 128], BF16)
                b_diag = Buf("diag")
                tt(V, diag[:].rearrange("p k c n -> p (k c) n"), ident_b[:].unsqueeze(1).to_broadcast([128, 40, 128]),
                   pcol[:, C_CONVW:C_CONVW + 40].unsqueeze(2).to_broadcast([128, 40, 128]), ALU.mult, [b_const, b_par], [b_diag])
                raw = [sb4(f"raw{i}", [128, S + 4], BF16) for i in range(2)]
                b_raw = bufs(2, "raw")
                for i in range(2):
                    memset(P, raw[i][:, 0:2], 0.0, [b_raw[i]])
                    memset(P, raw[i][:, S + 2:S + 4], 0.0, [b_raw[i]])
                psZ = [ps4(f"psZ{i}", [128, 512]) for i in range(2)]
                b_psZ = bufs(2, "psZ")
                psDT = ps4("psDT", [128, 512])
                b_psDT = Buf("psDT")
                psX = [ps4(f"psX{i}", [128, 512]) for i in range(2)]
                b_psX = bufs(2, "psX")
                psC = [ps4(f"psC{i}", [128, 512]) for i in range(2)]
                b_psC = bufs(2, "psC")
                ps_tr4 = ps4("ps_tr4", [128, 1024], BF16)
                b_ptr4 = Buf("ptr4")
                dtx = sb4("dtx", [128, NT, 16])
                b_dtx = bufs(NT, "dtx")
                def p4a(t):
                    tok = slice(t * 128, (t + 1) * 128)
                    i2 = t % 2
                    for dc in range(8):
                        mm(psZ[i2][:], hT[:, dc, tok], wBz_v[:, dc, 0:512], dc == 0, dc == 7, [b_hT[t], b_wBz[dc]], [b_psZ[i2]])
                    for dc in range(8):
                        mm(psDT[:, 0:16], hT[:, dc, tok], wBz_v[:, dc, 512:528], dc == 0, dc == 7, [b_hT[t], b_wBd[dc]], [b_psDT])

                def p4b(t):
                    i2 = t % 2
                    act(zact[:, t, :], psZ[i2][:], AF.Silu, [b_psZ[i2]], [b_zact[t]])
                    tt(V, dtx[:, t, :], psDT[:, 0:16], pbc[:, B_DTB:B_DTB + 16], ALU.add, [b_psDT, b_par], [b_dtx[t]])

                run_pipeline([p4a, p4b], NT)
                act(dtx[:], dtx[:], AF.Exp, b_dtx, b_dtx)
                act(dtsb[:].rearrange("p d c h -> p c d h"), dtx[:].rearrange("p c (d h) -> p c d h", h=8), AF.Ln, b_dtx, b_dt, bias=1.0)

                ps_acs = psDT[:, 0:256]
                ps_tot = psDT[:, 256:512]
                tt(V, da[:], dtsb[:], a_bc[:].rearrange("p (d h) -> p d h", h=8).unsqueeze(2).to_broadcast([128, 2, NT, 8]), ALU.mult,
                   [b_set, b_abc] + b_dt, [b_set])
                for d_ in range(2):
                    mm(ps_acs[:, d_ * 128:(d_ + 1) * 128], Tm[:, d_, :], da[:, d_].rearrange("p c h -> p (c h)"), True, True,
                       [b_set], [b_psDT])
                mm(ps_tot, ones_f[:], fl(da), True, True, [b_set, b_const], [b_psDT])
                cp(V, fl(acs), ps_acs, [b_psDT], [b_set])
                act(fl(negu), fl(dtsb), AF.Ln, b_dt, [b_set])
                tt(V, fl(negu), fl(negu), fl(acs), ALU.subtract, [b_set], [b_set])
                act(fl(eacs), fl(acs), AF.Exp, [b_set], [b_set])
                tt(V, tmp5[:], ps_tot, fl(acs), ALU.subtract, [b_psDT, b_set], [b_set])
                act(tmp5[:], tmp5[:], AF.Exp, [b_set], [b_set])
                tt(V, fl(wsb), tmp5[:], fl(dtsb), ALU.mult, [b_set] + b_dt, [b_set])
                act(fl(cdec), ps_tot, AF.Exp, [b_psDT], [b_set])

                cnt4 = [0, 0]

                def p4x(cc):
                    rw, brw = raw[cc % 2], b_raw[cc % 2]
                    for j in range(4):
                        i2 = cnt4[0] % 2
                        cnt4[0] += 1
                        for dc in range(8):
                            mm(psX[i2][:], wB[:, dc, cc * 128:(cc + 1) * 128], hT[:, dc, j * 512:(j + 1) * 512],
                               dc == 0, dc == 7, b_hT[j * 4:(j + 1) * 4] + [b_wB[dc]], [b_psX[i2]])
                        cp(V, rw[:, 2 + j * 512:2 + (j + 1) * 512], psX[i2][:], [b_psX[i2]], [brw])

                def p4c(cc):
                    rw, brw = raw[cc % 2], b_raw[cc % 2]
                    for j in range(4):
                        i2 = cnt4[1] % 2
                        cnt4[1] += 1
                        for k in range(5):
                            mm(psC[i2][:], diag[:, k, cc, :], rw[:, j * 512 + k:j * 512 + k + 512], k == 0, k == 4,
                               [b_diag, brw], [b_psC[i2]])
                        act(xbcT[:, cc, j * 512:(j + 1) * 512], psC[i2][:], AF.Silu, [b_psC[i2], b_par],
                            [b_xbcT[cc][j]] + ((b_wBz + b_wBd) if cc < 3 else []), bias=pcol[:, C_CONVB + cc:C_CONVB + cc + 1])

                def p4t(cc):
                    if cc >= 6:
                        return
                    for half in range(2):
                        for tl in range(8):
                            t = half * 8 + tl
                            tr(ps_tr4[:, tl * 128:(tl + 1) * 128], xbcT[:, cc, t * 128:(t + 1) * 128], ident_b[:],
                               [b_xbcT[cc][t // 4], b_const], [b_ptr4])
                        src = ps_tr4[:].rearrange("p (t k) -> p t k", k=128)
                        if cc < 4:
                            cp(V, xs_tm[:, half * 8:(half + 1) * 8, cc * 128:(cc + 1) * 128], src, [b_ptr4], [b_xs[cc][half]])
                        else:
                            cp(V, B_tm[:, half * 8:(half + 1) * 8, (cc - 4) * 128:(cc - 3) * 128], src, [b_ptr4], [b_Btm[cc - 4][half]])

                run_pipeline([p4x, p4c, p4t], 8)
            K.barrier()
            if "xbcT" in dbg_d:
                with ExitStack() as phd:
                    xf = phd.enter_context(nc.sbuf_tensor("s_xbcTf", [128, 8, 512], F32))
                    xsf = phd.enter_context(nc.sbuf_tensor("s_xsf", [128, 2, 512], F32))
                    b_d = Buf("dbg3")
                    cp(V, xf[:], xbcT[:, :, 0:512], [], [b_d])
                    cp(V, xsf[:], xs_tm[:, 0:2, :], [], [b_d])
                    dump("xbcT", xf[:], [b_d])
                    dump("xs_tm", xsf[:], [b_d])
                    dump("dtsb", dtsb[:], [b_d])
                K.barrier()

            for c_ in range(8):
                dma(wo[:, c_, :], w_out_d[c_ * 128:(c_ + 1) * 128, :], [], [b_wo[c_]], f"wo{c_}", q=P)
            with ExitStack() as ph5:
                sb5 = lambda name, shape, dt=F32: ph5.enter_context(nc.sbuf_tensor("s_" + name, shape, dt))
                ps5 = lambda name, shape, dt=F32: ph5.enter_context(nc.psum_tensor("p_" + name, shape, dt))
                ypart = hT[:].bitcast(F32).rearrange("p a (b c) -> p (a b) c", c=512)
                b_yp = bufs(NT, "ypart")
                state = [sb5(f"state{d_}", [128, 512]) for d_ in range(2)]
                prevbf = [sb5(f"prevbf{d_}", [128, 512], BF16) for d_ in range(2)]
                b_state = bufs(2, "state")
                b_prev = bufs(2, "prev")
                Et = [sb5(f"Et{i}", [128, 8, 128], BF16) for i in range(2)]
                Mt = [sb5(f"Mt{i}", [128, 8, 128], BF16) for i in range(2)]
                b_Et = bufs(2, "Et")
                b_Mt = bufs(2, "Mt")
                xsw = [sb5(f"xsw{i}", [128, 512], BF16) for i in range(2)]
                b_xsw = bufs(2, "xsw")
                t1s = [sb5(f"t1_{i}", [128, 512]) for i in range(2)]
                b_t1s = bufs(2, "t1")
                t1, b_t1 = t1s[0], b_t1s[0]
                print("[kernel] ph5 sbuf remaining", nc.sbuf_bytes_remaining)
                yg = sb5("yg", [128, 512])
                b_yg = Buf("yg")
                ynb = sb5("ynb", [128, 512], BF16)
                b_ynb = Buf("ynb")
                scr5 = sb5("scr5", [128, 512], BF16)
                ss5 = sb5("ss5", [128, NT, 4])
                b_ss5 = bufs(NT, "ss5")
                ps_zz = [ps5(f"ps_zz{i}", [128, 512]) for i in range(4)]
                b_pzz = bufs(4, "pzz")
                ps_cbtr = ps5("ps_cbtr", [128, 512])
                ps_cb = ps_cbtr
                ps_tr5 = ps_cbtr[:].bitcast(BF16)
                b_pcb = Buf("pcb")
                b_ptr5 = b_pcb
                ps_y = ps5("ps_y", [128, 512])
                b_py = Buf("py")
                ps_st = ps5("ps_st", [128, 512])
                b_pst5 = Buf("pst5")
                ps_yo = ps5("ps_yo", [128, 512])
                b_pyo = Buf("pyo")
                h8 = lambda ap: ap.rearrange("p (h d) -> p h d", d=64)
                da_hi = sb5("da_hi", [128, 2, NT, 8], BF16)
                da_lo = sb5("da_lo", [128, 2, NT, 8], BF16)
                Tb16 = sb5("Tb16", [128, 2, 128], BF16)
                cp(V, da_hi[:], da[:], [b_set], [b_set])
                cp(V, t1[:, 0:256], fl(da_hi), [b_set], [b_t1])
                tt(V, fl(da_lo), fl(da), t1[:, 0:256], ALU.subtract, [b_set, b_t1], [b_set])
                cp(V, Tb16[:], Tm[:], [b_set], [b_set])

                def make_xsw(c, d_):
                    i2 = c % 2
                    tt(P, h8(xsw[i2][:]), h8(xs_tm[:, c, :]), wsb[:, d_, c, :].unsqueeze(2).to_broadcast([128, 8, 64]), ALU.mult,
                       [b_xs[cc_][c // 8] for cc_ in range(4)] + [b_set], [b_xsw[i2]])

                def states_and_yoff(c, d_, first, ti=0, cast_eng=P):
                    tokc = slice(c * 128, (c + 1) * 128)
                    i2 = c % 2
                    for g in range(2):
                        mm(ps_st[:, g * 256:(g + 1) * 256], B_tm[:, c, g * 128:(g + 1) * 128], xsw[i2][:, g * 256:(g + 1) * 256], True, True,
                           [b_Btm[g][c // 8], b_xsw[i2]], [b_pst5])
                    if not first:
                        for g in range(2):
                            mm(ps_yo[:, g * 256:(g + 1) * 256], xbcT[:, 6 + g, tokc], prevbf[d_][:, g * 256:(g + 1) * 256], True, True,
                               [b_xbcT[6 + g][c // 4], b_prev[d_]], [b_pyo])
                        tt(V, h8(state[d_][:]), h8(state[d_][:]), cdec[:, d_, c, :].unsqueeze(2).to_broadcast([128, 8, 64]), ALU.mult,
                           [b_state[d_], b_set], [b_state[d_]])
                        tt(V, state[d_][:], state[d_][:], ps_st[:], ALU.add, [b_state[d_], b_pst5], [b_state[d_]])
                        tt(V, h8(t1s[ti][:]), h8(ps_yo[:]), eacs[:, d_, c, :].unsqueeze(2).to_broadcast([128, 8, 64]), ALU.mult,
                           [b_pyo, b_set], [b_t1s[ti]])
                    else:
                        cp(V, state[d_][:], ps_st[:], [b_pst5], [b_state[d_]])
                    cp(cast_eng, prevbf[d_][:], state[d_][:], [b_state[d_]], [b_prev[d_]])

                units = [(c, g) for c in range(NT) for g in range(2)]

                def w1A(u):
                    c, g = units[u]
                    for d_ in range(2):
                        zb = (u % 2) * 2 + d_
                        for hl in range(4):
                            h = g * 4 + hl
                            o_ = ps_zz[zb][:, hl * 128:(hl + 1) * 128]
                            mm(o_, da_hi[:, d_, c, h:h + 1].to_broadcast([128, 128]), Tb16[:, d_, :], True, False, [b_set], [b_pzz[zb]])
                            mm(o_, da_lo[:, d_, c, h:h + 1].to_broadcast([128, 128]), Tb16[:, d_, :], False, False, [b_set], [b_pzz[zb]])
                            mm(o_, ident_b[:], nmask[:, d_, :], False, True, [b_set, b_const], [b_pzz[zb]])

                def w1B(u):
                    c, g = units[u]
                    i2 = u % 2
                    for d_ in range(2):
                        zb = (u % 2) * 2 + d_
                        for hl in range(4):
                            h = g * 4 + hl
                            act(Et[i2][:, d_ * 4 + hl, :], ps_zz[zb][:, hl * 128:(hl + 1) * 128], AF.Exp, [b_pzz[zb], b_set], [b_Et[i2]],
                                bias=negu[:, d_, c, h:h + 1])

                def w1C(u):
                    c, g = units[u]
                    i2 = u % 2
                    tokc = slice(c * 128, (c + 1) * 128)
                    mm(ps_cb[:, 0:128], xbcT[:, 4 + g, tokc], xbcT[:, 6 + g, tokc], True, True,
                       [b_xbcT[4 + g][c // 4], b_xbcT[6 + g][c // 4]], [b_pcb])
                    tt(V, Mt[i2][:], Et[i2][:], ps_cb[:, 0:128].unsqueeze(1).to_broadcast([128, 8, 128]), ALU.mult,
                       [b_Et[i2], b_pcb], [b_Mt[i2]])

                def w1D(u):
                    c, g = units[u]
                    i2 = u % 2
                    for hl in range(4):
                        h = g * 4 + hl
                        o_ = ps_y[:, h * 64:(h + 1) * 64]
                        r_ = xs_tm[:, c, h * 64:(h + 1) * 64]
                        brd = [b_xs[h // 2][c // 8]]
                        mm(o_, Mt[i2][:, hl, :], r_, True, False, [b_Mt[i2]] + brd, [b_py])
                        mm(o_, Mt[i2][:, 4 + hl, :], r_, False, False, [b_Mt[i2]] + brd, [b_py])
                        mm(o_, dskipI[:, h, :], r_, False, True, [b_set] + brd, [b_py])
                    if g == 1:
                        make_xsw(c, 0)

                def w1E(u):
                    c, g = units[u]
                    if g == 1:
                        cp(V, ypart[:, c, :], ps_y[:], [b_py], [b_yp[c]])
                        states_and_yoff(c, 0, c == 0)
                        if c > 0:
                            tt(P, ypart[:, c, :], ypart[:, c, :], t1[:], ALU.add, [b_yp[c], b_t1], [b_yp[c]])

                run_pipeline([w1A, w1B, w1C, w1D, w1E], len(units), order=[2, 4, 0, 3, 1])

                def w2x(i):
                    make_xsw(NT - 1 - i, 1)

                def w2a(i):
                    c = NT - 1 - i
                    states_and_yoff(c, 1, i == 0, ti=i % 2, cast_eng=A)

                def w2b(i):
                    c = NT - 1 - i
                    if i == 0:
                        tt(P, yg[:], ypart[:, c, :], zact[:, c, :], ALU.mult, [b_yp[c], b_zact[c]], [b_yg])
                    else:
                        tt(V, yg[:], ypart[:, c, :], t1s[i % 2][:], ALU.add, [b_yp[c], b_t1s[i % 2]], [b_yg])
                        tt(V, yg[:], yg[:], zact[:, c, :], ALU.mult, [b_yg, b_zact[c]], [b_yg])
                    S5 = ss5[:, c, :]
                    for g in range(2):
                        act(scr5[:, g * 256:(g + 1) * 256], yg[:, g * 256:(g + 1) * 256], AF.Square, [b_yg], [b_ss5[c]], accum=S5[:, g:g + 1])
                    act(S5[:, 2:4], S5[:, 0:2], AF.Sqrt, [b_ss5[c]], [b_ss5[c]], bias=epsb[:, 0:1], scale=1.0 / 256)

                def w2b2(i):
                    c = NT - 1 - i
                    S5 = ss5[:, c, :]
                    recip(S5[:, 2:4], S5[:, 2:4], [b_ss5[c]], [b_ss5[c]])
                    for g in range(2):
                        K.op(A, lambda g=g: nc.scalar.mul(ynb[:, g * 256:(g + 1) * 256], yg[:, g * 256:(g + 1) * 256], S5[:, 2 + g:3 + g]),
                             [b_yg, b_ss5[c]], [b_ynb])

                def w2c(i):
                    c = NT - 1 - i
                    tokc = slice(c * 128, (c + 1) * 128)
                    for j in range(4):
                        tr(ps_tr5[:, j * 128:(j + 1) * 128], ynb[:, j * 128:(j + 1) * 128], ident_b[:], [b_ynb, b_const], [b_ptr5])
                    tt(V, mixS[:, :, tokc], ps_tr5[:, 0:512].rearrange("p (j k) -> p j k", k=128), gssm[:], ALU.mult,
                       [b_ptr5, b_set, b_gssm], [b_mixS[c]])

                run_pipeline([w2x, w2a, w2b, w2b2, w2c], NT, order=[4, 3, 2, 1, 0])
        K.barrier()
        if "mixS" in dbg_d:
            with ExitStack() as phd:
                mf = phd.enter_context(nc.sbuf_tensor("s_mixSf", [128, 4, 512], F32))
                b_d = Buf("dbg4")
                cp(V, mf[:], mixS[:, :, 0:512], b_mixS, [b_d])
                dump("mixS", mf[:], [b_d])
            K.barrier()

        x1 = sb("x1", [128, NT, D])
        b_x1 = [bufs(2, f"x1_{t}_") for t in range(NT)]
        wup0 = sb("wup0", [128, 8, 512], BF16)
        wdn0 = sb("wdn0", [128, 4, D], BF16)
        b_wup = bufs(2, "wup")
        b_wdn = bufs(2, "wdn")
        dma(wup0[:], w_up_d[:, 0:512].rearrange("(c p) f -> p c f", p=128), [], [b_wup[0]], "wup0", q=P)
        dma(wdn0[:], w_dn_d[0:512, :].rearrange("(c p) n -> p c n", p=128), [], [b_wdn[0]], "wdn0", q=P)
        with ExitStack() as ph6:
            sb6 = lambda name, shape, dt=F32: ph6.enter_context(nc.sbuf_tensor("s_" + name, shape, dt))
            ps6 = lambda name, shape, dt=F32: ph6.enter_context(nc.psum_tensor("p_" + name, shape, dt))
            xt6 = [sb6(f"xt6_{i}", [128, D]) for i in range(2)]
            b_xt6 = bufs(2, "xt6")
            xn6 = [sb6(f"xn6_{i}", [128, D], BF16) for i in range(2)]
            b_xn6 = bufs(2, "xn6")
            scr6 = sb6("scr6", [128, D], BF16)
            gmlp = sb6("gmlp", [128, 8, 128])
            b_g6 = Buf("gmlp")
            ss6 = sb6("ss6", [128, NT, 2])
            b_ss6 = bufs(NT, "ss6")
            ps_o6 = [ps6(f"ps_o6_{i}", [128, D]) for i in range(2)]
            b_po6 = bufs(2, "po6")
            ps_t6 = [ps6(f"ps_t6_{i}", [128, D], BF16) for i in range(2)]
            b_pt6 = bufs(2, "pt6")
            cp(V, gmlp[:], pcol[:, C_LNMLP:C_LNMLP + 8].unsqueeze(2).to_broadcast([128, 8, 128]), [b_par], [b_g6])
            def p6a(t):
                i2 = t % 2
                tok = slice(t * 128, (t + 1) * 128)
                dma(xt6[i2][:], x_d[tok, :], [], [b_xt6[i2]], f"x6_{i2}")
                for half in range(2):
                    cs = slice(half * 512, (half + 1) * 512)
                    for j in range(4):
                        mm(ps_o6[i2][:, cs], mixA[:, j, tok], wo[:, j, cs], j == 0, False, [b_mixA[t // 4], b_wo[j]], [b_po6[i2]])
                    for j in range(4):
                        mm(ps_o6[i2][:, cs], mixS[:, j, tok], wo[:, 4 + j, cs], False, j == 3, [b_mixS[t], b_wo[4 + j]], [b_po6[i2]])

            def p6b(t):
                i2 = t % 2
                tt(V, x1[:, t, :], xt6[i2][:], ps_o6[i2][:], ALU.add, [b_xt6[i2], b_po6[i2]], b_x1[t])
                act(scr6[:], x1[:, t, :], AF.Square, b_x1[t], [b_ss6[t]], accum=ss6[:, t, 0:1])
                act(ss6[:, t, 1:2], ss6[:, t, 0:1], AF.Sqrt, [b_ss6[t]], [b_ss6[t]], bias=epsb[:, 0:1], scale=1.0 / D)
                recip(ss6[:, t, 1:2], ss6[:, t, 1:2], [b_ss6[t]], [b_ss6[t]])

            def p6c(t):
                i2 = t % 2
                K.op(A, lambda: nc.scalar.mul(xn6[i2][:], x1[:, t, :], ss6[:, t, 1:2]), b_x1[t] + [b_ss6[t]], [b_xn6[i2]])

            def p6d(t):
                i2 = t % 2
                for dc in range(8):
                    tr(ps_t6[i2][:, dc * 128:(dc + 1) * 128], xn6[i2][:, dc * 128:(dc + 1) * 128], ident_b[:], [b_xn6[i2], b_const], [b_pt6[i2]])

            def p6e(t):
                i2 = t % 2
                tok = slice(t * 128, (t + 1) * 128)
                tt(V, hT[:, :, tok], ps_t6[i2][:].rearrange("p (c k) -> p c k", k=128), gmlp[:], ALU.mult, [b_pt6[i2], b_g6], [b_hT[t]])

            run_pipeline([p6a, p6b, p6c, p6d, p6e], NT)
        K.barrier()

        with ExitStack() as ph7:
            sb7 = lambda name, shape, dt=F32: ph7.enter_context(nc.sbuf_tensor("s_" + name, shape, dt))
            ps7 = lambda name, shape, dt=F32: ph7.enter_context(nc.psum_tensor("p_" + name, shape, dt))
            NFG = 8
            wup = [wup0, sb7("wup1", [128, 8, 512], BF16)]
            wdn = [wdn0, sb7("wdn1", [128, 4, D], BF16)]
            actT = sb7("actT", [128, 4, S], BF16)
            b_actT = [bufs(4, f"actT{fc}_") for fc in range(4)]
            rl = [sb7(f"rl{i}", [128, 512], BF16) for i in range(2)]
            b_rl = bufs(2, "rl")
            psU = [ps7(f"psU{i}", [128, 512]) for i in range(4)]
            b_pU = bufs(4, "pU")
            psD = [ps7(f"psD{i}", [128, 512]) for i in range(4)]
            b_pD = bufs(4, "pD")

            def load_w(fg):
                sl = fg % 2
                dma(wup[sl][:], w_up_d[:, fg * 512:(fg + 1) * 512].rearrange("(c p) f -> p c f", p=128), [], [b_wup[sl]], f"wup{sl}", q=P)
                dma(wdn[sl][:], w_dn_d[fg * 512:(fg + 1) * 512, :].rearrange("(c p) n -> p c n", p=128), [], [b_wdn[sl]], f"wdn{sl}", q=P)

            iu = 0
            idn = 0
            for fg in range(NFG):
                sl = fg % 2
                if fg + 1 < NFG:
                    load_w(fg + 1)
                for fc in range(4):
                    for tb in range(4):
                        iU, iR = iu % 4, iu % 2
                        iu += 1
                        ts_ = slice(tb * 512, (tb + 1) * 512)
                        for dc in range(8):
                            mm(psU[iU][:], wup[sl][:, dc, fc * 128:(fc + 1) * 128], hT[:, dc, ts_], dc == 0, dc == 7,
                               [b_wup[sl]] + b_hT[tb * 4:(tb + 1) * 4], [b_pU[iU]])
                        act(rl[iR][:], psU[iU][:], AF.Relu, [b_pU[iU]], [b_rl[iR]])
                        tt(P, actT[:, fc, ts_], rl[iR][:], rl[iR][:], ALU.mult, [b_rl[iR]], [b_actT[fc][tb]])
                for t in range(NT):
                    tok = slice(t * 128, (t + 1) * 128)
                    for half in range(2):
                        iD = idn % 4
                        idn += 1
                        cs = slice(half * 512, (half + 1) * 512)
                        for fc in range(4):
                            mm(psD[iD][:], actT[:, fc, tok], wdn[sl][:, fc, cs], fc == 0, fc == 3,
                               [b_actT[fc][t // 4], b_wdn[sl]], [b_pD[iD]])
                        tt(V, x1[:, t, cs], x1[:, t, cs], psD[iD][:], ALU.add, [b_x1[t][half], b_pD[iD]], [b_x1[t][half]])
                    if fg == NFG - 1:
                        dma(out_d[tok, :], x1[:, t, :], b_x1[t], [], "out")

        nw = K.emit(es)
        print("[kernel] psum bufs:", sorted(K.psum_names))
        print(f"[kernel] ops={len(K.ops)} waits={nw} sig={K.sig_counts} dma={K.dma_count}")
    return nc


def _host_params(inp, b):
    f = np.float32
    pcol = np.zeros((128, NCOL), f)
    col = lambda v, n: np.ascontiguousarray(np.asarray(v, f).reshape(n, 128).T)
    pcol[:, C_LNMIX:C_LNMIX + 8] = col(inp["ln_mix_g"][0], 8)
    pcol[:, C_LNMLP:C_LNMLP + 8] = col(inp["ln_mlp_g"][0], 8)
    pcol[:, C_QA:C_QA + 2] = col(inp["q_a_norm_g"][0], 2)
    pcol[:, C_KVA:C_KVA + 1] = col(inp["kv_a_norm_g"][0], 1)
    pcol[:, C_GK] = 1.0
    pcol[0:64, C_GK] = inp["k_norm_g"][0][0:64]
    pcol[:, C_GQ] = 1.0
    pcol[0:64, C_GQ] = inp["q_norm_g"][0][0:64]
    pcol[0:64, C_GATT:C_GATT + 8] = np.asarray(inp["attn_out_norm_g"][0], f).reshape(8, 64).T
    pcol[:, C_GSSM:C_GSSM + 4] = col(inp["ssm_norm_g"][0], 4)
    pcol[:, C_CONVB:C_CONVB + 8] = col(inp["conv_b"][0], 8)
    cw = np.asarray(inp["conv_w"][0][:, 0, :], f)
    for k in range(5):
        pcol[:, C_CONVW + k * 8:C_CONVW + (k + 1) * 8] = col(cw[k], 8)
    row = np.zeros((NBC,), f)
    row[B_DTB:B_DTB + 8] = inp["dt_bias_fwd"][0]
    row[B_DTB + 8:B_DTB + 16] = inp["dt_bias_bwd"][0]
    row[B_ALOG:B_ALOG + 8] = inp["a_log_fwd"][0]
    row[B_ALOG + 8:B_ALOG + 16] = inp["a_log_bwd"][0]
    row[B_DSKIP:B_DSKIP + 8] = inp["d_skip"][0]
    row[B_GQR:B_GQR + 32] = inp["q_norm_g"][0][64:96]
    row[B_GKR:B_GKR + 32] = inp["k_norm_g"][0][64:96]
    inv = (1.0 / (np.float32(10000.0) ** (np.arange(0, 32, 2, dtype=f) / np.float32(32)))).astype(f)
    row[B_INVF:B_INVF + 16] = inv
    pbc = np.ascontiguousarray(np.broadcast_to(row[None, :], (128, NBC)))
    pos = np.ascontiguousarray(np.asarray(inp["positions"][b], np.int32).reshape(NT, 128).T)
    return pcol, pbc, pos


_NC_CACHE = {}


def kernel(**inputs):
    inp = {k: np.asarray(v) for k, v in inputs.items()}
    if "nc" not in _NC_CACHE:
        _NC_CACHE["nc"] = build_nc()
    nc = _NC_CACHE["nc"]
    in_maps = []
    for b in range(8):
        pcol, pbc, pos = _host_params(inp, b)
        in_maps.append({
            "x": np.ascontiguousarray(inp["x"][b], np.float32),
            "pos": pos, "pcol": pcol, "pbc": pbc,
            "w_in": np.ascontiguousarray(inp["w_in"][0], np.float32),
            "w_uq": np.ascontiguousarray(inp["w_uq"][0], np.float32),
            "w_ukv": np.ascontiguousarray(inp["w_ukv"][0], np.float32),
            "w_out": np.ascontiguousarray(inp["w_out"][0], np.float32),
            "w_up": np.ascontiguousarray(inp["w_mlp_up"][0], np.float32),
            "w_dn": np.ascontiguousarray(inp["w_mlp_down"][0], np.float32),
        })
    res = run_bass_kernel_spmd(nc, in_maps, core_ids=list(range(8)))
    return np.stack([np.asarray(r["out"], np.float32) for r in res.results], axis=0)
```
